# Optimizing a Trainium2 kernel written in Bass

```python
import jax
import jax.numpy as jnp
from jax import lax
import numpy as np

D_MODEL = 2048
BATCH = 8
SEQ = 4096
DEPTH = 2

GRID_W = 64
CTX_LEN = 256
N_BRANCH = 4
MIX_W = D_MODEL // 4
NA_HEADS = 8
NA_HEAD_DIM = MIX_W // NA_HEADS
NA_WIN_H = 8
NA_WIN_W = 16
NA_KBLK_W = 2 * NA_WIN_W
NA_NBLK = GRID_W // NA_WIN_W
NA_SCALE = NA_HEAD_DIM ** -0.5
CONV_WIDTH = 3
FOURIER_GROUPS = 4
POOL_WINDOWS = (2, 4, 8, 16)
POOL_GC = MIX_W // len(POOL_WINDOWS)
N_EXPERTS = 16
EXPERT_FF = D_MODEL
EC_CAPACITY = 2
N_MOD = 6
EPS = 1e-6
NEG_INF = -1e30

IN_SPLITS = tuple(MIX_W * i for i in range(1, 9))
IN_COLS = 8 * MIX_W + N_BRANCH * D_MODEL
K_OFF = 4 * MIX_W
V_END = 6 * MIX_W

kernel_name = 'hybrid_gated_mixers_ec_moe_diffusion'


def rmsnorm(x, w):
    xf = x.astype(jnp.float32)
    y = xf * lax.rsqrt(jnp.mean(xf * xf, axis=-1, keepdims=True) + EPS)
    return (y * w.astype(jnp.float32)).astype(x.dtype)


def modulate(h, shift, scale):
    return h * (1 + scale) + shift


def short_conv(u, w):
    return lax.conv_general_dilated(
        u, w[:, None, :].astype(u.dtype), window_strides=(1,),
        padding=((CONV_WIDTH // 2, CONV_WIDTH // 2),),
        dimension_numbers=('NWC', 'WIO', 'NWC'), feature_group_count=u.shape[-1])


def fourier_mix(u):
    b, n, _ = u.shape
    ug = u.astype(jnp.float32).reshape(b, n, FOURIER_GROUPS, MIX_W // FOURIER_GROUPS)
    y = jnp.fft.fftn(ug, axes=(1, 3), norm='ortho').real
    return y.reshape(b, n, MIX_W).astype(u.dtype)


def multiscale_pool(u, w_grp, scale):
    b, n, _ = u.shape
    ng = len(POOL_WINDOWS)
    uf = u.astype(jnp.float32).reshape(b, n, ng, POOL_GC)
    csum = jnp.concatenate([jnp.zeros((b, 1, ng, POOL_GC), jnp.float32), lax.cumsum(uf, axis=1)], axis=1)
    t = np.arange(n)[:, None]
    win = np.asarray(POOL_WINDOWS)[None, :]
    lo = np.clip(t - win // 2, 0, n - 1)
    hi = np.clip(t + win - win // 2 - 1, 0, n - 1)
    gi = np.arange(ng)[None, :]
    count = (hi - lo + 1).astype(np.float32)[None, :, :, None]
    pooled = (csum[:, hi + 1, gi] - csum[:, lo, gi]) / count - uf
    y = jnp.einsum('bngc,gcd->bngd', pooled.astype(u.dtype), w_grp)
    return y.reshape(b, n, MIX_W) * scale


def na_tables(rows):
    kh = min(NA_WIN_H, rows)
    r = np.arange(rows)
    row_start = np.clip(r - kh // 2, 0, rows - kh)
    key_rows = row_start[:, None] + np.arange(kh)[None, :]
    dr = key_rows - r[:, None] + (NA_WIN_H - 1)
    qcol = np.arange(GRID_W).reshape(NA_NBLK, NA_WIN_W)
    kb_start = np.clip(np.arange(NA_NBLK) * NA_WIN_W - NA_WIN_W // 2, 0, GRID_W - NA_KBLK_W)
    key_cols = kb_start[:, None] + np.arange(NA_KBLK_W)[None, :]
    col_start = np.clip(qcol - NA_WIN_W // 2, 0, GRID_W - NA_WIN_W)
    kc = key_cols[:, None, :]
    cs = col_start[:, :, None]
    col_ok = (kc >= cs) & (kc < cs + NA_WIN_W)
    dc = np.clip(kc - qcol[:, :, None] + (NA_WIN_W - 1), 0, 2 * NA_WIN_W - 2)
    return kh, key_rows, key_cols, col_ok, dr, dc


def neighbourhood_attention(q, k, v, k_ctx, v_ctx, rpb):
    b, n, nh, dh = q.shape
    rows = n // GRID_W
    kh, key_rows, key_cols, col_ok, dr, dc = na_tables(rows)
    nkeys = kh * NA_KBLK_W
    qb = q.reshape(b, rows, NA_NBLK, NA_WIN_W, nh, dh)
    kg = k.reshape(b, rows, GRID_W, nh, dh)
    vg = v.reshape(b, rows, GRID_W, nh, dh)
    ri = key_rows[:, None, :, None]
    ci = key_cols[None, :, None, :]
    kb = kg[:, ri, ci]
    vb = vg[:, ri, ci]
    s_loc = jnp.einsum('brnqhd,brnijhd->bhrnqij', qb, kb, preferred_element_type=jnp.float32) * NA_SCALE
    bias = rpb[:, dr[:, None, None, :, None], dc[None, :, :, None, :]].astype(jnp.float32)
    s_loc = jnp.where(col_ok[:, :, None, :], s_loc + bias, NEG_INF)
    s_ctx = jnp.einsum('brnqhd,blhd->bhrnql', qb, k_ctx, preferred_element_type=jnp.float32) * NA_SCALE
    s = jnp.concatenate([s_loc.reshape(b, nh, rows, NA_NBLK, NA_WIN_W, nkeys), s_ctx], axis=-1)
    p = jax.nn.softmax(s, axis=-1).astype(v.dtype)
    p_loc = p[..., :nkeys].reshape(b, nh, rows, NA_NBLK, NA_WIN_W, kh, NA_KBLK_W)
    o = (jnp.einsum('bhrnqij,brnijhd->brnqhd', p_loc, vb)
         + jnp.einsum('bhrnql,blhd->brnqhd', p[..., nkeys:], v_ctx))
    return o.reshape(b, n, nh * dh)


def context_attention(q, k, v):
    b, l, nh, dh = q.shape
    s = jnp.einsum('blhd,bmhd->bhlm', q, k, preferred_element_type=jnp.float32) * NA_SCALE
    p = jax.nn.softmax(s, axis=-1).astype(v.dtype)
    return jnp.einsum('bhlm,bmhd->blhd', p, v).reshape(b, l, nh * dh)


def kv_heads(zk, zv, k_norm_w):
    b, n, _ = zk.shape
    k = rmsnorm(zk.reshape(b, n, NA_HEADS, NA_HEAD_DIM), k_norm_w)
    v = zv.reshape(b, n, NA_HEADS, NA_HEAD_DIM)
    return k, v


def token_mixer(z, kv_ctx, is_latent, conv_w, q_norm_w, k_norm_w, rpb, pool_w, pool_scale, w_branch, w_out):
    b, n, _ = z.shape
    xa, gb, gc, zq, zk, zv, zf, zp, zg = jnp.split(z, IN_SPLITS, axis=-1)
    y_conv = gb * short_conv(gc * xa, conv_w)
    q = rmsnorm(zq.reshape(b, n, NA_HEADS, NA_HEAD_DIM), q_norm_w)
    if is_latent:
        k, v = kv_heads(zk, zv, k_norm_w)
        y_att = neighbourhood_attention(q, k, v, kv_ctx[0], kv_ctx[1], rpb)
    else:
        y_att = context_attention(q, kv_ctx[0], kv_ctx[1])
    branches = (y_conv, y_att, fourier_mix(zf), multiscale_pool(zp, pool_w, pool_scale))
    gates = jax.nn.sigmoid(zg.reshape(b, n, N_BRANCH, D_MODEL))
    merged = gates[:, :, 0] * (branches[0] @ w_branch[0])
    for i in range(1, N_BRANCH):
        merged = merged + gates[:, :, i] * (branches[i] @ w_branch[i])
    return merged @ w_out


def expert_choice_ffn(h, w_router, w_gate, w_up, w_down):
    b, n, _ = h.shape
    cap = EC_CAPACITY * n // N_EXPERTS
    logits = jnp.einsum('bnd,de->ben', h, w_router, preferred_element_type=jnp.float32)
    affinity = jax.nn.softmax(logits, axis=1)
    gate, idx = lax.top_k(affinity, cap)
    bi = jnp.arange(b)[:, None, None]
    xs = h[bi, idx]
    a = jnp.einsum('becd,edf->becf', xs, w_gate)
    u = jnp.einsum('becd,edf->becf', xs, w_up)
    y = jnp.einsum('becf,efd->becd', jax.nn.silu(a) * u, w_down) * gate[..., None].astype(h.dtype)
    return jnp.zeros_like(h).at[bi, idx].add(y)


def setup_inputs(seed: int = 0) -> dict:
    key = jax.random.key(seed)
    ks = jax.random.split(key, 21)
    f32 = jnp.float32

    def nrm(k, shape, s):
        return jax.random.normal(k, shape, f32) * s

    return {
        'x': nrm(ks[0], (BATCH, SEQ, D_MODEL), 1.0),
        'c': nrm(ks[1], (BATCH, D_MODEL), 1.0),
        'ctx': nrm(ks[2], (BATCH, CTX_LEN, D_MODEL), 1.0),
        'c_ctx': nrm(ks[3], (D_MODEL,), 1.0),
        'w_ada': nrm(ks[4], (DEPTH, D_MODEL, N_MOD * D_MODEL), 0.5 * D_MODEL ** -0.5),
        'b_ada': nrm(ks[5], (DEPTH, N_MOD * D_MODEL), 0.02),
        'norm1_w': 1.0 + nrm(ks[6], (DEPTH, D_MODEL), 0.02),
        'norm2_w': 1.0 + nrm(ks[7], (DEPTH, D_MODEL), 0.02),
        'w_in': nrm(ks[8], (DEPTH, D_MODEL, IN_COLS), D_MODEL ** -0.5),
        'conv_w': nrm(ks[9], (DEPTH, CONV_WIDTH, MIX_W), CONV_WIDTH ** -0.5),
        'q_norm_w': 1.0 + nrm(ks[10], (DEPTH, NA_HEAD_DIM), 0.02),
        'k_norm_w': 1.0 + nrm(ks[11], (DEPTH, NA_HEAD_DIM), 0.02),
        'na_rpb': nrm(ks[12], (DEPTH, NA_HEADS, 2 * NA_WIN_H - 1, 2 * NA_WIN_W - 1), 0.1),
        'pool_w': nrm(ks[13], (DEPTH, len(POOL_WINDOWS), POOL_GC, POOL_GC), POOL_GC ** -0.5),
        'pool_scale': 1.0 + nrm(ks[14], (DEPTH, MIX_W), 0.1),
        'w_branch': nrm(ks[15], (DEPTH, N_BRANCH, MIX_W, D_MODEL), MIX_W ** -0.5),
        'w_out': nrm(ks[16], (DEPTH, D_MODEL, D_MODEL), D_MODEL ** -0.5),
        'w_router': nrm(ks[17], (DEPTH, D_MODEL, N_EXPERTS), D_MODEL ** -0.5),
        'w_exp_gate': nrm(ks[18], (DEPTH, N_EXPERTS, D_MODEL, EXPERT_FF), D_MODEL ** -0.5),
        'w_exp_up': nrm(ks[19], (DEPTH, N_EXPERTS, D_MODEL, EXPERT_FF), D_MODEL ** -0.5),
        'w_exp_down': nrm(ks[20], (DEPTH, N_EXPERTS, EXPERT_FF, D_MODEL), EXPERT_FF ** -0.5),
    }


def reference(x, c, ctx, c_ctx, w_ada, b_ada, norm1_w, norm2_w, w_in, conv_w, q_norm_w, k_norm_w,
              na_rpb, pool_w, pool_scale, w_branch, w_out, w_router, w_exp_gate, w_exp_up, w_exp_down):
    sc = jax.nn.silu(c)
    scc = jax.nn.silu(c_ctx)
    for l in range(DEPTH):
        last = l == DEPTH - 1
        mod = (sc @ w_ada[l] + b_ada[l])[:, None, :]
        sh1, sc1, g1, sh2, sc2, g2 = jnp.split(mod, N_MOD, axis=-1)
        n_cols = (2 if last else N_MOD) * D_MODEL
        mod_c = jnp.split(scc @ w_ada[l][:, :n_cols] + b_ada[l][:n_cols], n_cols // D_MODEL)
        hc = modulate(rmsnorm(ctx, norm1_w[l]), mod_c[0], mod_c[1])
        if last:
            zkv = hc @ w_in[l][:, K_OFF:V_END]
        else:
            zc = hc @ w_in[l]
            zkv = zc[..., K_OFF:V_END]
        kv_c = kv_heads(zkv[..., :MIX_W], zkv[..., MIX_W:], k_norm_w[l])
        mixer_p = (conv_w[l], q_norm_w[l], k_norm_w[l], na_rpb[l], pool_w[l], pool_scale[l], w_branch[l], w_out[l])
        ffn_p = (w_router[l], w_exp_gate[l], w_exp_up[l], w_exp_down[l])
        h = modulate(rmsnorm(x, norm1_w[l]), sh1, sc1)
        x = x + g1 * token_mixer(h @ w_in[l], kv_c, True, *mixer_p)
        h = modulate(rmsnorm(x, norm2_w[l]), sh2, sc2)
        x = x + g2 * expert_choice_ffn(h, *ffn_p)
        if not last:
            ctx = ctx + mod_c[2] * token_mixer(zc, kv_c, False, *mixer_p)
            hc2 = modulate(rmsnorm(ctx, norm2_w[l]), mod_c[3], mod_c[4])
            ctx = ctx + mod_c[5] * expert_choice_ffn(hc2, *ffn_p)
    return x
```

```python
import numpy as np
import ml_dtypes
import concourse.bass as bass
import concourse.mybir as mybir
from concourse.bass_utils import run_bass_kernel_spmd

F32 = mybir.dt.float32
BF16 = mybir.dt.bfloat16
U32 = mybir.dt.uint32
I32 = mybir.dt.int32
ALU = mybir.AluOpType
AF = mybir.ActivationFunctionType
AX = mybir.AxisListType

SEM_ROT = 24000


class Buf:
    __slots__ = ("name", "lw", "rd", "mo")

    def __init__(self, name):
        self.name = name
        self.lw = {}
        self.rd = {}
        self.mo = False


class Eng:
    def __init__(self, fw, e, name):
        self.fw = fw
        self.e = e
        self.name = name
        self.sem = fw.new_sem(name)
        self.cnt = 0
        self.seen = {}

    def _rotate(self):
        if self.cnt >= SEM_ROT:
            self.sem = self.fw.new_sem(self.name)
            self.cnt = 0


class FW:
    def __init__(self, nc):
        self.nc = nc
        self.nsem = 0
        self.sem_objs = []
        self.pe = Eng(self, nc.tensor, "pe")
        self.act = Eng(self, nc.scalar, "act")
        self.dve = Eng(self, nc.vector, "dve")
        self.pool = Eng(self, nc.gpsimd, "pool")
        self.sp = Eng(self, nc.sync, "sp")
        self.dma_sems = {}

    def new_sem(self, name):
        self.nsem += 1
        cm = self.nc.semaphore(f"{name}_{self.nsem}")
        s = cm.__enter__()
        self.sem_objs.append(s)
        return s

    def _waits(self, eng, reads, writes, extra=(), mwrites=()):
        need = {}

        def merge(d):
            for s, v in d.items():
                if need.get(s, 0) < v:
                    need[s] = v
        for b in reads:
            merge(b.lw)
        for b in writes:
            merge(b.lw)
            merge(b.rd)
        for b in mwrites:
            merge(b.rd)
            if not b.mo:
                merge(b.lw)
        for d in extra:
            merge(d)
        for s, v in need.items():
            if eng.seen.get(s, 0) < v:
                eng.e.wait_ge(s, v)
                eng.seen[s] = v

    def _commit(self, ev, reads, writes, mwrites):
        for b in writes:
            b.lw = {ev[0]: ev[1]}
            b.rd = {}
            b.mo = False
        for b in mwrites:
            if b.rd or not b.mo:
                b.lw = {}
            b.lw[ev[0]] = ev[1]
            b.rd = {}
            b.mo = True
        for b in reads:
            if b.rd.get(ev[0], 0) < ev[1]:
                b.rd[ev[0]] = ev[1]

    def op(self, eng, fn, reads=(), writes=(), mwrites=()):
        eng._rotate()
        self._waits(eng, reads, writes, mwrites=mwrites)
        ins = fn()
        eng.cnt += 1
        ins.then_inc(eng.sem, 1)
        self._commit((eng.sem, eng.cnt), reads, writes, mwrites)
        return ins

    def op_group(self, eng, fns, reads=(), writes=()):
        eng._rotate()
        self._waits(eng, reads, writes)
        for fn in fns[:-1]:
            fn()
        ins = fns[-1]()
        eng.cnt += 1
        ins.then_inc(eng.sem, 1)
        self._commit((eng.sem, eng.cnt), reads, writes, ())
        return ins

    def dma(self, eng, fn, key, reads=(), writes=(), mwrites=()):
        if key not in self.dma_sems:
            self.dma_sems[key] = [self.new_sem("d"), 0]
        ent = self.dma_sems[key]
        prev = {ent[0]: ent[1]} if ent[1] else {}
        self._waits(eng, reads, writes, extra=(prev,), mwrites=mwrites)
        ins = fn()
        ent[1] += 16
        ins.then_inc(ent[0], 16)
        self._commit((ent[0], ent[1]), reads, writes, mwrites)
        return ins

    def wait_all(self, eng, bufs):
        self._waits(eng, bufs, bufs)


class Tile:
    def __init__(self, fw, kind, name, shape, dtype):
        nc = fw.nc
        cm = nc.sbuf_tensor(name, shape, dtype) if kind == "sb" else nc.psum_tensor(name, shape, dtype)
        self.t = cm.__enter__()
        self.b = Buf(name)
        self.name = name
        self.bufs = [self.b]

    def __getitem__(self, idx):
        return self.t[idx]


class SubTile:
    def __init__(self, parent, name, c0, c1):
        self.ap = parent.t[:, c0:c1]
        self.b = Buf(name)
        self.name = name
        self.bufs = [self.b]
        parent.bufs.append(self.b)

    def __getitem__(self, idx):
        return self.ap[idx]


class Pool:
    def __init__(self, fw, kind, name, shape, dtype, n):
        self.tiles = [Tile(fw, kind, f"{name}{i}", shape, dtype) for i in range(n)]
        self.i = 0

    def next(self):
        t = self.tiles[self.i % len(self.tiles)]
        self.i += 1
        return t


D = 2048
T = 4096
L = 256
NT = T + L
KC = D // 128
MIXW = 512
NE = 16
CAP = 512
CAPC = 32
SLOTS = CAP + CAPC
NEG = -30000.0
EPS = 1e-6
NA_SCALE = 0.125
MOE_STOP = None


def build(NL=2, do_moe=True, dbg=False):
    nc = bass.Bass("TRN2", target_bir_lowering=False)
    fw = FW(nc)
    sp, pe, act, dve, pool = fw.sp, fw.pe, fw.act, fw.dve, fw.pool
    V, S, G, P = nc.vector, nc.scalar, nc.gpsimd, nc.tensor

    def din(name, shape, dt=F32):
        return nc.dram_tensor(name, list(shape), dt, kind="ExternalInput").ap()

    def dscr(name, shape, dt):
        if dbg and (dbg is True or name in dbg):
            return nc.dram_tensor(name, list(shape), dt, kind="ExternalOutput").ap()
        return nc.dram_tensor(name, list(shape), dt).ap()

    xin = din("xin", [NT, D])
    cvec = din("cvec", [2, D])
    dftc = din("dftc", [T, T], BF16)
    dfts = din("dfts", [T, T], BF16)
    dftcc = din("dftcc", [L, L], BF16)
    dftsc = din("dftsc", [L, L], BF16)
    chdft = din("chdft", [128, 256], BF16)
    invcnt = din("invcnt", [4, NT])
    W = []
    for l in range(NL):
        W.append(dict(
            w_ada=din(f"w_ada{l}", [D, 6 * D]), b_ada=din(f"b_ada{l}", [96, 128]),
            n1=din(f"n1_{l}", [16, 128]), n2=din(f"n2_{l}", [16, 128]),
            w_in=din(f"w_in{l}", [D, 6 * D]), conv=din(f"conv{l}", [3, MIXW]),
            qn=din(f"qn{l}", [128, 1]), kn=din(f"kn{l}", [128, 1]),
            bias=din(f"bias{l}", [5, 8, 640, 128]),
            pool_w=din(f"pool_w{l}", [4, 128, 128]), pool_s=din(f"pool_s{l}", [MIXW, 1]),
            w_br=din(f"w_br{l}", [D, D]), w_out=din(f"w_out{l}", [D, D]),
            w_rt=din(f"w_rt{l}", [D, NE]),
            weg=din(f"weg{l}", [NE, D, D]), weu=din(f"weu{l}", [NE, D, D]), wed=din(f"wed{l}", [NE, D, D]),
        ))
    out = nc.dram_tensor("out", [T, D], F32, kind="ExternalOutput").ap()

    xres = dscr("xres", [NT + 128, D], F32); b_xres = Buf("xres")
    hT = dscr("hT", [D, NT], BF16); b_hT = Buf("hT")
    zT = dscr("zT", [6 * D, NT], BF16); b_zT = Buf("zT")
    vtok = dscr("vtok", [NT, MIXW], BF16); b_vtok = Buf("vtok")
    qkT = dscr("qkT", [2 * MIXW, NT], BF16); b_qkT = Buf("qkT")
    brT = dscr("brT", [D, NT], BF16); b_brT = Buf("brT")
    fab = dscr("fab", [NT, 4, 256], BF16); b_fab = Buf("fab")
    h2tok = dscr("h2tok", [NT + 128, D], BF16); b_h2 = Buf("h2tok")
    vecs = dscr("vecs", [8, D], F32); b_vecs = Buf("vecs")
    b_out = Buf("out")
    wbr16 = dscr("wbr16", [16, 128, 16, 128], BF16); b_wbr = Buf("wbr16")
    wout16 = dscr("wout16", [4, 128, 16, 512], BF16); b_wout = Buf("wout16")

    WB = Pool(fw, "sb", "wb", [128, 8192], BF16, 2)
    WH = [SubTile(WB.tiles[i // 2], f"wh{i}", (i % 2) * 4096, (i % 2 + 1) * 4096) for i in range(4)]
    ABp = Pool(fw, "sb", "ab", [128, 8192], BF16, 3)
    FP = Pool(fw, "sb", "fp", [128, 2048], F32, 4)
    SM = Pool(fw, "sb", "sm", [128, 1056], F32, 5)
    small16 = Pool(fw, "sb", "s16", [128, 2048], BF16, 6)
    bcA = Tile(fw, "sb", "bcA", [128, 2048], F32)
    bcB = Tile(fw, "sb", "bcB", [128, 2048], F32)
    wr = Tile(fw, "sb", "wr", [128, 256], F32)
    affT = Tile(fw, "sb", "affT", [16, NT], F32)
    tv = Tile(fw, "sb", "tv", [16, SLOTS], F32)
    tix = Tile(fw, "sb", "tix", [16, SLOTS], U32)
    idxF = Tile(fw, "sb", "idxF", [128, 80], F32)
    idxF4 = Tile(fw, "sb", "idxF4", [128, 80], F32)
    idxT = Tile(fw, "sb", "idxT", [128, 80], I32)
    idxT4 = Tile(fw, "sb", "idxT4", [128, 320], I32)
    gateT = Tile(fw, "sb", "gateT", [128, 80], F32)
    chd = Tile(fw, "sb", "chd", [128, 256], BF16)
    ones = Tile(fw, "sb", "ones", [128, 8], BF16)
    xc = Tile(fw, "sb", "xc", [128, 512], BF16)
    ac = Tile(fw, "sb", "ac", [128, 512], BF16)
    PS = Pool(fw, "ps", "ps", [128, 512], F32, 4)
    psO = Tile(fw, "ps", "psO", [128, 512], F32)
    psD = Tile(fw, "ps", "psD", [128, 512], F32)
    qt = Tile(fw, "sb", "qt", [128, 512], BF16)
    PSB = Pool(fw, "ps", "psb", [128, 1024], BF16, 2)
    ident = Tile(fw, "sb", "ident", [128, 128], F32)
    identb = Tile(fw, "sb", "identb", [128, 128], BF16)
    blk64 = Tile(fw, "sb", "blk64", [128, 128], F32)
    modT = Tile(fw, "sb", "modT", [128, 192], F32)
    prm = Tile(fw, "sb", "prm", [128, 256], F32)
    small = Pool(fw, "sb", "tiny", [128, 64], F32, 8)

    def _b(xs):
        o = []
        for x in xs:
            if hasattr(x, "bufs"):
                o.extend(x.bufs)
            else:
                o.append(x)
        return o

    def O(eng, fn, r=(), w=(), mw=()):
        return fw.op(eng, fn, _b(r), _b(w), _b(mw))

    def OG(eng, fns, r=(), w=()):
        return fw.op_group(eng, fns, _b(r), _b(w))

    def DMA(eng, fn, key, r=(), w=(), mw=()):
        return fw.dma(eng, fn, key, _b(r), _b(w), _b(mw))

    def ld(dst_tile, dst_ap, src_ap, r=(), cast=False, **kw):
        if cast:
            return DMA(pool, lambda: G.dma_start(out=dst_ap, in_=src_ap, **kw), dst_tile.name, r, [dst_tile])
        return DMA(sp, lambda: nc.sync.dma_start(out=dst_ap, in_=src_ap, **kw), dst_tile.name, r, [dst_tile])

    def stq(dst_ap, src_tile, src_ap, dbuf, **kw):
        return DMA(act, lambda: S.dma_start(out=dst_ap, in_=src_ap, **kw), src_tile.name + "_st", [src_tile], mw=[dbuf])

    O(pool, lambda: G.memset(ident[:], 1.0), w=[ident])
    O(pool, lambda: G.affine_select(out=ident[:], in_=ident[:], pattern=[[-1, 128]], compare_op=ALU.is_equal,
                                   fill=0.0, base=0, channel_multiplier=1), r=[ident], w=[ident])
    O(dve, lambda: V.tensor_copy(out=identb[:], in_=ident[:]), r=[ident], w=[identb])
    O(pool, lambda: G.memset(blk64[:], 0.0), w=[blk64])
    O(pool, lambda: G.memset(blk64[0:64, 0:64], 1.0), r=[blk64], w=[blk64])
    O(pool, lambda: G.memset(blk64[64:128, 64:128], 1.0), r=[blk64], w=[blk64])

    def prm_ap(which, j, r):
        o = (which * 16 + j) * 2 + r
        return prm[:, o:o + 1]

    def rstd_from_ss(ss_ap, out_ap, n, tl):
        O(dve, lambda: V.tensor_scalar(out=out_ap, in0=ss_ap, scalar1=1.0 / n, scalar2=EPS, op0=ALU.mult, op1=ALU.add), r=[tl], w=[tl])
        O(act, lambda: S.sqrt(out=out_ap, in_=out_ap), r=[tl], w=[tl])
        O(dve, lambda: V.reciprocal(out=out_ap, in_=out_ap), r=[tl], w=[tl])

    def phase_precast(l):
        w = W[l]
        for dj in range(16):
            DMA(pool, lambda: G.dma_start(out=wbr16[dj], in_=w["w_br"][:, dj * 128:(dj + 1) * 128].rearrange("(k p) c -> p k c", p=128)),
                f"pc{dj % 4}", [], mw=[b_wbr])
        for db in range(4):
            DMA(pool, lambda: G.dma_start(out=wout16[db], in_=w["w_out"][:, db * 512:(db + 1) * 512].rearrange("(k p) c -> p k c", p=128)),
                f"pc{db % 4}", [], mw=[b_wout])

    def phase_mod(l):
        w = W[l]
        scT = small.next()
        t = small.next()
        with nc.allow_non_contiguous_dma(reason="tiny transposed load"):
            for r_ in range(2):
                ld(t, t[:, r_ * 16:(r_ + 1) * 16], cvec[r_].rearrange("(k p) -> p k", p=128))
        O(act, lambda: S.activation(out=scT[:, 0:32], in_=t[:, 0:32], func=AF.Silu), r=[t], w=[scT])
        psm = PS.next()
        for cb in range(48):
            wt = FP.next()
            for half in range(2):
                if half == 1:
                    wt = FP.next()
                ld(wt, wt[:, :].rearrange("p (k c) -> p k c", c=256),
                   w["w_ada"][half * 1024:(half + 1) * 1024, cb * 256:(cb + 1) * 256].rearrange("(k p) c -> p k c", p=128))
                if half == 0:
                    wt0 = wt
            for jj in range(2):
                col = cb * 2 + jj
                for k in range(16):
                    src = wt0 if k < 8 else wt
                    kk = k % 8
                    O(pe, lambda: P.matmul(psm[:, col * 2:col * 2 + 2], src[:, kk * 256 + jj * 128: kk * 256 + jj * 128 + 128],
                                           scT[:, 0:32].rearrange("p (r k) -> p k r", r=2)[:, k, :], start=(k == 0), stop=(k == 15)), r=[src, scT], w=[psm])
        bt = FP.next()
        ld(bt, bt[0:96, 0:128], w["b_ada"])
        ld(bt, bt[0:16, 128:256], w["n1"])
        ld(bt, bt[0:16, 256:384], w["n2"])
        pst = PS.next()
        O(pe, lambda: P.transpose(pst[:, 0:96], bt[0:96, 0:128], ident[0:96, 0:96]), r=[bt, ident], w=[pst])
        O(pe, lambda: P.transpose(pst[:, 96:112], bt[0:16, 128:256], ident[0:16, 0:16]), r=[bt, ident], w=[pst])
        O(pe, lambda: P.transpose(pst[:, 112:128], bt[0:16, 256:384], ident[0:16, 0:16]), r=[bt, ident], w=[pst])
        bn = small.next()
        bn = SM.next()
        O(dve, lambda: V.tensor_copy(out=bn[:, 0:128], in_=pst[:, 0:128]), r=[pst], w=[bn])
        O(dve, lambda: V.tensor_tensor(out=modT[:, 0:192].rearrange("p (c r) -> p c r", r=2),
                                       in0=psm[:, 0:192].rearrange("p (c r) -> p c r", r=2),
                                       in1=bn[:, 0:96].unsqueeze(2).to_broadcast([128, 96, 2]), op=ALU.add), r=[psm, bn], w=[modT])

        def mv(m):
            return modT[:, m * 32:(m + 1) * 32].rearrange("p (j r) -> p j r", r=2)

        def pv(which):
            return prm[:, which * 32:(which + 1) * 32].rearrange("p (j r) -> p j r", r=2)
        for which, (msc, msh, nwo) in enumerate([(1, 0, 96), (4, 3, 112)]):
            base = 0 if which == 0 else 3
            O(dve, lambda: V.tensor_scalar(out=pv(base), in0=mv(msc), scalar1=1.0, scalar2=None, op0=ALU.add), r=[modT], w=[prm])
            O(dve, lambda: V.tensor_tensor(out=pv(base), in0=pv(base), in1=bn[:, nwo:nwo + 16].unsqueeze(2).to_broadcast([128, 16, 2]),
                                           op=ALU.mult), r=[prm, bn], w=[prm])
            O(dve, lambda: V.tensor_copy(out=pv(base + 1), in_=mv(msh)), r=[modT], w=[prm])
            O(dve, lambda: V.tensor_copy(out=pv(base + 2), in_=mv(msc + 1)), r=[modT], w=[prm])
        rows = [(3, 0), (4, 0), (5, 0), (3, 1), (4, 1), (5, 1), (2, 0), (2, 1)]
        with nc.allow_non_contiguous_dma(reason="param row layout"):
            for ri, (which, r) in enumerate(rows):
                src = prm[:, which * 32:(which + 1) * 32].rearrange("p (j r) -> p j r", r=2)[:, :, r:r + 1]
                DMA(sp, lambda: nc.sync.dma_start(out=vecs[ri:ri + 1, :].rearrange("o (j p) -> p j o", p=128), in_=src),
                    f"vecs{ri}", [prm], mw=[b_vecs])

    def bc_row(ri):
        t = FP.next()
        ld(t, t[:, :], vecs[ri:ri + 1, :].to_broadcast([128, D]), r=[b_vecs])
        return t

    def norm_tile(src_dram, ti, which, want_h2=None):
        xt = FP.next()
        ld(xt, xt[:, :], src_dram[ti * 128:(ti + 1) * 128, :], r=[b_xres] if src_dram is xres else [])
        sq = FP.next()
        st = small.next()
        O(act, lambda: S.activation(out=sq[:, :], in_=xt[:, :], func=AF.Square, accum_out=st[:, 0:1]), r=[xt], w=[sq, st])
        rstd_from_ss(st[:, 0:1], st[:, 1:2], D, st)
        O(act, lambda: S.activation(out=sq[:, :], in_=xt[:, :], func=AF.Copy, scale=st[:, 1:2]), r=[xt, st], w=[sq])
        return xt, sq

    def phase_norm1(l):
        src = xin if l == 0 else xres
        for g in range(9):
            ntile = 4 if g < 8 else 2
            r = 0 if g < 8 else 1
            hs = ABp.next()
            hv = hs[:, :].rearrange("p (k t) -> p k t", t=512)
            for tt in range(ntile):
                ti = g * 4 + tt
                xt, xn = norm_tile(src, ti, 0)
                for kq in range(4):
                    ps = PS.next()
                    for q in range(4):
                        k = kq * 4 + q
                        O(pe, lambda: P.transpose(ps[:, q * 128:(q + 1) * 128], xn[:, k * 128:(k + 1) * 128], ident[:]), r=[xn, ident], w=[ps])
                    for q in range(4):
                        k = kq * 4 + q
                        e = dve if q % 2 == 0 else pool
                        if q % 2 == 0:
                            O(dve, lambda: V.tensor_scalar(out=hv[:, k, tt * 128:(tt + 1) * 128], in0=ps[:, q * 128:(q + 1) * 128],
                                                           scalar1=prm_ap(0, k, r), scalar2=prm_ap(1, k, r), op0=ALU.mult, op1=ALU.add),
                              r=[ps, prm], w=[hs])
                        else:
                            O(act, lambda: S.activation(out=hv[:, k, tt * 128:(tt + 1) * 128], in_=ps[:, q * 128:(q + 1) * 128],
                                                        func=AF.Identity, scale=prm_ap(0, k, r), bias=prm_ap(1, k, r)),
                              r=[ps, prm], w=[hs])
            n = ntile * 128
            stq(hT[:, g * 512:g * 512 + n].rearrange("(k p) t -> p k t", p=128), hs, hv[:, :, 0:n], b_hT)

    def phase_inproj(l):
        w = W[l]
        def wload(cb):
            wt_ = WB.next()
            ld(wt_, wt_[:, :].rearrange("p (k c) -> p k c", c=512),
               w["w_in"][:, cb * 512:(cb + 1) * 512].rearrange("(k p) c -> p k c", p=128), cast=True)
            return wt_
        nxt = wload(0)
        for cb in range(24):
            wt = nxt
            wv = wt[:, :].rearrange("p (k c) -> p k c", c=512)
            if cb + 1 < 24:
                nxt = wload(cb + 1)
            for g in range(9):
                n = 512 if g < 8 else 256
                ht = ABp.next()
                hv = ht[:, :].rearrange("p (k t) -> p k t", t=512)
                ld(ht, hv[:, :, 0:n], hT[:, g * 512:g * 512 + n].rearrange("(k p) t -> p k t", p=128), r=[b_hT])
                if cb == 5:
                    for tt in range(n // 128):
                        ps = PS.next()
                        OG(pe, [(lambda k=k: P.matmul(ps[:, :], hv[:, k, tt * 128:(tt + 1) * 128], wv[:, k, :], start=(k == 0), stop=(k == 15)))
                                for k in range(16)], r=[ht, wt], w=[ps])
                        vt = small16.next()
                        O(act, lambda: S.copy(out=vt[:, 0:512], in_=ps[:, :]), r=[ps], w=[vt])
                        stq(vtok[g * 512 + tt * 128: g * 512 + (tt + 1) * 128, :], vt, vt[:, 0:512], b_vtok)
                    continue
                zt = small16.next()
                zv = zt[:, :].rearrange("p (j t) -> p j t", t=512)
                for j in range(4):
                    ps = PS.next()
                    OG(pe, [(lambda k=k: P.matmul(ps[:, 0:n], wv[:, k, j * 128:(j + 1) * 128], hv[:, k, 0:n], start=(k == 0), stop=(k == 15)))
                            for k in range(16)], r=[ht, wt], w=[ps])
                    if cb >= 8:
                        O(act, lambda: S.activation(out=zv[:, j, 0:n], in_=ps[:, 0:n], func=AF.Sigmoid), r=[ps], w=[zt])
                    else:
                        O(dve, lambda: V.tensor_copy(out=zv[:, j, 0:n], in_=ps[:, 0:n]), r=[ps], w=[zt])
                stq(zT[cb * 512:(cb + 1) * 512, g * 512:g * 512 + n].rearrange("(j p) t -> p j t", p=128), zt, zv[:, :, 0:n], b_zT)

    def seg_list():
        if cur["last"]:
            return [(0, T)]
        return [(0, T), (T, L)]

    def phase_conv(l):
        w = W[l]
        cw = small.next()
        with nc.allow_non_contiguous_dma(reason="tiny"):
            for k_ in range(3):
                ld(cw, cw[:, k_ * 4:(k_ + 1) * 4], w["conv"][k_].rearrange("(j p) -> p j", p=128))
        for j in range(4):
            for (s0, n) in seg_list():
                for c0 in range(0, n, 1024):
                    m = min(1024, n - c0)
                    xa = small16.next(); gb = small16.next(); gc = small16.next()
                    lo = 1 if c0 > 0 else 0
                    hi = 1 if c0 + m < n else 0
                    for tl, row in ((xa, 0), (gc, 1024)):
                        if not lo:
                            O(pool, lambda: G.memset(tl[:, 0:1], 0.0), w=[tl])
                        if not hi:
                            O(pool, lambda: G.memset(tl[:, m + 1:m + 2], 0.0), w=[tl])
                        ld(tl, tl[:, 1 - lo:m + 1 + hi], zT[row + j * 128: row + (j + 1) * 128, s0 + c0 - lo: s0 + c0 + m + hi], r=[b_zT])
                    ld(gb, gb[:, 0:m], zT[512 + j * 128: 512 + (j + 1) * 128, s0 + c0: s0 + c0 + m], r=[b_zT])
                    u = SM.next()
                    u2 = SM.next()
                    O(dve, lambda: V.tensor_tensor(out=u[:, 0:m + 2], in0=xa[:, 0:m + 2], in1=gc[:, 0:m + 2], op=ALU.mult), r=[xa, gc], w=[u])
                    O(dve, lambda: V.tensor_scalar(out=u2[:, 0:m], in0=u[:, 1:m + 1], scalar1=cw[:, 4 + j:5 + j], scalar2=None, op0=ALU.mult), r=[u, cw], w=[u2])
                    O(dve, lambda: V.scalar_tensor_tensor(out=u2[:, 0:m], in0=u[:, 0:m], scalar=cw[:, j:j + 1], in1=u2[:, 0:m], op0=ALU.mult, op1=ALU.add), r=[u, cw, u2], w=[u2])
                    O(dve, lambda: V.scalar_tensor_tensor(out=u2[:, 0:m], in0=u[:, 2:m + 2], scalar=cw[:, 8 + j:9 + j], in1=u2[:, 0:m], op0=ALU.mult, op1=ALU.add), r=[u, cw, u2], w=[u2])
                    yo = small16.next()
                    O(dve, lambda: V.tensor_tensor(out=yo[:, 0:m], in0=u2[:, 0:m], in1=gb[:, 0:m], op=ALU.mult), r=[u2, gb], w=[yo])
                    stq(brT[j * 128:(j + 1) * 128, s0 + c0: s0 + c0 + m], yo, yo[:, 0:m], b_brT)

    def phase_pool(l):
        w = W[l]
        pw = WB.next()
        pwv = pw[:, 0:512].rearrange("p (g c) -> p g c", c=128)
        ld(pw, pwv, w["pool_w"].rearrange("g p c -> p g c"), cast=True)
        psc = small.next()
        with nc.allow_non_contiguous_dma(reason="tiny"):
            ld(psc, psc[:, 0:4], w["pool_s"].rearrange("(g p) o -> p (g o)", p=128))
        for g in range(4):
            win = (2, 4, 8, 16)[g]
            for (s0, n) in seg_list():
                for c0 in range(0, n, 512):
                    m = min(512, n - c0)
                    H = 16
                    lo = min(H, c0); hi = min(H, n - c0 - m)
                    zb = small16.next()
                    ld(zb, zb[:, H - lo:H + m + hi], zT[3584 + g * 128: 3584 + (g + 1) * 128, s0 + c0 - lo: s0 + c0 + m + hi], r=[b_zT])
                    u = SM.next(); a = SM.next(); b2 = SM.next()
                    O(pool, lambda: G.memset(u[:, 0:m + 2 * H], 0.0), w=[u])
                    O(dve, lambda: V.tensor_copy(out=u[:, H - lo:H + m + hi], in_=zb[:, H - lo:H + m + hi]), r=[zb, u], w=[u])
                    O(dve, lambda: V.tensor_tensor(out=a[:, 1:m + 2 * H], in0=u[:, 0:m + 2 * H - 1], in1=u[:, 1:m + 2 * H], op=ALU.add), r=[u], w=[a])
                    cur, oth = a, b2
                    lo_v = 1; hi_v = m + 2 * H
                    sh = 1
                    wdt = 2
                    while wdt < win:
                        nlo = lo_v + sh; nhi = hi_v - sh
                        O(dve, lambda: V.tensor_tensor(out=oth[:, nlo:nhi], in0=cur[:, nlo - sh:nhi - sh], in1=cur[:, nlo + sh:nhi + sh], op=ALU.add), r=[cur], w=[oth])
                        cur, oth = oth, cur
                        lo_v, hi_v = nlo, nhi
                        sh *= 2; wdt *= 2
                    ic = SM.next()
                    ld(ic, ic[:, 0:m], invcnt[g:g + 1, s0 + c0:s0 + c0 + m].to_broadcast([128, m]))
                    O(dve, lambda: V.tensor_tensor(out=oth[:, 0:m], in0=cur[:, H:H + m], in1=ic[:, 0:m], op=ALU.mult), r=[cur, ic], w=[oth])
                    pb = small16.next()
                    O(dve, lambda: V.tensor_tensor(out=pb[:, 0:m], in0=oth[:, 0:m], in1=u[:, H:H + m], op=ALU.subtract), r=[oth, u], w=[pb])
                    ps = PS.next()
                    O(pe, lambda: P.matmul(ps[:, 0:m], pwv[:, g, :], pb[:, 0:m], start=True, stop=True), r=[pw, pb], w=[ps])
                    yo = small16.next()
                    O(act, lambda: S.activation(out=yo[:, 0:m], in_=ps[:, 0:m], func=AF.Copy, scale=psc[:, g:g + 1]), r=[ps, psc], w=[yo])
                    stq(brT[1536 + g * 128:1536 + (g + 1) * 128, s0 + c0:s0 + c0 + m], yo, yo[:, 0:m], b_brT)

    def phase_fourier(l):
        ld(chd, chd[:, 0:256], chdft)
        for ti in range((T if cur["last"] else NT) // 128):
            zf = small16.next()
            ld(zf, zf[:, 0:512].rearrange("p (g t) -> p g t", t=128),
               zT[3072:3584, ti * 128:(ti + 1) * 128].rearrange("(g p) t -> p g t", p=128), r=[b_zT])
            ab = small16.next()
            for half in range(2):
                ps = PS.next()
                for gg in range(2):
                    g = half * 2 + gg
                    O(pe, lambda: P.matmul(ps[:, gg * 256:(gg + 1) * 256], zf[:, g * 128:(g + 1) * 128], chd[:, 0:256], start=True, stop=True),
                      r=[zf, chd], w=[ps])
                O(act if half else dve, (lambda: S.copy(out=ab[:, half * 512:(half + 1) * 512], in_=ps[:, :])) if half else
                  (lambda: V.tensor_copy(out=ab[:, half * 512:(half + 1) * 512], in_=ps[:, :])), r=[ps], w=[ab])
            stq(fab[ti * 128:(ti + 1) * 128, :, :].rearrange("t g c -> t (g c)"), ab, ab[:, 0:1024], b_fab)
        for (s0, n, tc, ts) in ((0, T, dftc, dfts), (T, L, dftcc, dftsc)):
            if cur["last"] and s0 == T:
                continue
            na = n // 128
            for g in range(4):
                abt = ABp.next()
                abv = abt[:, 0:na * 256].rearrange("p (a c) -> p a c", c=256)
                ld(abt, abv, fab[s0:s0 + n, g, :].rearrange("(a p) c -> p a c", p=128), r=[b_fab])
                for nb in range(n // 256):
                    ct = WB.next(); st_ = WB.next()
                    cv = ct[:, 0:na * 256].rearrange("p (a c) -> p a c", c=256)
                    sv = st_[:, 0:na * 256].rearrange("p (a c) -> p a c", c=256)
                    ld(ct, cv, tc[:, nb * 256:(nb + 1) * 256].rearrange("(a p) c -> p a c", p=128))
                    ld(st_, sv, ts[:, nb * 256:(nb + 1) * 256].rearrange("(a p) c -> p a c", p=128))
                    ps = PS.next()
                    fns = []
                    for a in range(na):
                        fns.append(lambda a=a: P.matmul(ps[:, 0:256], abv[:, a, 0:128], cv[:, a, :], start=(a == 0), stop=False))
                        fns.append(lambda a=a: P.matmul(ps[:, 0:256], abv[:, a, 128:256], sv[:, a, :], start=False, stop=(a == na - 1)))
                    OG(pe, fns, r=[abt, ct, st_], w=[ps])
                    yo = small16.next()
                    O(act, lambda: S.activation(out=yo[:, 0:256], in_=ps[:, 0:256], func=AF.Copy, scale=float((n * 128) ** -0.5)), r=[ps], w=[yo])
                    stq(brT[1024 + g * 128:1024 + (g + 1) * 128, s0 + nb * 256:s0 + (nb + 1) * 256], yo, yo[:, 0:256], b_brT)

    def phase_qknorm(l):
        w = W[l]
        qw = small.next()
        ld(qw, qw[:, 0:1], w["qn"])
        ld(qw, qw[:, 1:2], w["kn"])
        O(dve, lambda: V.tensor_scalar(out=qw[:, 0:1], in0=qw[:, 0:1], scalar1=NA_SCALE, scalar2=None, op0=ALU.mult), r=[qw], w=[qw])
        for g in range(9):
            n = 512 if g < 8 else 256
            for c in range(8):
                row = 1536 + c * 128 if c < 4 else 2048 + (c - 4) * 128
                z = small16.next()
                ld(z, z[:, 0:n], zT[row:row + 128, g * 512:g * 512 + n], r=[b_zT])
                sq = SM.next()
                O(act, lambda: S.activation(out=sq[:, 0:n], in_=z[:, 0:n], func=AF.Square), r=[z], w=[sq])
                ps = PS.next()
                O(pe, lambda: P.matmul(ps[:, 0:n], blk64[:, :], sq[:, 0:n], start=True, stop=True), r=[blk64, sq], w=[ps])
                rs = SM.next()
                O(dve, lambda: V.tensor_scalar(out=rs[:, 0:n], in0=ps[:, 0:n], scalar1=1.0 / 64, scalar2=EPS, op0=ALU.mult, op1=ALU.add), r=[ps], w=[rs])
                O(act, lambda: S.sqrt(out=rs[:, 0:n], in_=rs[:, 0:n]), r=[rs], w=[rs])
                O(dve, lambda: V.reciprocal(out=rs[:, 0:n], in_=rs[:, 0:n]), r=[rs], w=[rs])
                O(dve, lambda: V.tensor_tensor(out=rs[:, 0:n], in0=rs[:, 0:n], in1=z[:, 0:n], op=ALU.mult), r=[rs, z], w=[rs])
                zo = small16.next()
                O(act, lambda: S.activation(out=zo[:, 0:n], in_=rs[:, 0:n], func=AF.Copy, scale=qw[:, (0 if c < 4 else 1):(1 if c < 4 else 2)]), r=[rs, qw], w=[zo])
                stq(qkT[c * 128:(c + 1) * 128, g * 512:g * 512 + n], zo, zo[:, 0:n], b_qkT)

    def phase_attn(l):
        w = W[l]
        O(pool, lambda: G.memset(ones[:, 0:8], 1.0), w=[ones])
        onesb = ones
        for j in range((T if cur["last"] else NT) // 128):
            lat = j < 32
            if lat:
                ks = min(max(2 * j - 4, 0), 54)
                kt0 = ks // 2
                cls = {0: 0, 1: 1, 30: 3, 31: 4}.get(j, 2)
                nloc = 5
            else:
                nloc = 0
            nch = nloc + 2
            qv = qt[:, 0:512].rearrange("p (c t) -> p c t", t=128)
            ld(qt, qv, qkT[0:512, j * 128:(j + 1) * 128].rearrange("(c p) t -> p c t", p=128), r=[b_qkT])
            kt = ABp.next()
            kv = kt[:, 0:4 * 896].rearrange("p (c t) -> p c t", t=896)
            if lat:
                ld(kt, kv[:, :, 0:640], qkT[512:1024, kt0 * 128:kt0 * 128 + 640].rearrange("(c p) t -> p c t", p=128), r=[b_qkT])
            ld(kt, kv[:, :, nloc * 128:nloc * 128 + 256], qkT[512:1024, T:T + 256].rearrange("(c p) t -> p c t", p=128), r=[b_qkT])
            vt = ABp.next()
            vv = vt[:, 0:7 * 512].rearrange("p (a c) -> p a c", c=512)
            if lat:
                ld(vt, vv[:, 0:5, :], vtok[kt0 * 128:kt0 * 128 + 640, :].rearrange("(a p) c -> p a c", p=128), r=[b_vtok])
            ld(vt, vv[:, nloc:nloc + 2, :], vtok[T:T + 256, :].rearrange("(a p) c -> p a c", p=128), r=[b_vtok])
            for h in range(8):
                pc, pb = h // 2, (h % 2) * 64
                psA = PS.next(); psB = PS.next()

                def sdst(c):
                    return (psA if c < 4 else psB)[:, (c % 4) * 128:(c % 4 + 1) * 128]
                OG(pe, [(lambda c=c: P.matmul(sdst(c), kv[pb:pb + 64, pc, c * 128:(c + 1) * 128], qv[pb:pb + 64, pc, :], start=True, stop=True))
                        for c in range(nch)], r=[kt, qt], w=[psA, psB])
                pt = small16.next()
                if lat:
                    bt = SM.next()
                    bv = bt[:, 0:640].rearrange("p (c q) -> p c q", q=128)
                    ld(bt, bv, w["bias"][cls, h].rearrange("(c p) q -> p c q", p=128))
                    O(dve, lambda: V.tensor_tensor(out=bt[:, 0:512], in0=bt[:, 0:512], in1=psA[:, 0:512], op=ALU.add), r=[bt, psA], w=[bt])
                    O(dve, lambda: V.tensor_tensor(out=bt[:, 512:640], in0=bt[:, 512:640], in1=psB[:, 0:128], op=ALU.add), r=[bt, psB], w=[bt])
                    O(act, lambda: S.activation(out=pt[:, 0:640], in_=bt[:, 0:640], func=AF.Exp), r=[bt], w=[pt])
                    O(act, lambda: S.activation(out=pt[:, 640:896], in_=psB[:, 128:384], func=AF.Exp), r=[psB], w=[pt])
                else:
                    O(act, lambda: S.activation(out=pt[:, 0:256], in_=psA[:, 0:256], func=AF.Exp), r=[psA], w=[pt])
                OG(pe, [(lambda c=c: P.matmul(psO[:, h * 64:(h + 1) * 64], pt[:, c * 128:(c + 1) * 128], vv[:, c, h * 64:(h + 1) * 64],
                                           start=(c == 0), stop=(c == nch - 1))) for c in range(nch)], r=[pt, vt], w=[psO])
                OG(pe, [(lambda c=c: P.matmul(psD[:, h:h + 1], pt[:, c * 128:(c + 1) * 128], onesb[:, 0:1],
                                           start=(c == 0), stop=(c == nch - 1))) for c in range(nch)], r=[pt, onesb], w=[psD])
            rc = small.next()
            O(dve, lambda: V.reciprocal(out=rc[:, 0:8], in_=psD[:, 0:8]), r=[psD], w=[rc])
            yt = small16.next()
            O(dve, lambda: V.tensor_tensor(out=yt[:, 0:512].rearrange("p (h d) -> p h d", d=64), in0=psO[:, 0:512].rearrange("p (h d) -> p h d", d=64),
                                           in1=rc[:, 0:8].unsqueeze(2).to_broadcast([128, 8, 64]), op=ALU.mult), r=[psO, rc], w=[yt])
            pst = PSB.next()
            for c in range(4):
                O(pe, lambda: P.transpose(pst[:, c * 128:(c + 1) * 128], yt[:, c * 128:(c + 1) * 128], identb[:]), r=[yt, identb], w=[pst])
            yo = small16.next()
            O(act, lambda: S.copy(out=yo[:, 0:512], in_=pst[:, 0:512]), r=[pst], w=[yo])
            stq(brT[512:1024, j * 128:(j + 1) * 128].rearrange("(c p) t -> p c t", p=128), yo, yo[:, 0:512].rearrange("p (c t) -> p c t", t=128), b_brT)

    def phase_merge(l):
        w = W[l]
        src = xin if l == 0 else xres
        ld(bcA, bcA[:, :], vecs[6:7, :].to_broadcast([128, D]), r=[b_vecs])
        ld(bcB, bcB[:, :], vecs[7:8, :].to_broadcast([128, D]), r=[b_vecs])
        for g in range(8 if cur["last"] else 9):
            n = 512 if g < 8 else 256
            gbc = bcA if g < 8 else bcB
            bt = ABp.next()
            bv = bt[:, :].rearrange("p (k t) -> p k t", t=512)
            ld(bt, bv[:, :, 0:n], brT[:, g * 512:g * 512 + n].rearrange("(k p) t -> p k t", p=128), r=[b_brT])
            mt = ABp.next()
            mv_ = mt[:, :].rearrange("p (k t) -> p k t", t=512)
            for dj in range(16):
                wb = small16.next()
                wbv = wb[:, :].rearrange("p (k c) -> p k c", c=128)
                ld(wb, wbv, wbr16[dj], r=[b_wbr])
                gt = small16.next()
                gv = gt[:, :].rearrange("p (i t) -> p i t", t=512)
                ld(gt, gv[:, :, 0:n], zT[4096:4096 + 4 * D, g * 512:g * 512 + n].rearrange("(i d) t -> d i t", d=D)[dj * 128:(dj + 1) * 128], r=[b_zT])
                acc = SM.next()
                for i in range(4):
                    ps = PS.next()
                    OG(pe, [(lambda kk=kk: P.matmul(ps[:, 0:n], wbv[:, i * 4 + kk, :], bv[:, i * 4 + kk, 0:n], start=(kk == 0), stop=(kk == 3))) for kk in range(4)], r=[wb, bt], w=[ps])
                    if i == 0:
                        O(dve, lambda: V.tensor_tensor(out=acc[:, 0:n], in0=ps[:, 0:n], in1=gv[:, i, 0:n], op=ALU.mult), r=[ps, gt], w=[acc])
                    else:
                        tmp = SM.next()
                        O(dve, lambda: V.tensor_tensor(out=tmp[:, 0:n], in0=ps[:, 0:n], in1=gv[:, i, 0:n], op=ALU.mult), r=[ps, gt], w=[tmp])
                        O(pool, lambda: G.tensor_tensor(out=acc[:, 0:n], in0=acc[:, 0:n], in1=tmp[:, 0:n], op=ALU.add), r=[acc, tmp], w=[acc])
                O(act, lambda: S.copy(out=mv_[:, dj, 0:n], in_=acc[:, 0:n]), r=[acc], w=[mt])
            for db in range(4):
                wo = WB.next()
                wov = wo[:, :].rearrange("p (k c) -> p k c", c=512)
                ld(wo, wov, wout16[db], r=[b_wout])
                for tt in range(n // 128):
                    ti = g * 4 + tt
                    ps = PS.next()
                    OG(pe, [(lambda k=k: P.matmul(ps[:, :], mv_[:, k, tt * 128:(tt + 1) * 128], wov[:, k, :], start=(k == 0), stop=(k == 15))) for k in range(16)], r=[mt, wo], w=[ps])
                    xs = SM.next()
                    ld(xs, xs[:, 0:512], src[ti * 128:(ti + 1) * 128, db * 512:(db + 1) * 512], r=[b_xres] if l > 0 else [])
                    tmp = SM.next()
                    O(dve, lambda: V.tensor_tensor(out=tmp[:, 0:512], in0=ps[:, :], in1=gbc[:, db * 512:(db + 1) * 512], op=ALU.mult), r=[ps, gbc], w=[tmp])
                    O(pool, lambda: G.tensor_tensor(out=tmp[:, 0:512], in0=tmp[:, 0:512], in1=xs[:, 0:512], op=ALU.add), r=[tmp, xs], w=[tmp])
                    stq(xres[ti * 128:(ti + 1) * 128, db * 512:(db + 1) * 512], tmp, tmp[:, 0:512], b_xres)

    def phase_norm2_router(l):
        w = W[l]
        ld(wr, wr[:, 0:256].rearrange("p (k e) -> p k e", e=16), w["w_rt"].rearrange("(k p) e -> p k e", p=128))
        for ti in range((T if cur["last"] else NT) // 128):
            r = 0 if ti < 32 else 1
            if ti == 0 or ti == 32:
                ld(bcA, bcA[:, :], vecs[3 * r:3 * r + 1, :].to_broadcast([128, D]), r=[b_vecs])
                ld(bcB, bcB[:, :], vecs[3 * r + 1:3 * r + 2, :].to_broadcast([128, D]), r=[b_vecs])
            xt, xn = norm_tile(xres, ti, 1)
            h2 = xt
            O(dve, lambda: V.tensor_tensor(out=h2[:, :], in0=xn[:, :], in1=bcA[:, :], op=ALU.mult), r=[xn, bcA], w=[h2])
            O(pool, lambda: G.tensor_tensor(out=h2[:, :], in0=h2[:, :], in1=bcB[:, :], op=ALU.add), r=[h2, bcB], w=[h2])
            hb = small16.next()
            O(act, lambda: S.copy(out=hb[:, :], in_=h2[:, :]), r=[h2], w=[hb])
            stq(h2tok[ti * 128:(ti + 1) * 128, :], hb, hb[:, :], b_h2)
            hT_ = xn
            for kq in range(4):
                ps = PS.next()
                for q in range(4):
                    k = kq * 4 + q
                    O(pe, lambda: P.transpose(ps[:, q * 128:(q + 1) * 128], h2[:, k * 128:(k + 1) * 128], ident[:]), r=[h2, ident], w=[ps])
                O(dve if kq % 2 else act, (lambda: V.tensor_copy(out=hT_[:, kq * 512:(kq + 1) * 512], in_=ps[:, :])) if kq % 2 else
                  (lambda: S.copy(out=hT_[:, kq * 512:(kq + 1) * 512], in_=ps[:, :])), r=[ps], w=[hT_])
            psl = PS.next()
            OG(pe, [(lambda k=k: P.matmul(psl[:, 0:16], hT_[:, k * 128:(k + 1) * 128], wr[:, k * 16:(k + 1) * 16], start=(k == 0), stop=(k == 15))) for k in range(16)], r=[hT_, wr], w=[psl])
            sm_ = small.next()
            O(dve, lambda: V.reduce_max(out=sm_[:, 16:17], in_=psl[:, 0:16], axis=AX.X), r=[psl], w=[sm_])
            O(dve, lambda: V.tensor_scalar(out=sm_[:, 16:17], in0=sm_[:, 16:17], scalar1=-1.0, scalar2=None, op0=ALU.mult), r=[sm_], w=[sm_])
            O(act, lambda: S.activation(out=sm_[:, 0:16], in_=psl[:, 0:16], func=AF.Exp, bias=sm_[:, 16:17], accum_out=sm_[:, 17:18]), r=[psl, sm_], w=[sm_])
            O(dve, lambda: V.reciprocal(out=sm_[:, 18:19], in_=sm_[:, 17:18]), r=[sm_], w=[sm_])
            O(dve, lambda: V.tensor_scalar(out=sm_[:, 0:16], in0=sm_[:, 0:16], scalar1=sm_[:, 18:19], scalar2=None, op0=ALU.mult), r=[sm_], w=[sm_])
            pst = PS.next()
            O(pe, lambda: P.transpose(pst[0:16, 0:128], sm_[:, 0:16], ident[:]), r=[sm_, ident], w=[pst])
            O(dve, lambda: V.tensor_copy(out=affT[0:16, ti * 128:(ti + 1) * 128], in_=pst[0:16, 0:128]), r=[pst], w=[affT])

    def phase_topk():
        for (c0, n, cap, o0) in ((0, T, CAP, 0), (T, L, CAPC, CAP)):
            if cur["last"] and c0 == T:
                continue
            for it in range(cap // 8):
                vs = tv[0:16, o0 + it * 8:o0 + it * 8 + 8]
                O(dve, lambda: V.max(out=vs, in_=affT[0:16, c0:c0 + n]), r=[affT], w=[tv])
                O(dve, lambda: V.max_index(out=tix[0:16, o0 + it * 8:o0 + it * 8 + 8], in_max=vs, in_values=affT[0:16, c0:c0 + n]), r=[affT, tv], w=[tix])
                O(dve, lambda: V.match_replace(out=affT[0:16, c0:c0 + n], in_to_replace=vs, in_values=affT[0:16, c0:c0 + n], imm_value=-1.0), r=[tv, affT], w=[affT])
        tf = SM.next()
        O(dve, lambda: V.tensor_copy(out=tf[0:16, 0:SLOTS], in_=tix[0:16, 0:SLOTS]), r=[tix], w=[tf])
        O(dve, lambda: V.tensor_scalar(out=tf[0:16, CAP:SLOTS], in0=tf[0:16, CAP:SLOTS], scalar1=float(T), scalar2=None, op0=ALU.add), r=[tf], w=[tf])
        O(pool, lambda: G.iota(idxF[:, 64:80], pattern=[[0, 16]], base=NT, channel_multiplier=1, allow_small_or_imprecise_dtypes=True), w=[idxF])
        O(pool, lambda: G.memset(gateT[:, :], 0.0), w=[gateT])
        for s in range(5):
            n = 128 if s < 4 else CAPC
            ps = PS.next()
            O(pe, lambda: P.transpose(ps[0:n, 0:16], tf[0:16, s * 128:s * 128 + n], ident[0:16, 0:16]), r=[tf, ident], w=[ps])
            O(pe, lambda: P.transpose(ps[0:n, 16:32], tv[0:16, s * 128:s * 128 + n], ident[0:16, 0:16]), r=[tv, ident], w=[ps])
            O(dve, lambda: V.tensor_copy(out=idxF[0:n, s * 16:(s + 1) * 16], in_=ps[0:n, 0:16]), r=[ps, idxF], w=[idxF])
            O(dve, lambda: V.tensor_copy(out=gateT[0:n, s * 16:(s + 1) * 16], in_=ps[0:n, 16:32]), r=[ps, gateT], w=[gateT])
        O(dve, lambda: V.tensor_copy(out=idxT[:, 0:80], in_=idxF[:, 0:80]), r=[idxF], w=[idxT])
        for db in range(4):
            tq = small.next()
            O(dve, lambda: V.tensor_scalar(out=tq[:, 0:80].bitcast(F32) if False else idxF4[:, 0:80], in0=idxF[:, 0:80], scalar1=4.0, scalar2=float(db), op0=ALU.mult, op1=ALU.add), r=[idxF], w=[idxF4])
            O(dve, lambda: V.tensor_copy(out=idxT4[:, db * 80:(db + 1) * 80], in_=idxF4[:, 0:80]), r=[idxF4], w=[idxT4])

    nsc = [0]
    cur = {"last": False}

    def phase_moe(l):
        w = W[l]
        ld(bcA, bcA[:, :], vecs[2:3, :].to_broadcast([128, D]), r=[b_vecs])
        ld(bcB, bcB[:, :], vecs[5:6, :].to_broadcast([128, D]), r=[b_vecs])
        for e in range(NE):
            xs = ABp.next()
            xsv = xs[:, :].rearrange("p (k t) -> p k t", t=512)
            xcv = xc[:, 0:512].rearrange("p (k t) -> p k t", t=32)
            nst = 4 if cur["last"] else 5
            for s in range(nst):
                xg = small16.next()
                DMA(pool, lambda: G.indirect_dma_start(out=xg[:, :], out_offset=None, in_=h2tok,
                                                       in_offset=bass.IndirectOffsetOnAxis(ap=idxT[:, s * 16 + e:s * 16 + e + 1], axis=0)),
                    xg.name, [idxT, b_h2], [xg])
                for kh in range(2):
                    pst = PSB.next()
                    for q in range(8):
                        k = kh * 8 + q
                        O(pe, lambda: P.transpose(pst[:, q * 128:(q + 1) * 128], xg[:, k * 128:(k + 1) * 128], identb[:]), r=[xg, identb], w=[pst])
                    pv_ = pst[:, :].rearrange("p (q t) -> p q t", t=128)
                    if s < 4:
                        O(act if kh else dve, (lambda: S.copy(out=xsv[:, kh * 8:(kh + 1) * 8, s * 128:(s + 1) * 128], in_=pv_)) if kh else
                          (lambda: V.tensor_copy(out=xsv[:, kh * 8:(kh + 1) * 8, s * 128:(s + 1) * 128], in_=pv_)), r=[pst], w=[xs])
                    else:
                        O(dve, lambda: V.tensor_copy(out=xcv[:, kh * 8:(kh + 1) * 8, :], in_=pv_[:, :, 0:32]), r=[pst], w=[xc])
            if MOE_STOP == "gather":
                return
            at = ABp.next()
            atv = at[:, :].rearrange("p (k t) -> p k t", t=512)
            acv = ac[:, 0:512].rearrange("p (k t) -> p k t", t=32)

            def gu_load(fb):
                a_, b_ = WH[(fb % 2) * 2], WH[(fb % 2) * 2 + 1]
                ld(a_, a_[:, :].rearrange("p (k c) -> p k c", c=256), w["weg"][e, :, fb * 256:(fb + 1) * 256].rearrange("(k p) c -> p k c", p=128), cast=True)
                ld(b_, b_[:, :].rearrange("p (k c) -> p k c", c=256), w["weu"][e, :, fb * 256:(fb + 1) * 256].rearrange("(k p) c -> p k c", p=128), cast=True)
                return a_, b_

            def d_load(db):
                t_ = WB.tiles[db % 2]
                ld(t_, t_[:, :].rearrange("p (k c) -> p k c", c=512), w["wed"][e, :, db * 512:(db + 1) * 512].rearrange("(k p) c -> p k c", p=128), cast=True)
                return t_
            nxt = gu_load(0)
            nxt_d = None
            for fb in range(8):
                wg, wu = nxt
                wgv = wg[:, :].rearrange("p (k c) -> p k c", c=256)
                wuv = wu[:, :].rearrange("p (k c) -> p k c", c=256)
                if fb + 1 < 8:
                    nxt = gu_load(fb + 1)
                for fj in range(2):
                    f = fb * 2 + fj
                    pa = PS.next(); pu = PS.next(); pc_ = PS.next()
                    OG(pe, [(lambda k=k: P.matmul(pa[:, :], wgv[:, k, fj * 128:(fj + 1) * 128], xsv[:, k, :], start=(k == 0), stop=(k == 15))) for k in range(16)], r=[wg, xs], w=[pa])
                    OG(pe, [(lambda k=k: P.matmul(pu[:, :], wuv[:, k, fj * 128:(fj + 1) * 128], xsv[:, k, :], start=(k == 0), stop=(k == 15))) for k in range(16)], r=[wu, xs], w=[pu])
                    if nst == 5:
                        OG(pe, [(lambda k=k: P.matmul(pc_[:, 0:32], wgv[:, k, fj * 128:(fj + 1) * 128], xcv[:, k, :], start=(k == 0), stop=(k == 15))) for k in range(16)], r=[wg, xc], w=[pc_])
                        OG(pe, [(lambda k=k: P.matmul(pc_[:, 32:64], wuv[:, k, fj * 128:(fj + 1) * 128], xcv[:, k, :], start=(k == 0), stop=(k == 15))) for k in range(16)], r=[wu, xc], w=[pc_])
                    sa = SM.next()
                    O(act, lambda: S.activation(out=sa[:, 0:512], in_=pa[:, :], func=AF.Silu), r=[pa], w=[sa])
                    O(dve, lambda: V.tensor_tensor(out=atv[:, f, :], in0=sa[:, 0:512], in1=pu[:, :], op=ALU.mult), r=[sa, pu], w=[at])
                    if nst == 5:
                        O(act, lambda: S.activation(out=sa[:, 512:544], in_=pc_[:, 0:32], func=AF.Silu), r=[pc_, sa], w=[sa])
                        O(dve, lambda: V.tensor_tensor(out=acv[:, f, :], in0=sa[:, 512:544], in1=pc_[:, 32:64], op=ALU.mult), r=[sa, pc_], w=[ac])
                if fb == 6:
                    nxt_d = d_load(0)
            if MOE_STOP == "gateup":
                return
            for db in range(4):
                wd = nxt_d
                wdv = wd[:, :].rearrange("p (k c) -> p k c", c=512)
                if db + 1 < 4:
                    nxt_d = d_load(db + 1)
                for s in range(nst):
                    n = 128 if s < 4 else CAPC
                    ps = PS.next()
                    OG(pe, [(lambda f=f: P.matmul(ps[0:n, :], (atv[:, f, s * 128:(s + 1) * 128] if s < 4 else acv[:, f, :]), wdv[:, f, :],
                                                  start=(f == 0), stop=(f == 15))) for f in range(16)], r=[at if s < 4 else ac, wd], w=[ps])
                    yo = SM.next()
                    if s == 4:
                        O(pool, lambda: G.memset(yo[:, 0:512], 0.0), w=[yo])
                    gbc = bcA if s < 4 else bcB
                    O(dve, lambda: V.scalar_tensor_tensor(out=yo[0:n, 0:512], in0=ps[0:n, :], scalar=gateT[0:n, s * 16 + e:s * 16 + e + 1],
                                                          in1=gbc[0:n, db * 512:(db + 1) * 512], op0=ALU.mult, op1=ALU.mult), r=[ps, gateT, gbc, yo], w=[yo])
                    DMA(pool, lambda: G.indirect_dma_start(out=xres.rearrange("n (q c) -> (n q) c", c=512),
                                                           out_offset=bass.IndirectOffsetOnAxis(ap=idxT4[:, db * 80 + s * 16 + e:db * 80 + s * 16 + e + 1], axis=0),
                                                           in_=yo[:, 0:512], in_offset=None, compute_op=ALU.add),
                        yo.name + "_sc", [idxT4, yo], [b_xres])

    for l in range(NL):
        cur["last"] = (l == NL - 1) and NL > 1
        phase_precast(l)
        phase_mod(l)
        if dbg and l == 0:
            dbgv = nc.dram_tensor("dbgv", [128, 448], F32, kind="ExternalOutput").ap()
            DMA(sp, lambda: nc.sync.dma_start(out=dbgv[:, 0:192], in_=modT[:, :]), "dbg0", [modT], [b_out])
            DMA(sp, lambda: nc.sync.dma_start(out=dbgv[:, 192:448], in_=prm[:, :]), "dbg1", [prm], [b_out])
        phase_norm1(l)
        phase_inproj(l)
        phase_conv(l)
        phase_pool(l)
        phase_fourier(l)
        phase_qknorm(l)
        phase_attn(l)
        phase_merge(l)
        if do_moe:
            phase_norm2_router(l)
            phase_topk()
            if do_moe == "topk":
                dbi = nc.dram_tensor("dbi", [128, 80], I32, kind="ExternalOutput").ap()
                dbg_ = nc.dram_tensor("dbgate", [128, 80], F32, kind="ExternalOutput").ap()
                DMA(sp, lambda: nc.sync.dma_start(out=dbi, in_=idxT[:, :]), "dbg2", [idxT], [b_out])
                DMA(sp, lambda: nc.sync.dma_start(out=dbg_, in_=gateT[:, :]), "dbg3", [gateT], [b_out])
                continue
            phase_moe(l)
    DMA(sp, lambda: nc.sync.dma_start(out=out, in_=xres[0:T, :]), "final", [b_xres], [b_out])
    fw.wait_all(sp, [b_out])
    if dbg:
        top = sorted(((v[1], k) for k, v in fw.dma_sems.items()), reverse=True)[:6]
        print("nsem", fw.nsem, "top dma sem counts", top, "eng counts", [(e.name, e.cnt) for e in (pe, act, dve, pool, sp)], flush=True)
    return nc


def _const_tables():
    bf = ml_dtypes.bfloat16
    n = np.arange(T, dtype=np.int64)
    ph = (np.outer(n, n) % T).astype(np.float64) * (2 * np.pi / T)
    dftc = np.cos(ph).astype(np.float32).astype(bf)
    dfts = (-np.sin(ph)).astype(np.float32).astype(bf)
    m = np.arange(L, dtype=np.int64)
    phc = (np.outer(m, m) % L).astype(np.float64) * (2 * np.pi / L)
    dftcc = np.cos(phc).astype(np.float32).astype(bf)
    dftsc = (-np.sin(phc)).astype(np.float32).astype(bf)
    c = np.arange(128, dtype=np.int64)
    pc = (np.outer(c, c) % 128).astype(np.float64) * (2 * np.pi / 128)
    chdft = np.concatenate([np.cos(pc), np.sin(pc)], axis=1).astype(np.float32).astype(bf)
    inv = np.zeros((4, NT), np.float32)
    for g, wdt in enumerate((2, 4, 8, 16)):
        for s0, nn in ((0, T), (T, L)):
            t = np.arange(nn)
            lo = np.clip(t - wdt // 2, 0, nn - 1)
            hi = np.clip(t + wdt - wdt // 2 - 1, 0, nn - 1)
            inv[g, s0:s0 + nn] = 1.0 / (hi - lo + 1)
    return dict(dftc=dftc, dfts=dfts, dftcc=dftcc, dftsc=dftsc, chdft=chdft, invcnt=inv)


def _bias_table(rpb):
    out = np.full((5, 8, 640, 128), NEG, np.float32)
    ql = np.arange(128)
    kl = np.arange(640)
    for cls, j in enumerate((0, 1, 2, 30, 31)):
        ks = min(max(2 * j - 4, 0), 54)
        r = 2 * j + ql // 64
        c = ql % 64
        kr = ks + kl // 64
        kc = kl % 64
        rs = np.clip(r - 4, 0, 56)
        cs = np.clip(c - 8, 0, 48)
        ok = ((kr[:, None] >= rs[None, :]) & (kr[:, None] < rs[None, :] + 8) &
              (kc[:, None] >= cs[None, :]) & (kc[:, None] < cs[None, :] + 16))
        dr = np.clip(kr[:, None] - r[None, :] + 7, 0, 14)
        dc = np.clip(kc[:, None] - c[None, :] + 15, 0, 30)
        for h in range(8):
            gath = rpb[h][dr, dc]
            out[cls, h] = np.where(ok, gath, np.float32(NEG))
    return out


def _layer_inputs(inp, l):
    f = lambda a: np.ascontiguousarray(a, dtype=np.float32)
    return {
        f"w_ada{l}": f(inp["w_ada"][l]), f"b_ada{l}": f(inp["b_ada"][l].reshape(96, 128)),
        f"n1_{l}": f(inp["norm1_w"][l].reshape(16, 128)), f"n2_{l}": f(inp["norm2_w"][l].reshape(16, 128)),
        f"w_in{l}": f(inp["w_in"][l]), f"conv{l}": f(inp["conv_w"][l]),
        f"qn{l}": f(np.tile(inp["q_norm_w"][l], 2).reshape(128, 1)), f"kn{l}": f(np.tile(inp["k_norm_w"][l], 2).reshape(128, 1)),
        f"bias{l}": _bias_table(np.asarray(inp["na_rpb"][l], np.float32)),
        f"pool_w{l}": f(inp["pool_w"][l]), f"pool_s{l}": f(inp["pool_scale"][l].reshape(MIXW, 1)),
        f"w_br{l}": f(inp["w_branch"][l].reshape(D, D)), f"w_out{l}": f(inp["w_out"][l]),
        f"w_rt{l}": f(inp["w_router"][l]),
        f"weg{l}": f(inp["w_exp_gate"][l]), f"weu{l}": f(inp["w_exp_up"][l]), f"wed{l}": f(inp["w_exp_down"][l]),
    }


def make_in_maps(inp, cores, NL=2):
    shared = _const_tables()
    for l in range(NL):
        shared.update(_layer_inputs(inp, l))
    maps = []
    for b in cores:
        m = dict(shared)
        m["xin"] = np.ascontiguousarray(np.concatenate([inp["x"][b], inp["ctx"][b]], axis=0), dtype=np.float32)
        m["cvec"] = np.ascontiguousarray(np.stack([inp["c"][b], inp["c_ctx"]]), dtype=np.float32)
        maps.append(m)
    return maps


def kernel(**inputs):
    inp = {k: np.asarray(v) for k, v in inputs.items()}
    nc = build(NL=2)
    maps = make_in_maps(inp, list(range(8)))
    res = run_bass_kernel_spmd(nc, maps, core_ids=list(range(8)))
    return np.stack([np.asarray(r["out"], dtype=np.float32) for r in res.results], axis=0)
```

```python
import numpy as np
import ml_dtypes
import concourse.bass as bass
import concourse.mybir as mybir
from concourse.bass_utils import run_bass_kernel_spmd

F32 = mybir.dt.float32
BF16 = mybir.dt.bfloat16
U32 = mybir.dt.uint32
I32 = mybir.dt.int32
ALU = mybir.AluOpType
AF = mybir.ActivationFunctionType
AX = mybir.AxisListType

SEM_ROT = 24000


class Buf:
    __slots__ = ("name", "lw", "rd", "mo")

    def __init__(self, name):
        self.name = name
        self.lw = {}
        self.rd = {}
        self.mo = False


class Eng:
    def __init__(self, fw, e, name):
        self.fw = fw
        self.e = e
        self.name = name
        self.sem = fw.new_sem(name)
        self.cnt = 0
        self.seen = {}

    def _rotate(self):
        if self.cnt >= SEM_ROT:
            self.sem = self.fw.new_sem(self.name)
            self.cnt = 0


class FW:
    def __init__(self, nc):
        self.nc = nc
        self.nsem = 0
        self.sem_objs = []
        self.pe = Eng(self, nc.tensor, "pe")
        self.act = Eng(self, nc.scalar, "act")
        self.dve = Eng(self, nc.vector, "dve")
        self.pool = Eng(self, nc.gpsimd, "pool")
        self.sp = Eng(self, nc.sync, "sp")
        self.dma_sems = {}

    def new_sem(self, name):
        self.nsem += 1
        cm = self.nc.semaphore(f"{name}_{self.nsem}")
        s = cm.__enter__()
        self.sem_objs.append(s)
        return s

    def _waits(self, eng, reads, writes, extra=(), mwrites=()):
        need = {}

        def merge(d):
            for s, v in d.items():
                if need.get(s, 0) < v:
                    need[s] = v
        for b in reads:
            merge(b.lw)
        for b in writes:
            merge(b.lw)
            merge(b.rd)
        for b in mwrites:
            merge(b.rd)
            if not b.mo:
                merge(b.lw)
        for d in extra:
            merge(d)
        for s, v in need.items():
            if eng.seen.get(s, 0) < v:
                eng.e.wait_ge(s, v)
                eng.seen[s] = v

    def _commit(self, ev, reads, writes, mwrites):
        for b in writes:
            b.lw = {ev[0]: ev[1]}
            b.rd = {}
            b.mo = False
        for b in mwrites:
            if b.rd or not b.mo:
                b.lw = {}
            b.lw[ev[0]] = ev[1]
            b.rd = {}
            b.mo = True
        for b in reads:
            if b.rd.get(ev[0], 0) < ev[1]:
                b.rd[ev[0]] = ev[1]

    def op(self, eng, fn, reads=(), writes=(), mwrites=()):
        eng._rotate()
        self._waits(eng, reads, writes, mwrites=mwrites)
        ins = fn()
        eng.cnt += 1
        ins.then_inc(eng.sem, 1)
        self._commit((eng.sem, eng.cnt), reads, writes, mwrites)
        return ins

    def op_group(self, eng, fns, reads=(), writes=()):
        eng._rotate()
        self._waits(eng, reads, writes)
        for fn in fns[:-1]:
            fn()
        ins = fns[-1]()
        eng.cnt += 1
        ins.then_inc(eng.sem, 1)
        self._commit((eng.sem, eng.cnt), reads, writes, ())
        return ins

    def dma(self, eng, fn, key, reads=(), writes=(), mwrites=()):
        if key not in self.dma_sems:
            self.dma_sems[key] = [self.new_sem("d"), 0]
        ent = self.dma_sems[key]
        prev = {ent[0]: ent[1]} if ent[1] else {}
        self._waits(eng, reads, writes, extra=(prev,), mwrites=mwrites)
        ins = fn()
        ent[1] += 16
        ins.then_inc(ent[0], 16)
        self._commit((ent[0], ent[1]), reads, writes, mwrites)
        return ins

    def wait_all(self, eng, bufs):
        self._waits(eng, bufs, bufs)


class Tile:
    def __init__(self, fw, kind, name, shape, dtype):
        nc = fw.nc
        cm = nc.sbuf_tensor(name, shape, dtype) if kind == "sb" else nc.psum_tensor(name, shape, dtype)
        self.t = cm.__enter__()
        self.b = Buf(name)
        self.name = name
        self.bufs = [self.b]

    def __getitem__(self, idx):
        return self.t[idx]


class SubTile:
    def __init__(self, parent, name, c0, c1):
        self.ap = parent.t[:, c0:c1]
        self.b = Buf(name)
        self.name = name
        self.bufs = [self.b]
        parent.bufs.append(self.b)

    def __getitem__(self, idx):
        return self.ap[idx]


class Pool:
    def __init__(self, fw, kind, name, shape, dtype, n):
        self.tiles = [Tile(fw, kind, f"{name}{i}", shape, dtype) for i in range(n)]
        self.i = 0

    def next(self):
        t = self.tiles[self.i % len(self.tiles)]
        self.i += 1
        return t


D = 2048
T = 4096
L = 256
NT = T + L
KC = D // 128
MIXW = 512
NE = 16
CAP = 512
CAPC = 32
SLOTS = CAP + CAPC
NEG = -30000.0
EPS = 1e-6
NA_SCALE = 0.125
MOE_STOP = None


def build(NL=2, do_moe=True, dbg=False):
    nc = bass.Bass("TRN2", target_bir_lowering=False)
    fw = FW(nc)
    sp, pe, act, dve, pool = fw.sp, fw.pe, fw.act, fw.dve, fw.pool
    V, S, G, P = nc.vector, nc.scalar, nc.gpsimd, nc.tensor

    def din(name, shape, dt=F32):
        return nc.dram_tensor(name, list(shape), dt, kind="ExternalInput").ap()

    def dscr(name, shape, dt):
        if dbg and (dbg is True or name in dbg):
            return nc.dram_tensor(name, list(shape), dt, kind="ExternalOutput").ap()
        return nc.dram_tensor(name, list(shape), dt).ap()

    xin = din("xin", [NT, D])
    cvec = din("cvec", [2, D])
    dftc = din("dftc", [T, T], BF16)
    dfts = din("dfts", [T, T], BF16)
    dftcc = din("dftcc", [L, L], BF16)
    dftsc = din("dftsc", [L, L], BF16)
    chdft = din("chdft", [128, 256], BF16)
    invcnt = din("invcnt", [4, NT])
    W = []
    for l in range(NL):
        W.append(dict(
            w_ada=din(f"w_ada{l}", [D, 6 * D]), b_ada=din(f"b_ada{l}", [96, 128]),
            n1=din(f"n1_{l}", [16, 128]), n2=din(f"n2_{l}", [16, 128]),
            w_in=din(f"w_in{l}", [D, 6 * D]), conv=din(f"conv{l}", [3, MIXW]),
            qn=din(f"qn{l}", [128, 1]), kn=din(f"kn{l}", [128, 1]),
            bias=din(f"bias{l}", [5, 8, 640, 128]),
            pool_w=din(f"pool_w{l}", [4, 128, 128]), pool_s=din(f"pool_s{l}", [MIXW, 1]),
            w_br=din(f"w_br{l}", [D, D]), w_out=din(f"w_out{l}", [D, D]),
            w_rt=din(f"w_rt{l}", [D, NE]),
            weg=din(f"weg{l}", [NE, D, D]), weu=din(f"weu{l}", [NE, D, D]), wed=din(f"wed{l}", [NE, D, D]),
        ))
    out = nc.dram_tensor("out", [T, D], F32, kind="ExternalOutput").ap()

    xres = dscr("xres", [NT + 128, D], F32); b_xres = Buf("xres")
    hT = dscr("hT", [D, NT], BF16); b_hT = Buf("hT")
    zT = dscr("zT", [6 * D, NT], BF16); b_zT = Buf("zT")
    vtok = dscr("vtok", [NT, MIXW], BF16); b_vtok = Buf("vtok")
    qkT = dscr("qkT", [2 * MIXW, NT], BF16); b_qkT = Buf("qkT")
    brT = dscr("brT", [D, NT], BF16); b_brT = Buf("brT")
    fab = dscr("fab", [NT, 4, 256], BF16); b_fab = Buf("fab")
    h2tok = dscr("h2tok", [NT + 128, D], BF16); b_h2 = Buf("h2tok")
    vecs = dscr("vecs", [8, D], F32); b_vecs = Buf("vecs")
    b_out = Buf("out")
    wbr16 = dscr("wbr16", [16, 128, 16, 128], BF16); b_wbr = Buf("wbr16")
    wout16 = dscr("wout16", [4, 128, 16, 512], BF16); b_wout = Buf("wout16")

    WB = Pool(fw, "sb", "wb", [128, 8192], BF16, 2)
    WH = [SubTile(WB.tiles[i // 2], f"wh{i}", (i % 2) * 4096, (i % 2 + 1) * 4096) for i in range(4)]
    ABp = Pool(fw, "sb", "ab", [128, 8192], BF16, 3)
    FP = Pool(fw, "sb", "fp", [128, 2048], F32, 4)
    SM = Pool(fw, "sb", "sm", [128, 1056], F32, 5)
    small16 = Pool(fw, "sb", "s16", [128, 2048], BF16, 6)
    bcA = Tile(fw, "sb", "bcA", [128, 2048], F32)
    bcB = Tile(fw, "sb", "bcB", [128, 2048], F32)
    wr = Tile(fw, "sb", "wr", [128, 256], F32)
    affT = Tile(fw, "sb", "affT", [16, NT], F32)
    tv = Tile(fw, "sb", "tv", [16, SLOTS], F32)
    tix = Tile(fw, "sb", "tix", [16, SLOTS], U32)
    idxF = Tile(fw, "sb", "idxF", [128, 80], F32)
    idxF4 = Tile(fw, "sb", "idxF4", [128, 80], F32)
    idxT = Tile(fw, "sb", "idxT", [128, 80], I32)
    idxT4 = Tile(fw, "sb", "idxT4", [128, 320], I32)
    gateT = Tile(fw, "sb", "gateT", [128, 80], F32)
    chd = Tile(fw, "sb", "chd", [128, 256], BF16)
    ones = Tile(fw, "sb", "ones", [128, 8], BF16)
    xc = Tile(fw, "sb", "xc", [128, 512], BF16)
    ac = Tile(fw, "sb", "ac", [128, 512], BF16)
    PS = Pool(fw, "ps", "ps", [128, 512], F32, 4)
    psO = Tile(fw, "ps", "psO", [128, 512], F32)
    psD = Tile(fw, "ps", "psD", [128, 512], F32)
    qt = Tile(fw, "sb", "qt", [128, 512], BF16)
    PSB = Pool(fw, "ps", "psb", [128, 1024], BF16, 2)
    ident = Tile(fw, "sb", "ident", [128, 128], F32)
    identb = Tile(fw, "sb", "identb", [128, 128], BF16)
    blk64 = Tile(fw, "sb", "blk64", [128, 128], F32)
    modT = Tile(fw, "sb", "modT", [128, 192], F32)
    prm = Tile(fw, "sb", "prm", [128, 256], F32)
    small = Pool(fw, "sb", "tiny", [128, 64], F32, 8)

    def _b(xs):
        o = []
        for x in xs:
            if hasattr(x, "bufs"):
                o.extend(x.bufs)
            else:
                o.append(x)
        return o

    def O(eng, fn, r=(), w=(), mw=()):
        return fw.op(eng, fn, _b(r), _b(w), _b(mw))

    def OG(eng, fns, r=(), w=()):
        return fw.op_group(eng, fns, _b(r), _b(w))

    def DMA(eng, fn, key, r=(), w=(), mw=()):
        return fw.dma(eng, fn, key, _b(r), _b(w), _b(mw))

    def ld(dst_tile, dst_ap, src_ap, r=(), cast=False, **kw):
        if cast:
            return DMA(pool, lambda: G.dma_start(out=dst_ap, in_=src_ap, **kw), dst_tile.name, r, [dst_tile])
        return DMA(sp, lambda: nc.sync.dma_start(out=dst_ap, in_=src_ap, **kw), dst_tile.name, r, [dst_tile])

    def stq(dst_ap, src_tile, src_ap, dbuf, **kw):
        return DMA(act, lambda: S.dma_start(out=dst_ap, in_=src_ap, **kw), src_tile.name + "_st", [src_tile], mw=[dbuf])

    O(pool, lambda: G.memset(ident[:], 1.0), w=[ident])
    O(pool, lambda: G.affine_select(out=ident[:], in_=ident[:], pattern=[[-1, 128]], compare_op=ALU.is_equal,
                                   fill=0.0, base=0, channel_multiplier=1), r=[ident], w=[ident])
    O(dve, lambda: V.tensor_copy(out=identb[:], in_=ident[:]), r=[ident], w=[identb])
    O(pool, lambda: G.memset(blk64[:], 0.0), w=[blk64])
    O(pool, lambda: G.memset(blk64[0:64, 0:64], 1.0), r=[blk64], w=[blk64])
    O(pool, lambda: G.memset(blk64[64:128, 64:128], 1.0), r=[blk64], w=[blk64])

    def prm_ap(which, j, r):
        o = (which * 16 + j) * 2 + r
        return prm[:, o:o + 1]

    def rstd_from_ss(ss_ap, out_ap, n, tl):
        O(dve, lambda: V.tensor_scalar(out=out_ap, in0=ss_ap, scalar1=1.0 / n, scalar2=EPS, op0=ALU.mult, op1=ALU.add), r=[tl], w=[tl])
        O(act, lambda: S.sqrt(out=out_ap, in_=out_ap), r=[tl], w=[tl])
        O(dve, lambda: V.reciprocal(out=out_ap, in_=out_ap), r=[tl], w=[tl])

    def phase_precast(l):
        w = W[l]
        for dj in range(16):
            DMA(pool, lambda: G.dma_start(out=wbr16[dj], in_=w["w_br"][:, dj * 128:(dj + 1) * 128].rearrange("(k p) c -> p k c", p=128)),
                f"pc{dj % 4}", [], mw=[b_wbr])
        for db in range(4):
            DMA(pool, lambda: G.dma_start(out=wout16[db], in_=w["w_out"][:, db * 512:(db + 1) * 512].rearrange("(k p) c -> p k c", p=128)),
                f"pc{db % 4}", [], mw=[b_wout])

    def phase_mod(l):
        w = W[l]
        scT = small.next()
        t = small.next()
        with nc.allow_non_contiguous_dma(reason="tiny transposed load"):
            for r_ in range(2):
                ld(t, t[:, r_ * 16:(r_ + 1) * 16], cvec[r_].rearrange("(k p) -> p k", p=128))
        O(act, lambda: S.activation(out=scT[:, 0:32], in_=t[:, 0:32], func=AF.Silu), r=[t], w=[scT])
        psm = PS.next()
        for cb in range(48):
            wt = FP.next()
            for half in range(2):
                if half == 1:
                    wt = FP.next()
                ld(wt, wt[:, :].rearrange("p (k c) -> p k c", c=256),
                   w["w_ada"][half * 1024:(half + 1) * 1024, cb * 256:(cb + 1) * 256].rearrange("(k p) c -> p k c", p=128))
                if half == 0:
                    wt0 = wt
            for jj in range(2):
                col = cb * 2 + jj
                for k in range(16):
                    src = wt0 if k < 8 else wt
                    kk = k % 8
                    O(pe, lambda: P.matmul(psm[:, col * 2:col * 2 + 2], src[:, kk * 256 + jj * 128: kk * 256 + jj * 128 + 128],
                                           scT[:, 0:32].rearrange("p (r k) -> p k r", r=2)[:, k, :], start=(k == 0), stop=(k == 15)), r=[src, scT], w=[psm])
        bt = FP.next()
        ld(bt, bt[0:96, 0:128], w["b_ada"])
        ld(bt, bt[0:16, 128:256], w["n1"])
        ld(bt, bt[0:16, 256:384], w["n2"])
        pst = PS.next()
        O(pe, lambda: P.transpose(pst[:, 0:96], bt[0:96, 0:128], ident[0:96, 0:96]), r=[bt, ident], w=[pst])
        O(pe, lambda: P.transpose(pst[:, 96:112], bt[0:16, 128:256], ident[0:16, 0:16]), r=[bt, ident], w=[pst])
        O(pe, lambda: P.transpose(pst[:, 112:128], bt[0:16, 256:384], ident[0:16, 0:16]), r=[bt, ident], w=[pst])
        bn = small.next()
        bn = SM.next()
        O(dve, lambda: V.tensor_copy(out=bn[:, 0:128], in_=pst[:, 0:128]), r=[pst], w=[bn])
        O(dve, lambda: V.tensor_tensor(out=modT[:, 0:192].rearrange("p (c r) -> p c r", r=2),
                                       in0=psm[:, 0:192].rearrange("p (c r) -> p c r", r=2),
                                       in1=bn[:, 0:96].unsqueeze(2).to_broadcast([128, 96, 2]), op=ALU.add), r=[psm, bn], w=[modT])

        def mv(m):
            return modT[:, m * 32:(m + 1) * 32].rearrange("p (j r) -> p j r", r=2)

        def pv(which):
            return prm[:, which * 32:(which + 1) * 32].rearrange("p (j r) -> p j r", r=2)
        for which, (msc, msh, nwo) in enumerate([(1, 0, 96), (4, 3, 112)]):
            base = 0 if which == 0 else 3
            O(dve, lambda: V.tensor_scalar(out=pv(base), in0=mv(msc), scalar1=1.0, scalar2=None, op0=ALU.add), r=[modT], w=[prm])
            O(dve, lambda: V.tensor_tensor(out=pv(base), in0=pv(base), in1=bn[:, nwo:nwo + 16].unsqueeze(2).to_broadcast([128, 16, 2]),
                                           op=ALU.mult), r=[prm, bn], w=[prm])
            O(dve, lambda: V.tensor_copy(out=pv(base + 1), in_=mv(msh)), r=[modT], w=[prm])
            O(dve, lambda: V.tensor_copy(out=pv(base + 2), in_=mv(msc + 1)), r=[modT], w=[prm])
        rows = [(3, 0), (4, 0), (5, 0), (3, 1), (4, 1), (5, 1), (2, 0), (2, 1)]
        with nc.allow_non_contiguous_dma(reason="param row layout"):
            for ri, (which, r) in enumerate(rows):
                src = prm[:, which * 32:(which + 1) * 32].rearrange("p (j r) -> p j r", r=2)[:, :, r:r + 1]
                DMA(sp, lambda: nc.sync.dma_start(out=vecs[ri:ri + 1, :].rearrange("o (j p) -> p j o", p=128), in_=src),
                    f"vecs{ri}", [prm], mw=[b_vecs])

    def bc_row(ri):
        t = FP.next()
        ld(t, t[:, :], vecs[ri:ri + 1, :].to_broadcast([128, D]), r=[b_vecs])
        return t

    def norm_tile(src_dram, ti, which, want_h2=None):
        xt = FP.next()
        ld(xt, xt[:, :], src_dram[ti * 128:(ti + 1) * 128, :], r=[b_xres] if src_dram is xres else [])
        sq = FP.next()
        st = small.next()
        O(act, lambda: S.activation(out=sq[:, :], in_=xt[:, :], func=AF.Square, accum_out=st[:, 0:1]), r=[xt], w=[sq, st])
        rstd_from_ss(st[:, 0:1], st[:, 1:2], D, st)
        O(act, lambda: S.activation(out=sq[:, :], in_=xt[:, :], func=AF.Copy, scale=st[:, 1:2]), r=[xt, st], w=[sq])
        return xt, sq

    def phase_norm1(l):
        src = xin if l == 0 else xres
        for g in range(9):
            ntile = 4 if g < 8 else 2
            r = 0 if g < 8 else 1
            hs = ABp.next()
            hv = hs[:, :].rearrange("p (k t) -> p k t", t=512)
            for tt in range(ntile):
                ti = g * 4 + tt
                xt, xn = norm_tile(src, ti, 0)
                for kq in range(4):
                    ps = PS.next()
                    for q in range(4):
                        k = kq * 4 + q
                        O(pe, lambda: P.transpose(ps[:, q * 128:(q + 1) * 128], xn[:, k * 128:(k + 1) * 128], ident[:]), r=[xn, ident], w=[ps])
                    for q in range(4):
                        k = kq * 4 + q
                        e = dve if q % 2 == 0 else pool
                        if q % 2 == 0:
                            O(dve, lambda: V.tensor_scalar(out=hv[:, k, tt * 128:(tt + 1) * 128], in0=ps[:, q * 128:(q + 1) * 128],
                                                           scalar1=prm_ap(0, k, r), scalar2=prm_ap(1, k, r), op0=ALU.mult, op1=ALU.add),
                              r=[ps, prm], w=[hs])
                        else:
                            O(act, lambda: S.activation(out=hv[:, k, tt * 128:(tt + 1) * 128], in_=ps[:, q * 128:(q + 1) * 128],
                                                        func=AF.Identity, scale=prm_ap(0, k, r), bias=prm_ap(1, k, r)),
                              r=[ps, prm], w=[hs])
            n = ntile * 128
            stq(hT[:, g * 512:g * 512 + n].rearrange("(k p) t -> p k t", p=128), hs, hv[:, :, 0:n], b_hT)

    def phase_inproj(l):
        w = W[l]
        def wload(cb):
            wt_ = WB.next()
            ld(wt_, wt_[:, :].rearrange("p (k c) -> p k c", c=512),
               w["w_in"][:, cb * 512:(cb + 1) * 512].rearrange("(k p) c -> p k c", p=128), cast=True)
            return wt_
        nxt = wload(0)
        for cb in range(24):
            wt = nxt
            wv = wt[:, :].rearrange("p (k c) -> p k c", c=512)
            if cb + 1 < 24:
                nxt = wload(cb + 1)
            for g in range(9):
                n = 512 if g < 8 else 256
                ht = ABp.next()
                hv = ht[:, :].rearrange("p (k t) -> p k t", t=512)
                ld(ht, hv[:, :, 0:n], hT[:, g * 512:g * 512 + n].rearrange("(k p) t -> p k t", p=128), r=[b_hT])
                if cb == 5:
                    for tt in range(n // 128):
                        ps = PS.next()
                        OG(pe, [(lambda k=k: P.matmul(ps[:, :], hv[:, k, tt * 128:(tt + 1) * 128], wv[:, k, :], start=(k == 0), stop=(k == 15)))
                                for k in range(16)], r=[ht, wt], w=[ps])
                        vt = small16.next()
                        O(act, lambda: S.copy(out=vt[:, 0:512], in_=ps[:, :]), r=[ps], w=[vt])
                        stq(vtok[g * 512 + tt * 128: g * 512 + (tt + 1) * 128, :], vt, vt[:, 0:512], b_vtok)
                    continue
                zt = small16.next()
                zv = zt[:, :].rearrange("p (j t) -> p j t", t=512)
                for j in range(4):
                    ps = PS.next()
                    OG(pe, [(lambda k=k: P.matmul(ps[:, 0:n], wv[:, k, j * 128:(j + 1) * 128], hv[:, k, 0:n], start=(k == 0), stop=(k == 15)))
                            for k in range(16)], r=[ht, wt], w=[ps])
                    if cb >= 8:
                        O(act, lambda: S.activation(out=zv[:, j, 0:n], in_=ps[:, 0:n], func=AF.Sigmoid), r=[ps], w=[zt])
                    else:
                        O(dve, lambda: V.tensor_copy(out=zv[:, j, 0:n], in_=ps[:, 0:n]), r=[ps], w=[zt])
                stq(zT[cb * 512:(cb + 1) * 512, g * 512:g * 512 + n].rearrange("(j p) t -> p j t", p=128), zt, zv[:, :, 0:n], b_zT)

    def seg_list():
        if cur["last"]:
            return [(0, T)]
        return [(0, T), (T, L)]

    def phase_conv(l):
        w = W[l]
        cw = small.next()
        with nc.allow_non_contiguous_dma(reason="tiny"):
            for k_ in range(3):
                ld(cw, cw[:, k_ * 4:(k_ + 1) * 4], w["conv"][k_].rearrange("(j p) -> p j", p=128))
        for j in range(4):
            for (s0, n) in seg_list():
                for c0 in range(0, n, 1024):
                    m = min(1024, n - c0)
                    xa = small16.next(); gb = small16.next(); gc = small16.next()
                    lo = 1 if c0 > 0 else 0
                    hi = 1 if c0 + m < n else 0
                    for tl, row in ((xa, 0), (gc, 1024)):
                        if not lo:
                            O(pool, lambda: G.memset(tl[:, 0:1], 0.0), w=[tl])
                        if not hi:
                            O(pool, lambda: G.memset(tl[:, m + 1:m + 2], 0.0), w=[tl])
                        ld(tl, tl[:, 1 - lo:m + 1 + hi], zT[row + j * 128: row + (j + 1) * 128, s0 + c0 - lo: s0 + c0 + m + hi], r=[b_zT])
                    ld(gb, gb[:, 0:m], zT[512 + j * 128: 512 + (j + 1) * 128, s0 + c0: s0 + c0 + m], r=[b_zT])
                    u = SM.next()
                    u2 = SM.next()
                    O(dve, lambda: V.tensor_tensor(out=u[:, 0:m + 2], in0=xa[:, 0:m + 2], in1=gc[:, 0:m + 2], op=ALU.mult), r=[xa, gc], w=[u])
                    O(dve, lambda: V.tensor_scalar(out=u2[:, 0:m], in0=u[:, 1:m + 1], scalar1=cw[:, 4 + j:5 + j], scalar2=None, op0=ALU.mult), r=[u, cw], w=[u2])
                    O(dve, lambda: V.scalar_tensor_tensor(out=u2[:, 0:m], in0=u[:, 0:m], scalar=cw[:, j:j + 1], in1=u2[:, 0:m], op0=ALU.mult, op1=ALU.add), r=[u, cw, u2], w=[u2])
                    O(dve, lambda: V.scalar_tensor_tensor(out=u2[:, 0:m], in0=u[:, 2:m + 2], scalar=cw[:, 8 + j:9 + j], in1=u2[:, 0:m], op0=ALU.mult, op1=ALU.add), r=[u, cw, u2], w=[u2])
                    yo = small16.next()
                    O(dve, lambda: V.tensor_tensor(out=yo[:, 0:m], in0=u2[:, 0:m], in1=gb[:, 0:m], op=ALU.mult), r=[u2, gb], w=[yo])
                    stq(brT[j * 128:(j + 1) * 128, s0 + c0: s0 + c0 + m], yo, yo[:, 0:m], b_brT)

    def phase_pool(l):
        w = W[l]
        pw = WB.next()
        pwv = pw[:, 0:512].rearrange("p (g c) -> p g c", c=128)
        ld(pw, pwv, w["pool_w"].rearrange("g p c -> p g c"), cast=True)
        psc = small.next()
        with nc.allow_non_contiguous_dma(reason="tiny"):
            ld(psc, psc[:, 0:4], w["pool_s"].rearrange("(g p) o -> p (g o)", p=128))
        for g in range(4):
            win = (2, 4, 8, 16)[g]
            for (s0, n) in seg_list():
                for c0 in range(0, n, 512):
                    m = min(512, n - c0)
                    H = 16
                    lo = min(H, c0); hi = min(H, n - c0 - m)
                    zb = small16.next()
                    ld(zb, zb[:, H - lo:H + m + hi], zT[3584 + g * 128: 3584 + (g + 1) * 128, s0 + c0 - lo: s0 + c0 + m + hi], r=[b_zT])
                    u = SM.next(); a = SM.next(); b2 = SM.next()
                    O(pool, lambda: G.memset(u[:, 0:m + 2 * H], 0.0), w=[u])
                    O(dve, lambda: V.tensor_copy(out=u[:, H - lo:H + m + hi], in_=zb[:, H - lo:H + m + hi]), r=[zb, u], w=[u])
                    O(dve, lambda: V.tensor_tensor(out=a[:, 1:m + 2 * H], in0=u[:, 0:m + 2 * H - 1], in1=u[:, 1:m + 2 * H], op=ALU.add), r=[u], w=[a])
                    cur, oth = a, b2
                    lo_v = 1; hi_v = m + 2 * H
                    sh = 1
                    wdt = 2
                    while wdt < win:
                        nlo = lo_v + sh; nhi = hi_v - sh
                        O(dve, lambda: V.tensor_tensor(out=oth[:, nlo:nhi], in0=cur[:, nlo - sh:nhi - sh], in1=cur[:, nlo + sh:nhi + sh], op=ALU.add), r=[cur], w=[oth])
                        cur, oth = oth, cur
                        lo_v, hi_v = nlo, nhi
                        sh *= 2; wdt *= 2
                    ic = SM.next()
                    ld(ic, ic[:, 0:m], invcnt[g:g + 1, s0 + c0:s0 + c0 + m].to_broadcast([128, m]))
                    O(dve, lambda: V.tensor_tensor(out=oth[:, 0:m], in0=cur[:, H:H + m], in1=ic[:, 0:m], op=ALU.mult), r=[cur, ic], w=[oth])
                    pb = small16.next()
                    O(dve, lambda: V.tensor_tensor(out=pb[:, 0:m], in0=oth[:, 0:m], in1=u[:, H:H + m], op=ALU.subtract), r=[oth, u], w=[pb])
                    ps = PS.next()
                    O(pe, lambda: P.matmul(ps[:, 0:m], pwv[:, g, :], pb[:, 0:m], start=True, stop=True), r=[pw, pb], w=[ps])
                    yo = small16.next()
                    O(act, lambda: S.activation(out=yo[:, 0:m], in_=ps[:, 0:m], func=AF.Copy, scale=psc[:, g:g + 1]), r=[ps, psc], w=[yo])
                    stq(brT[1536 + g * 128:1536 + (g + 1) * 128, s0 + c0:s0 + c0 + m], yo, yo[:, 0:m], b_brT)

    def phase_fourier(l):
        ld(chd, chd[:, 0:256], chdft)
        for ti in range((T if cur["last"] else NT) // 128):
            zf = small16.next()
            ld(zf, zf[:, 0:512].rearrange("p (g t) -> p g t", t=128),
               zT[3072:3584, ti * 128:(ti + 1) * 128].rearrange("(g p) t -> p g t", p=128), r=[b_zT])
            ab = small16.next()
            for half in range(2):
                ps = PS.next()
                for gg in range(2):
                    g = half * 2 + gg
                    O(pe, lambda: P.matmul(ps[:, gg * 256:(gg + 1) * 256], zf[:, g * 128:(g + 1) * 128], chd[:, 0:256], start=True, stop=True),
                      r=[zf, chd], w=[ps])
                O(act if half else dve, (lambda: S.copy(out=ab[:, half * 512:(half + 1) * 512], in_=ps[:, :])) if half else
                  (lambda: V.tensor_copy(out=ab[:, half * 512:(half + 1) * 512], in_=ps[:, :])), r=[ps], w=[ab])
            stq(fab[ti * 128:(ti + 1) * 128, :, :].rearrange("t g c -> t (g c)"), ab, ab[:, 0:1024], b_fab)
        for (s0, n, tc, ts) in ((0, T, dftc, dfts), (T, L, dftcc, dftsc)):
            if cur["last"] and s0 == T:
                continue
            na = n // 128
            for g in range(4):
                abt = ABp.next()
                abv = abt[:, 0:na * 256].rearrange("p (a c) -> p a c", c=256)
                ld(abt, abv, fab[s0:s0 + n, g, :].rearrange("(a p) c -> p a c", p=128), r=[b_fab])
                for nb in range(n // 256):
                    ct = WB.next(); st_ = WB.next()
                    cv = ct[:, 0:na * 256].rearrange("p (a c) -> p a c", c=256)
                    sv = st_[:, 0:na * 256].rearrange("p (a c) -> p a c", c=256)
                    ld(ct, cv, tc[:, nb * 256:(nb + 1) * 256].rearrange("(a p) c -> p a c", p=128))
                    ld(st_, sv, ts[:, nb * 256:(nb + 1) * 256].rearrange("(a p) c -> p a c", p=128))
                    ps = PS.next()
                    fns = []
                    for a in range(na):
                        fns.append(lambda a=a: P.matmul(ps[:, 0:256], abv[:, a, 0:128], cv[:, a, :], start=(a == 0), stop=False))
                        fns.append(lambda a=a: P.matmul(ps[:, 0:256], abv[:, a, 128:256], sv[:, a, :], start=False, stop=(a == na - 1)))
                    OG(pe, fns, r=[abt, ct, st_], w=[ps])
                    yo = small16.next()
                    O(act, lambda: S.activation(out=yo[:, 0:256], in_=ps[:, 0:256], func=AF.Copy, scale=float((n * 128) ** -0.5)), r=[ps], w=[yo])
                    stq(brT[1024 + g * 128:1024 + (g + 1) * 128, s0 + nb * 256:s0 + (nb + 1) * 256], yo, yo[:, 0:256], b_brT)

    def phase_qknorm(l):
        w = W[l]
        qw = small.next()
        ld(qw, qw[:, 0:1], w["qn"])
        ld(qw, qw[:, 1:2], w["kn"])
        O(dve, lambda: V.tensor_scalar(out=qw[:, 0:1], in0=qw[:, 0:1], scalar1=NA_SCALE, scalar2=None, op0=ALU.mult), r=[qw], w=[qw])
        for g in range(9):
            n = 512 if g < 8 else 256
            for c in range(8):
                row = 1536 + c * 128 if c < 4 else 2048 + (c - 4) * 128
                z = small16.next()
                ld(z, z[:, 0:n], zT[row:row + 128, g * 512:g * 512 + n], r=[b_zT])
                sq = SM.next()
                O(act, lambda: S.activation(out=sq[:, 0:n], in_=z[:, 0:n], func=AF.Square), r=[z], w=[sq])
                ps = PS.next()
                O(pe, lambda: P.matmul(ps[:, 0:n], blk64[:, :], sq[:, 0:n], start=True, stop=True), r=[blk64, sq], w=[ps])
                rs = SM.next()
                O(dve, lambda: V.tensor_scalar(out=rs[:, 0:n], in0=ps[:, 0:n], scalar1=1.0 / 64, scalar2=EPS, op0=ALU.mult, op1=ALU.add), r=[ps], w=[rs])
                O(act, lambda: S.sqrt(out=rs[:, 0:n], in_=rs[:, 0:n]), r=[rs], w=[rs])
                O(dve, lambda: V.reciprocal(out=rs[:, 0:n], in_=rs[:, 0:n]), r=[rs], w=[rs])
                O(dve, lambda: V.tensor_tensor(out=rs[:, 0:n], in0=rs[:, 0:n], in1=z[:, 0:n], op=ALU.mult), r=[rs, z], w=[rs])
                zo = small16.next()
                O(act, lambda: S.activation(out=zo[:, 0:n], in_=rs[:, 0:n], func=AF.Copy, scale=qw[:, (0 if c < 4 else 1):(1 if c < 4 else 2)]), r=[rs, qw], w=[zo])
                stq(qkT[c * 128:(c + 1) * 128, g * 512:g * 512 + n], zo, zo[:, 0:n], b_qkT)

    def phase_attn(l):
        w = W[l]
        O(pool, lambda: G.memset(ones[:, 0:8], 1.0), w=[ones])
        onesb = ones
        for j in range((T if cur["last"] else NT) // 128):
            lat = j < 32
            if lat:
                ks = min(max(2 * j - 4, 0), 54)
                kt0 = ks // 2
                cls = {0: 0, 1: 1, 30: 3, 31: 4}.get(j, 2)
                nloc = 5
            else:
                nloc = 0
            nch = nloc + 2
            qv = qt[:, 0:512].rearrange("p (c t) -> p c t", t=128)
            ld(qt, qv, qkT[0:512, j * 128:(j + 1) * 128].rearrange("(c p) t -> p c t", p=128), r=[b_qkT])
            kt = ABp.next()
            kv = kt[:, 0:4 * 896].rearrange("p (c t) -> p c t", t=896)
            if lat:
                ld(kt, kv[:, :, 0:640], qkT[512:1024, kt0 * 128:kt0 * 128 + 640].rearrange("(c p) t -> p c t", p=128), r=[b_qkT])
            ld(kt, kv[:, :, nloc * 128:nloc * 128 + 256], qkT[512:1024, T:T + 256].rearrange("(c p) t -> p c t", p=128), r=[b_qkT])
            vt = ABp.next()
            vv = vt[:, 0:7 * 512].rearrange("p (a c) -> p a c", c=512)
            if lat:
                ld(vt, vv[:, 0:5, :], vtok[kt0 * 128:kt0 * 128 + 640, :].rearrange("(a p) c -> p a c", p=128), r=[b_vtok])
            ld(vt, vv[:, nloc:nloc + 2, :], vtok[T:T + 256, :].rearrange("(a p) c -> p a c", p=128), r=[b_vtok])
            for h in range(8):
                pc, pb = h // 2, (h % 2) * 64
                psA = PS.next(); psB = PS.next()

                def sdst(c):
                    return (psA if c < 4 else psB)[:, (c % 4) * 128:(c % 4 + 1) * 128]
                OG(pe, [(lambda c=c: P.matmul(sdst(c), kv[pb:pb + 64, pc, c * 128:(c + 1) * 128], qv[pb:pb + 64, pc, :], start=True, stop=True))
                        for c in range(nch)], r=[kt, qt], w=[psA, psB])
                pt = small16.next()
                if lat:
                    bt = SM.next()
                    bv = bt[:, 0:640].rearrange("p (c q) -> p c q", q=128)
                    ld(bt, bv, w["bias"][cls, h].rearrange("(c p) q -> p c q", p=128))
                    O(dve, lambda: V.tensor_tensor(out=bt[:, 0:512], in0=bt[:, 0:512], in1=psA[:, 0:512], op=ALU.add), r=[bt, psA], w=[bt])
                    O(dve, lambda: V.tensor_tensor(out=bt[:, 512:640], in0=bt[:, 512:640], in1=psB[:, 0:128], op=ALU.add), r=[bt, psB], w=[bt])
                    O(act, lambda: S.activation(out=pt[:, 0:640], in_=bt[:, 0:640], func=AF.Exp), r=[bt], w=[pt])
                    O(act, lambda: S.activation(out=pt[:, 640:896], in_=psB[:, 128:384], func=AF.Exp), r=[psB], w=[pt])
                else:
                    O(act, lambda: S.activation(out=pt[:, 0:256], in_=psA[:, 0:256], func=AF.Exp), r=[psA], w=[pt])
                OG(pe, [(lambda c=c: P.matmul(psO[:, h * 64:(h + 1) * 64], pt[:, c * 128:(c + 1) * 128], vv[:, c, h * 64:(h + 1) * 64],
                                           start=(c == 0), stop=(c == nch - 1))) for c in range(nch)], r=[pt, vt], w=[psO])
                OG(pe, [(lambda c=c: P.matmul(psD[:, h:h + 1], pt[:, c * 128:(c + 1) * 128], onesb[:, 0:1],
                                           start=(c == 0), stop=(c == nch - 1))) for c in range(nch)], r=[pt, onesb], w=[psD])
            rc = small.next()
            O(dve, lambda: V.reciprocal(out=rc[:, 0:8], in_=psD[:, 0:8]), r=[psD], w=[rc])
            yt = small16.next()
            O(dve, lambda: V.tensor_tensor(out=yt[:, 0:512].rearrange("p (h d) -> p h d", d=64), in0=psO[:, 0:512].rearrange("p (h d) -> p h d", d=64),
                                           in1=rc[:, 0:8].unsqueeze(2).to_broadcast([128, 8, 64]), op=ALU.mult), r=[psO, rc], w=[yt])
            pst = PSB.next()
            for c in range(4):
                O(pe, lambda: P.transpose(pst[:, c * 128:(c + 1) * 128], yt[:, c * 128:(c + 1) * 128], identb[:]), r=[yt, identb], w=[pst])
            yo = small16.next()
            O(act, lambda: S.copy(out=yo[:, 0:512], in_=pst[:, 0:512]), r=[pst], w=[yo])
            stq(brT[512:1024, j * 128:(j + 1) * 128].rearrange("(c p) t -> p c t", p=128), yo, yo[:, 0:512].rearrange("p (c t) -> p c t", t=128), b_brT)

    def phase_merge(l):
        w = W[l]
        src = xin if l == 0 else xres
        ld(bcA, bcA[:, :], vecs[6:7, :].to_broadcast([128, D]), r=[b_vecs])
        ld(bcB, bcB[:, :], vecs[7:8, :].to_broadcast([128, D]), r=[b_vecs])
        for g in range(8 if cur["last"] else 9):
            n = 512 if g < 8 else 256
            gbc = bcA if g < 8 else bcB
            bt = ABp.next()
            bv = bt[:, :].rearrange("p (k t) -> p k t", t=512)
            ld(bt, bv[:, :, 0:n], brT[:, g * 512:g * 512 + n].rearrange("(k p) t -> p k t", p=128), r=[b_brT])
            mt = ABp.next()
            mv_ = mt[:, :].rearrange("p (k t) -> p k t", t=512)
            for dj in range(16):
                wb = small16.next()
                wbv = wb[:, :].rearrange("p (k c) -> p k c", c=128)
                ld(wb, wbv, wbr16[dj], r=[b_wbr])
                gt = small16.next()
                gv = gt[:, :].rearrange("p (i t) -> p i t", t=512)
                ld(gt, gv[:, :, 0:n], zT[4096:4096 + 4 * D, g * 512:g * 512 + n].rearrange("(i d) t -> d i t", d=D)[dj * 128:(dj + 1) * 128], r=[b_zT])
                acc = SM.next()
                for i in range(4):
                    ps = PS.next()
                    OG(pe, [(lambda kk=kk: P.matmul(ps[:, 0:n], wbv[:, i * 4 + kk, :], bv[:, i * 4 + kk, 0:n], start=(kk == 0), stop=(kk == 3))) for kk in range(4)], r=[wb, bt], w=[ps])
                    if i == 0:
                        O(dve, lambda: V.tensor_tensor(out=acc[:, 0:n], in0=ps[:, 0:n], in1=gv[:, i, 0:n], op=ALU.mult), r=[ps, gt], w=[acc])
                    else:
                        tmp = SM.next()
                        O(dve, lambda: V.tensor_tensor(out=tmp[:, 0:n], in0=ps[:, 0:n], in1=gv[:, i, 0:n], op=ALU.mult), r=[ps, gt], w=[tmp])
                        O(pool, lambda: G.tensor_tensor(out=acc[:, 0:n], in0=acc[:, 0:n], in1=tmp[:, 0:n], op=ALU.add), r=[acc, tmp], w=[acc])
                O(act, lambda: S.copy(out=mv_[:, dj, 0:n], in_=acc[:, 0:n]), r=[acc], w=[mt])
            for db in range(4):
                wo = WB.next()
                wov = wo[:, :].rearrange("p (k c) -> p k c", c=512)
                ld(wo, wov, wout16[db], r=[b_wout])
                for tt in range(n // 128):
                    ti = g * 4 + tt
                    ps = PS.next()
                    OG(pe, [(lambda k=k: P.matmul(ps[:, :], mv_[:, k, tt * 128:(tt + 1) * 128], wov[:, k, :], start=(k == 0), stop=(k == 15))) for k in range(16)], r=[mt, wo], w=[ps])
                    xs = SM.next()
                    ld(xs, xs[:, 0:512], src[ti * 128:(ti + 1) * 128, db * 512:(db + 1) * 512], r=[b_xres] if l > 0 else [])
                    tmp = SM.next()
                    O(dve, lambda: V.tensor_tensor(out=tmp[:, 0:512], in0=ps[:, :], in1=gbc[:, db * 512:(db + 1) * 512], op=ALU.mult), r=[ps, gbc], w=[tmp])
                    O(pool, lambda: G.tensor_tensor(out=tmp[:, 0:512], in0=tmp[:, 0:512], in1=xs[:, 0:512], op=ALU.add), r=[tmp, xs], w=[tmp])
                    stq(xres[ti * 128:(ti + 1) * 128, db * 512:(db + 1) * 512], tmp, tmp[:, 0:512], b_xres)

    def phase_norm2_router(l):
        w = W[l]
        ld(wr, wr[:, 0:256].rearrange("p (k e) -> p k e", e=16), w["w_rt"].rearrange("(k p) e -> p k e", p=128))
        for ti in range((T if cur["last"] else NT) // 128):
            r = 0 if ti < 32 else 1
            if ti == 0 or ti == 32:
                ld(bcA, bcA[:, :], vecs[3 * r:3 * r + 1, :].to_broadcast([128, D]), r=[b_vecs])
                ld(bcB, bcB[:, :], vecs[3 * r + 1:3 * r + 2, :].to_broadcast([128, D]), r=[b_vecs])
            xt, xn = norm_tile(xres, ti, 1)
            h2 = xt
            O(dve, lambda: V.tensor_tensor(out=h2[:, :], in0=xn[:, :], in1=bcA[:, :], op=ALU.mult), r=[xn, bcA], w=[h2])
            O(pool, lambda: G.tensor_tensor(out=h2[:, :], in0=h2[:, :], in1=bcB[:, :], op=ALU.add), r=[h2, bcB], w=[h2])
            hb = small16.next()
            O(act, lambda: S.copy(out=hb[:, :], in_=h2[:, :]), r=[h2], w=[hb])
            stq(h2tok[ti * 128:(ti + 1) * 128, :], hb, hb[:, :], b_h2)
            hT_ = xn
            for kq in range(4):
                ps = PS.next()
                for q in range(4):
                    k = kq * 4 + q
                    O(pe, lambda: P.transpose(ps[:, q * 128:(q + 1) * 128], h2[:, k * 128:(k + 1) * 128], ident[:]), r=[h2, ident], w=[ps])
                O(dve if kq % 2 else act, (lambda: V.tensor_copy(out=hT_[:, kq * 512:(kq + 1) * 512], in_=ps[:, :])) if kq % 2 else
                  (lambda: S.copy(out=hT_[:, kq * 512:(kq + 1) * 512], in_=ps[:, :])), r=[ps], w=[hT_])
            psl = PS.next()
            OG(pe, [(lambda k=k: P.matmul(psl[:, 0:16], hT_[:, k * 128:(k + 1) * 128], wr[:, k * 16:(k + 1) * 16], start=(k == 0), stop=(k == 15))) for k in range(16)], r=[hT_, wr], w=[psl])
            sm_ = small.next()
            O(dve, lambda: V.reduce_max(out=sm_[:, 16:17], in_=psl[:, 0:16], axis=AX.X), r=[psl], w=[sm_])
            O(dve, lambda: V.tensor_scalar(out=sm_[:, 16:17], in0=sm_[:, 16:17], scalar1=-1.0, scalar2=None, op0=ALU.mult), r=[sm_], w=[sm_])
            O(act, lambda: S.activation(out=sm_[:, 0:16], in_=psl[:, 0:16], func=AF.Exp, bias=sm_[:, 16:17], accum_out=sm_[:, 17:18]), r=[psl, sm_], w=[sm_])
            O(dve, lambda: V.reciprocal(out=sm_[:, 18:19], in_=sm_[:, 17:18]), r=[sm_], w=[sm_])
            O(dve, lambda: V.tensor_scalar(out=sm_[:, 0:16], in0=sm_[:, 0:16], scalar1=sm_[:, 18:19], scalar2=None, op0=ALU.mult), r=[sm_], w=[sm_])
            pst = PS.next()
            O(pe, lambda: P.transpose(pst[0:16, 0:128], sm_[:, 0:16], ident[:]), r=[sm_, ident], w=[pst])
            O(dve, lambda: V.tensor_copy(out=affT[0:16, ti * 128:(ti + 1) * 128], in_=pst[0:16, 0:128]), r=[pst], w=[affT])

    def phase_topk():
        for (c0, n, cap, o0) in ((0, T, CAP, 0), (T, L, CAPC, CAP)):
            if cur["last"] and c0 == T:
                continue
            for it in range(cap // 8):
                vs = tv[0:16, o0 + it * 8:o0 + it * 8 + 8]
                O(dve, lambda: V.max(out=vs, in_=affT[0:16, c0:c0 + n]), r=[affT], w=[tv])
                O(dve, lambda: V.max_index(out=tix[0:16, o0 + it * 8:o0 + it * 8 + 8], in_max=vs, in_values=affT[0:16, c0:c0 + n]), r=[affT, tv], w=[tix])
                O(dve, lambda: V.match_replace(out=affT[0:16, c0:c0 + n], in_to_replace=vs, in_values=affT[0:16, c0:c0 + n], imm_value=-1.0), r=[tv, affT], w=[affT])
        tf = SM.next()
        O(dve, lambda: V.tensor_copy(out=tf[0:16, 0:SLOTS], in_=tix[0:16, 0:SLOTS]), r=[tix], w=[tf])
        O(dve, lambda: V.tensor_scalar(out=tf[0:16, CAP:SLOTS], in0=tf[0:16, CAP:SLOTS], scalar1=float(T), scalar2=None, op0=ALU.add), r=[tf], w=[tf])
        O(pool, lambda: G.iota(idxF[:, 64:80], pattern=[[0, 16]], base=NT, channel_multiplier=1, allow_small_or_imprecise_dtypes=True), w=[idxF])
        O(pool, lambda: G.memset(gateT[:, :], 0.0), w=[gateT])
        for s in range(5):
            n = 128 if s < 4 else CAPC
            ps = PS.next()
            O(pe, lambda: P.transpose(ps[0:n, 0:16], tf[0:16, s * 128:s * 128 + n], ident[0:16, 0:16]), r=[tf, ident], w=[ps])
            O(pe, lambda: P.transpose(ps[0:n, 16:32], tv[0:16, s * 128:s * 128 + n], ident[0:16, 0:16]), r=[tv, ident], w=[ps])
            O(dve, lambda: V.tensor_copy(out=idxF[0:n, s * 16:(s + 1) * 16], in_=ps[0:n, 0:16]), r=[ps, idxF], w=[idxF])
            O(dve, lambda: V.tensor_copy(out=gateT[0:n, s * 16:(s + 1) * 16], in_=ps[0:n, 16:32]), r=[ps, gateT], w=[gateT])
        O(dve, lambda: V.tensor_copy(out=idxT[:, 0:80], in_=idxF[:, 0:80]), r=[idxF], w=[idxT])
        for db in range(4):
            tq = small.next()
            O(dve, lambda: V.tensor_scalar(out=tq[:, 0:80].bitcast(F32) if False else idxF4[:, 0:80], in0=idxF[:, 0:80], scalar1=4.0, scalar2=float(db), op0=ALU.mult, op1=ALU.add), r=[idxF], w=[idxF4])
            O(dve, lambda: V.tensor_copy(out=idxT4[:, db * 80:(db + 1) * 80], in_=idxF4[:, 0:80]), r=[idxF4], w=[idxT4])

    nsc = [0]
    cur = {"last": False}

    def phase_moe(l):
        w = W[l]
        ld(bcA, bcA[:, :], vecs[2:3, :].to_broadcast([128, D]), r=[b_vecs])
        ld(bcB, bcB[:, :], vecs[5:6, :].to_broadcast([128, D]), r=[b_vecs])
        nst = 4 if cur["last"] else 5

        def issue_gather(e):
            xgs = []
            for s in range(nst):
                xg = small16.next()
                DMA(pool, lambda: G.indirect_dma_start(out=xg[:, :], out_offset=None, in_=h2tok,
                                                       in_offset=bass.IndirectOffsetOnAxis(ap=idxT[:, s * 16 + e:s * 16 + e + 1], axis=0)),
                    xg.name, [idxT, b_h2], [xg])
                xgs.append(xg)
            return xgs
        pending = issue_gather(0)
        for e in range(NE):
            xs = ABp.next()
            xsv = xs[:, :].rearrange("p (k t) -> p k t", t=512)
            xcv = xc[:, 0:512].rearrange("p (k t) -> p k t", t=32)
            for s in range(nst):
                xg = pending[s]
                for kh in range(2):
                    pst = PSB.next()
                    for q in range(8):
                        k = kh * 8 + q
                        O(pe, lambda: P.transpose(pst[:, q * 128:(q + 1) * 128], xg[:, k * 128:(k + 1) * 128], identb[:]), r=[xg, identb], w=[pst])
                    pv_ = pst[:, :].rearrange("p (q t) -> p q t", t=128)
                    if s < 4:
                        O(act if kh else dve, (lambda: S.copy(out=xsv[:, kh * 8:(kh + 1) * 8, s * 128:(s + 1) * 128], in_=pv_)) if kh else
                          (lambda: V.tensor_copy(out=xsv[:, kh * 8:(kh + 1) * 8, s * 128:(s + 1) * 128], in_=pv_)), r=[pst], w=[xs])
                    else:
                        O(dve, lambda: V.tensor_copy(out=xcv[:, kh * 8:(kh + 1) * 8, :], in_=pv_[:, :, 0:32]), r=[pst], w=[xc])
            if e + 1 < NE:
                pending = issue_gather(e + 1)
            if MOE_STOP == "gather":
                return
            at = ABp.next()
            atv = at[:, :].rearrange("p (k t) -> p k t", t=512)
            acv = ac[:, 0:512].rearrange("p (k t) -> p k t", t=32)

            def gu_load(fb):
                a_, b_ = WH[(fb % 2) * 2], WH[(fb % 2) * 2 + 1]
                ld(a_, a_[:, :].rearrange("p (k c) -> p k c", c=256), w["weg"][e, :, fb * 256:(fb + 1) * 256].rearrange("(k p) c -> p k c", p=128), cast=True)
                ld(b_, b_[:, :].rearrange("p (k c) -> p k c", c=256), w["weu"][e, :, fb * 256:(fb + 1) * 256].rearrange("(k p) c -> p k c", p=128), cast=True)
                return a_, b_

            def d_load(db):
                t_ = WB.tiles[db % 2]
                ld(t_, t_[:, :].rearrange("p (k c) -> p k c", c=512), w["wed"][e, :, db * 512:(db + 1) * 512].rearrange("(k p) c -> p k c", p=128), cast=True)
                return t_
            nxt = gu_load(0)
            nxt_d = None
            for fb in range(8):
                wg, wu = nxt
                wgv = wg[:, :].rearrange("p (k c) -> p k c", c=256)
                wuv = wu[:, :].rearrange("p (k c) -> p k c", c=256)
                if fb + 1 < 8:
                    nxt = gu_load(fb + 1)
                for fj in range(2):
                    f = fb * 2 + fj
                    pa = PS.next(); pu = PS.next(); pc_ = PS.next()
                    OG(pe, [(lambda k=k: P.matmul(pa[:, :], wgv[:, k, fj * 128:(fj + 1) * 128], xsv[:, k, :], start=(k == 0), stop=(k == 15))) for k in range(16)], r=[wg, xs], w=[pa])
                    OG(pe, [(lambda k=k: P.matmul(pu[:, :], wuv[:, k, fj * 128:(fj + 1) * 128], xsv[:, k, :], start=(k == 0), stop=(k == 15))) for k in range(16)], r=[wu, xs], w=[pu])
                    if nst == 5:
                        OG(pe, [(lambda k=k: P.matmul(pc_[:, 0:32], wgv[:, k, fj * 128:(fj + 1) * 128], xcv[:, k, :], start=(k == 0), stop=(k == 15))) for k in range(16)], r=[wg, xc], w=[pc_])
                        OG(pe, [(lambda k=k: P.matmul(pc_[:, 32:64], wuv[:, k, fj * 128:(fj + 1) * 128], xcv[:, k, :], start=(k == 0), stop=(k == 15))) for k in range(16)], r=[wu, xc], w=[pc_])
                    sa = SM.next()
                    O(act, lambda: S.activation(out=sa[:, 0:512], in_=pa[:, :], func=AF.Silu), r=[pa], w=[sa])
                    O(dve, lambda: V.tensor_tensor(out=atv[:, f, :], in0=sa[:, 0:512], in1=pu[:, :], op=ALU.mult), r=[sa, pu], w=[at])
                    if nst == 5:
                        O(act, lambda: S.activation(out=sa[:, 512:544], in_=pc_[:, 0:32], func=AF.Silu), r=[pc_, sa], w=[sa])
                        O(dve, lambda: V.tensor_tensor(out=acv[:, f, :], in0=sa[:, 512:544], in1=pc_[:, 32:64], op=ALU.mult), r=[sa, pc_], w=[ac])
                if fb == 6:
                    nxt_d = d_load(0)
            if MOE_STOP == "gateup":
                return
            for db in range(4):
                wd = nxt_d
                wdv = wd[:, :].rearrange("p (k c) -> p k c", c=512)
                if db + 1 < 4:
                    nxt_d = d_load(db + 1)
                for s in range(nst):
                    n = 128 if s < 4 else CAPC
                    ps = PS.next()
                    OG(pe, [(lambda f=f: P.matmul(ps[0:n, :], (atv[:, f, s * 128:(s + 1) * 128] if s < 4 else acv[:, f, :]), wdv[:, f, :],
                                                  start=(f == 0), stop=(f == 15))) for f in range(16)], r=[at if s < 4 else ac, wd], w=[ps])
                    yo = SM.next()
                    if s == 4:
                        O(pool, lambda: G.memset(yo[:, 0:512], 0.0), w=[yo])
                    gbc = bcA if s < 4 else bcB
                    O(dve, lambda: V.scalar_tensor_tensor(out=yo[0:n, 0:512], in0=ps[0:n, :], scalar=gateT[0:n, s * 16 + e:s * 16 + e + 1],
                                                          in1=gbc[0:n, db * 512:(db + 1) * 512], op0=ALU.mult, op1=ALU.mult), r=[ps, gateT, gbc, yo], w=[yo])
                    DMA(pool, lambda: G.indirect_dma_start(out=xres.rearrange("n (q c) -> (n q) c", c=512),
                                                           out_offset=bass.IndirectOffsetOnAxis(ap=idxT4[:, db * 80 + s * 16 + e:db * 80 + s * 16 + e + 1], axis=0),
                                                           in_=yo[:, 0:512], in_offset=None, compute_op=ALU.add),
                        yo.name + "_sc", [idxT4, yo], [b_xres])

    for l in range(NL):
        cur["last"] = (l == NL - 1) and NL > 1
        phase_precast(l)
        phase_mod(l)
        if dbg and l == 0:
            dbgv = nc.dram_tensor("dbgv", [128, 448], F32, kind="ExternalOutput").ap()
            DMA(sp, lambda: nc.sync.dma_start(out=dbgv[:, 0:192], in_=modT[:, :]), "dbg0", [modT], [b_out])
            DMA(sp, lambda: nc.sync.dma_start(out=dbgv[:, 192:448], in_=prm[:, :]), "dbg1", [prm], [b_out])
        phase_norm1(l)
        phase_inproj(l)
        phase_conv(l)
        phase_pool(l)
        phase_fourier(l)
        phase_qknorm(l)
        phase_attn(l)
        phase_merge(l)
        if do_moe:
            phase_norm2_router(l)
            phase_topk()
            if do_moe == "topk":
                dbi = nc.dram_tensor("dbi", [128, 80], I32, kind="ExternalOutput").ap()
                dbg_ = nc.dram_tensor("dbgate", [128, 80], F32, kind="ExternalOutput").ap()
                DMA(sp, lambda: nc.sync.dma_start(out=dbi, in_=idxT[:, :]), "dbg2", [idxT], [b_out])
                DMA(sp, lambda: nc.sync.dma_start(out=dbg_, in_=gateT[:, :]), "dbg3", [gateT], [b_out])
                continue
            phase_moe(l)
    DMA(sp, lambda: nc.sync.dma_start(out=out, in_=xres[0:T, :]), "final", [b_xres], [b_out])
    fw.wait_all(sp, [b_out])
    if dbg:
        top = sorted(((v[1], k) for k, v in fw.dma_sems.items()), reverse=True)[:6]
        print("nsem", fw.nsem, "top dma sem counts", top, "eng counts", [(e.name, e.cnt) for e in (pe, act, dve, pool, sp)], flush=True)
    return nc


def _const_tables():
    bf = ml_dtypes.bfloat16
    n = np.arange(T, dtype=np.int64)
    ph = (np.outer(n, n) % T).astype(np.float64) * (2 * np.pi / T)
    dftc = np.cos(ph).astype(np.float32).astype(bf)
    dfts = (-np.sin(ph)).astype(np.float32).astype(bf)
    m = np.arange(L, dtype=np.int64)
    phc = (np.outer(m, m) % L).astype(np.float64) * (2 * np.pi / L)
    dftcc = np.cos(phc).astype(np.float32).astype(bf)
    dftsc = (-np.sin(phc)).astype(np.float32).astype(bf)
    c = np.arange(128, dtype=np.int64)
    pc = (np.outer(c, c) % 128).astype(np.float64) * (2 * np.pi / 128)
    chdft = np.concatenate([np.cos(pc), np.sin(pc)], axis=1).astype(np.float32).astype(bf)
    inv = np.zeros((4, NT), np.float32)
    for g, wdt in enumerate((2, 4, 8, 16)):
        for s0, nn in ((0, T), (T, L)):
            t = np.arange(nn)
            lo = np.clip(t - wdt // 2, 0, nn - 1)
            hi = np.clip(t + wdt - wdt // 2 - 1, 0, nn - 1)
            inv[g, s0:s0 + nn] = 1.0 / (hi - lo + 1)
    return dict(dftc=dftc, dfts=dfts, dftcc=dftcc, dftsc=dftsc, chdft=chdft, invcnt=inv)


def _bias_table(rpb):
    out = np.full((5, 8, 640, 128), NEG, np.float32)
    ql = np.arange(128)
    kl = np.arange(640)
    for cls, j in enumerate((0, 1, 2, 30, 31)):
        ks = min(max(2 * j - 4, 0), 54)
        r = 2 * j + ql // 64
        c = ql % 64
        kr = ks + kl // 64
        kc = kl % 64
        rs = np.clip(r - 4, 0, 56)
        cs = np.clip(c - 8, 0, 48)
        ok = ((kr[:, None] >= rs[None, :]) & (kr[:, None] < rs[None, :] + 8) &
              (kc[:, None] >= cs[None, :]) & (kc[:, None] < cs[None, :] + 16))
        dr = np.clip(kr[:, None] - r[None, :] + 7, 0, 14)
        dc = np.clip(kc[:, None] - c[None, :] + 15, 0, 30)
        for h in range(8):
            gath = rpb[h][dr, dc]
            out[cls, h] = np.where(ok, gath, np.float32(NEG))
    return out


def _layer_inputs(inp, l):
    f = lambda a: np.ascontiguousarray(a, dtype=np.float32)
    return {
        f"w_ada{l}": f(inp["w_ada"][l]), f"b_ada{l}": f(inp["b_ada"][l].reshape(96, 128)),
        f"n1_{l}": f(inp["norm1_w"][l].reshape(16, 128)), f"n2_{l}": f(inp["norm2_w"][l].reshape(16, 128)),
        f"w_in{l}": f(inp["w_in"][l]), f"conv{l}": f(inp["conv_w"][l]),
        f"qn{l}": f(np.tile(inp["q_norm_w"][l], 2).reshape(128, 1)), f"kn{l}": f(np.tile(inp["k_norm_w"][l], 2).reshape(128, 1)),
        f"bias{l}": _bias_table(np.asarray(inp["na_rpb"][l], np.float32)),
        f"pool_w{l}": f(inp["pool_w"][l]), f"pool_s{l}": f(inp["pool_scale"][l].reshape(MIXW, 1)),
        f"w_br{l}": f(inp["w_branch"][l].reshape(D, D)), f"w_out{l}": f(inp["w_out"][l]),
        f"w_rt{l}": f(inp["w_router"][l]),
        f"weg{l}": f(inp["w_exp_gate"][l]), f"weu{l}": f(inp["w_exp_up"][l]), f"wed{l}": f(inp["w_exp_down"][l]),
    }


def make_in_maps(inp, cores, NL=2):
    shared = _const_tables()
    for l in range(NL):
        shared.update(_layer_inputs(inp, l))
    maps = []
    for b in cores:
        m = dict(shared)
        m["xin"] = np.ascontiguousarray(np.concatenate([inp["x"][b], inp["ctx"][b]], axis=0), dtype=np.float32)
        m["cvec"] = np.ascontiguousarray(np.stack([inp["c"][b], inp["c_ctx"]]), dtype=np.float32)
        maps.append(m)
    return maps


def kernel(**inputs):
    inp = {k: np.asarray(v) for k, v in inputs.items()}
    nc = build(NL=2)
    maps = make_in_maps(inp, list(range(8)))
    res = run_bass_kernel_spmd(nc, maps, core_ids=list(range(8)))
    return np.stack([np.asarray(r["out"], dtype=np.float32) for r in res.results], axis=0)
```

```python
import numpy as np
import ml_dtypes
import concourse.bass as bass
import concourse.mybir as mybir
from concourse.bass_utils import run_bass_kernel_spmd

F32 = mybir.dt.float32
BF16 = mybir.dt.bfloat16
U32 = mybir.dt.uint32
I32 = mybir.dt.int32
ALU = mybir.AluOpType
AF = mybir.ActivationFunctionType
AX = mybir.AxisListType

SEM_ROT = 24000


class Buf:
    __slots__ = ("name", "lw", "rd", "mo")

    def __init__(self, name):
        self.name = name
        self.lw = {}
        self.rd = {}
        self.mo = False


class Eng:
    def __init__(self, fw, e, name):
        self.fw = fw
        self.e = e
        self.name = name
        self.sem = fw.new_sem(name)
        self.cnt = 0
        self.seen = {}

    def _rotate(self):
        if self.cnt >= SEM_ROT:
            self.sem = self.fw.new_sem(self.name)
            self.cnt = 0


class FW:
    def __init__(self, nc):
        self.nc = nc
        self.nsem = 0
        self.sem_objs = []
        self.pe = Eng(self, nc.tensor, "pe")
        self.act = Eng(self, nc.scalar, "act")
        self.dve = Eng(self, nc.vector, "dve")
        self.pool = Eng(self, nc.gpsimd, "pool")
        self.sp = Eng(self, nc.sync, "sp")
        self.dma_sems = {}

    def new_sem(self, name):
        self.nsem += 1
        cm = self.nc.semaphore(f"{name}_{self.nsem}")
        s = cm.__enter__()
        self.sem_objs.append(s)
        return s

    def _waits(self, eng, reads, writes, extra=(), mwrites=()):
        need = {}

        def merge(d):
            for s, v in d.items():
                if need.get(s, 0) < v:
                    need[s] = v
        for b in reads:
            merge(b.lw)
        for b in writes:
            merge(b.lw)
            merge(b.rd)
        for b in mwrites:
            merge(b.rd)
            if not b.mo:
                merge(b.lw)
        for d in extra:
            merge(d)
        for s, v in need.items():
            if eng.seen.get(s, 0) < v:
                eng.e.wait_ge(s, v)
                eng.seen[s] = v

    def _commit(self, ev, reads, writes, mwrites):
        for b in writes:
            b.lw = {ev[0]: ev[1]}
            b.rd = {}
            b.mo = False
        for b in mwrites:
            if b.rd or not b.mo:
                b.lw = {}
            b.lw[ev[0]] = ev[1]
            b.rd = {}
            b.mo = True
        for b in reads:
            if b.rd.get(ev[0], 0) < ev[1]:
                b.rd[ev[0]] = ev[1]

    def op(self, eng, fn, reads=(), writes=(), mwrites=()):
        eng._rotate()
        self._waits(eng, reads, writes, mwrites=mwrites)
        ins = fn()
        eng.cnt += 1
        ins.then_inc(eng.sem, 1)
        self._commit((eng.sem, eng.cnt), reads, writes, mwrites)
        return ins

    def op_group(self, eng, fns, reads=(), writes=()):
        eng._rotate()
        self._waits(eng, reads, writes)
        for fn in fns[:-1]:
            fn()
        ins = fns[-1]()
        eng.cnt += 1
        ins.then_inc(eng.sem, 1)
        self._commit((eng.sem, eng.cnt), reads, writes, ())
        return ins

    def dma(self, eng, fn, key, reads=(), writes=(), mwrites=()):
        if key not in self.dma_sems:
            self.dma_sems[key] = [self.new_sem("d"), 0]
        ent = self.dma_sems[key]
        prev = {ent[0]: ent[1]} if ent[1] else {}
        self._waits(eng, reads, writes, extra=(prev,), mwrites=mwrites)
        ins = fn()
        ent[1] += 16
        ins.then_inc(ent[0], 16)
        self._commit((ent[0], ent[1]), reads, writes, mwrites)
        return ins

    def wait_all(self, eng, bufs):
        self._waits(eng, bufs, bufs)


class Tile:
    def __init__(self, fw, kind, name, shape, dtype):
        nc = fw.nc
        cm = nc.sbuf_tensor(name, shape, dtype) if kind == "sb" else nc.psum_tensor(name, shape, dtype)
        self.t = cm.__enter__()
        self.b = Buf(name)
        self.name = name
        self.bufs = [self.b]

    def __getitem__(self, idx):
        return self.t[idx]


class SubTile:
    def __init__(self, parent, name, c0, c1):
        self.ap = parent.t[:, c0:c1]
        self.b = Buf(name)
        self.name = name
        self.bufs = [self.b]
        parent.bufs.append(self.b)

    def __getitem__(self, idx):
        return self.ap[idx]


class Pool:
    def __init__(self, fw, kind, name, shape, dtype, n):
        self.tiles = [Tile(fw, kind, f"{name}{i}", shape, dtype) for i in range(n)]
        self.i = 0

    def next(self):
        t = self.tiles[self.i % len(self.tiles)]
        self.i += 1
        return t


D = 2048
T = 4096
L = 256
NT = T + L
KC = D // 128
MIXW = 512
NE = 16
CAP = 512
CAPC = 32
SLOTS = CAP + CAPC
NEG = -30000.0
EPS = 1e-6
NA_SCALE = 0.125
MOE_STOP = None


def build(NL=2, do_moe=True, dbg=False):
    nc = bass.Bass("TRN2", target_bir_lowering=False)
    fw = FW(nc)
    sp, pe, act, dve, pool = fw.sp, fw.pe, fw.act, fw.dve, fw.pool
    V, S, G, P = nc.vector, nc.scalar, nc.gpsimd, nc.tensor

    def din(name, shape, dt=F32):
        return nc.dram_tensor(name, list(shape), dt, kind="ExternalInput").ap()

    def dscr(name, shape, dt):
        if dbg and (dbg is True or name in dbg):
            return nc.dram_tensor(name, list(shape), dt, kind="ExternalOutput").ap()
        return nc.dram_tensor(name, list(shape), dt).ap()

    xin = din("xin", [NT, D])
    cvec = din("cvec", [2, D])
    dftc = din("dftc", [T, T], BF16)
    dfts = din("dfts", [T, T], BF16)
    dftcc = din("dftcc", [L, L], BF16)
    dftsc = din("dftsc", [L, L], BF16)
    chdft = din("chdft", [128, 256], BF16)
    invcnt = din("invcnt", [4, NT])
    W = []
    for l in range(NL):
        W.append(dict(
            w_ada=din(f"w_ada{l}", [D, 6 * D]), b_ada=din(f"b_ada{l}", [96, 128]),
            n1=din(f"n1_{l}", [16, 128]), n2=din(f"n2_{l}", [16, 128]),
            w_in=din(f"w_in{l}", [D, 6 * D]), conv=din(f"conv{l}", [3, MIXW]),
            qn=din(f"qn{l}", [128, 1]), kn=din(f"kn{l}", [128, 1]),
            bias=din(f"bias{l}", [5, 8, 640, 128]),
            pool_w=din(f"pool_w{l}", [4, 128, 128]), pool_s=din(f"pool_s{l}", [MIXW, 1]),
            w_br=din(f"w_br{l}", [D, D]), w_out=din(f"w_out{l}", [D, D]),
            w_rt=din(f"w_rt{l}", [D, NE]),
            weg=din(f"weg{l}", [NE, D, D]), weu=din(f"weu{l}", [NE, D, D]), wed=din(f"wed{l}", [NE, D, D]),
        ))
    out = nc.dram_tensor("out", [T, D], F32, kind="ExternalOutput").ap()

    xres = dscr("xres", [NT + 128, D], F32); b_xres = Buf("xres")
    hT = dscr("hT", [D, NT], BF16); b_hT = Buf("hT")
    zT = dscr("zT", [6 * D, NT], BF16); b_zT = Buf("zT")
    vtok = dscr("vtok", [NT, MIXW], BF16); b_vtok = Buf("vtok")
    qkT = dscr("qkT", [2 * MIXW, NT], BF16); b_qkT = Buf("qkT")
    brT = dscr("brT", [D, NT], BF16); b_brT = Buf("brT")
    fab = dscr("fab", [NT, 4, 256], BF16); b_fab = Buf("fab")
    h2tok = dscr("h2tok", [NT + 128, D], BF16); b_h2 = Buf("h2tok")
    vecs = dscr("vecs", [8, D], F32); b_vecs = Buf("vecs")
    b_out = Buf("out")
    wbr16 = dscr("wbr16", [16, 128, 16, 128], BF16); b_wbr = Buf("wbr16")
    wout16 = dscr("wout16", [4, 128, 16, 512], BF16); b_wout = Buf("wout16")

    WB = Pool(fw, "sb", "wb", [128, 8192], BF16, 2)
    WH = [SubTile(WB.tiles[i // 2], f"wh{i}", (i % 2) * 4096, (i % 2 + 1) * 4096) for i in range(4)]
    ABp = Pool(fw, "sb", "ab", [128, 8192], BF16, 3)
    FP = Pool(fw, "sb", "fp", [128, 2048], F32, 4)
    SM = Pool(fw, "sb", "sm", [128, 1056], F32, 5)
    small16 = Pool(fw, "sb", "s16", [128, 2048], BF16, 6)
    bcA = Tile(fw, "sb", "bcA", [128, 2048], F32)
    bcB = Tile(fw, "sb", "bcB", [128, 2048], F32)
    wr = Tile(fw, "sb", "wr", [128, 256], F32)
    affT = Tile(fw, "sb", "affT", [16, NT], F32)
    tv = Tile(fw, "sb", "tv", [16, SLOTS], F32)
    tix = Tile(fw, "sb", "tix", [16, SLOTS], U32)
    idxF = Tile(fw, "sb", "idxF", [128, 80], F32)
    idxF4 = Tile(fw, "sb", "idxF4", [128, 80], F32)
    idxT = Tile(fw, "sb", "idxT", [128, 80], I32)
    idxT4 = Tile(fw, "sb", "idxT4", [128, 320], I32)
    gateT = Tile(fw, "sb", "gateT", [128, 80], F32)
    chd = Tile(fw, "sb", "chd", [128, 256], BF16)
    ones = Tile(fw, "sb", "ones", [128, 8], BF16)
    xc = Tile(fw, "sb", "xc", [128, 512], BF16)
    ac = Tile(fw, "sb", "ac", [128, 512], BF16)
    PS = Pool(fw, "ps", "ps", [128, 512], F32, 4)
    psO = Tile(fw, "ps", "psO", [128, 512], F32)
    psD = Tile(fw, "ps", "psD", [128, 512], F32)
    qt = Tile(fw, "sb", "qt", [128, 512], BF16)
    PSB = Pool(fw, "ps", "psb", [128, 1024], BF16, 2)
    ident = Tile(fw, "sb", "ident", [128, 128], F32)
    identb = Tile(fw, "sb", "identb", [128, 128], BF16)
    blk64 = Tile(fw, "sb", "blk64", [128, 128], F32)
    modT = Tile(fw, "sb", "modT", [128, 192], F32)
    prm = Tile(fw, "sb", "prm", [128, 256], F32)
    small = Pool(fw, "sb", "tiny", [128, 64], F32, 8)

    def _b(xs):
        o = []
        for x in xs:
            if hasattr(x, "bufs"):
                o.extend(x.bufs)
            else:
                o.append(x)
        return o

    def O(eng, fn, r=(), w=(), mw=()):
        return fw.op(eng, fn, _b(r), _b(w), _b(mw))

    def OG(eng, fns, r=(), w=()):
        return fw.op_group(eng, fns, _b(r), _b(w))

    def DMA(eng, fn, key, r=(), w=(), mw=()):
        return fw.dma(eng, fn, key, _b(r), _b(w), _b(mw))

    def ld(dst_tile, dst_ap, src_ap, r=(), cast=False, **kw):
        if cast:
            return DMA(pool, lambda: G.dma_start(out=dst_ap, in_=src_ap, **kw), dst_tile.name, r, [dst_tile])
        return DMA(sp, lambda: nc.sync.dma_start(out=dst_ap, in_=src_ap, **kw), dst_tile.name, r, [dst_tile])

    def stq(dst_ap, src_tile, src_ap, dbuf, **kw):
        return DMA(act, lambda: S.dma_start(out=dst_ap, in_=src_ap, **kw), src_tile.name + "_st", [src_tile], mw=[dbuf])

    O(pool, lambda: G.memset(ident[:], 1.0), w=[ident])
    O(pool, lambda: G.affine_select(out=ident[:], in_=ident[:], pattern=[[-1, 128]], compare_op=ALU.is_equal,
                                   fill=0.0, base=0, channel_multiplier=1), r=[ident], w=[ident])
    O(dve, lambda: V.tensor_copy(out=identb[:], in_=ident[:]), r=[ident], w=[identb])
    O(pool, lambda: G.memset(blk64[:], 0.0), w=[blk64])
    O(pool, lambda: G.memset(blk64[0:64, 0:64], 1.0), r=[blk64], w=[blk64])
    O(pool, lambda: G.memset(blk64[64:128, 64:128], 1.0), r=[blk64], w=[blk64])

    def prm_ap(which, j, r):
        o = (which * 16 + j) * 2 + r
        return prm[:, o:o + 1]

    def rstd_from_ss(ss_ap, out_ap, n, tl):
        O(dve, lambda: V.tensor_scalar(out=out_ap, in0=ss_ap, scalar1=1.0 / n, scalar2=EPS, op0=ALU.mult, op1=ALU.add), r=[tl], w=[tl])
        O(act, lambda: S.sqrt(out=out_ap, in_=out_ap), r=[tl], w=[tl])
        O(dve, lambda: V.reciprocal(out=out_ap, in_=out_ap), r=[tl], w=[tl])

    def phase_precast(l):
        w = W[l]
        for dj in range(16):
            DMA(pool, lambda: G.dma_start(out=wbr16[dj], in_=w["w_br"][:, dj * 128:(dj + 1) * 128].rearrange("(k p) c -> p k c", p=128)),
                f"pc{dj % 4}", [], mw=[b_wbr])
        for db in range(4):
            DMA(pool, lambda: G.dma_start(out=wout16[db], in_=w["w_out"][:, db * 512:(db + 1) * 512].rearrange("(k p) c -> p k c", p=128)),
                f"pc{db % 4}", [], mw=[b_wout])

    def phase_mod(l):
        w = W[l]
        scT = small.next()
        t = small.next()
        with nc.allow_non_contiguous_dma(reason="tiny transposed load"):
            for r_ in range(2):
                ld(t, t[:, r_ * 16:(r_ + 1) * 16], cvec[r_].rearrange("(k p) -> p k", p=128))
        O(act, lambda: S.activation(out=scT[:, 0:32], in_=t[:, 0:32], func=AF.Silu), r=[t], w=[scT])
        psm = PS.next()
        for cb in range(48):
            wt = FP.next()
            for half in range(2):
                if half == 1:
                    wt = FP.next()
                ld(wt, wt[:, :].rearrange("p (k c) -> p k c", c=256),
                   w["w_ada"][half * 1024:(half + 1) * 1024, cb * 256:(cb + 1) * 256].rearrange("(k p) c -> p k c", p=128))
                if half == 0:
                    wt0 = wt
            for jj in range(2):
                col = cb * 2 + jj
                for k in range(16):
                    src = wt0 if k < 8 else wt
                    kk = k % 8
                    O(pe, lambda: P.matmul(psm[:, col * 2:col * 2 + 2], src[:, kk * 256 + jj * 128: kk * 256 + jj * 128 + 128],
                                           scT[:, 0:32].rearrange("p (r k) -> p k r", r=2)[:, k, :], start=(k == 0), stop=(k == 15)), r=[src, scT], w=[psm])
        bt = FP.next()
        ld(bt, bt[0:96, 0:128], w["b_ada"])
        ld(bt, bt[0:16, 128:256], w["n1"])
        ld(bt, bt[0:16, 256:384], w["n2"])
        pst = PS.next()
        O(pe, lambda: P.transpose(pst[:, 0:96], bt[0:96, 0:128], ident[0:96, 0:96]), r=[bt, ident], w=[pst])
        O(pe, lambda: P.transpose(pst[:, 96:112], bt[0:16, 128:256], ident[0:16, 0:16]), r=[bt, ident], w=[pst])
        O(pe, lambda: P.transpose(pst[:, 112:128], bt[0:16, 256:384], ident[0:16, 0:16]), r=[bt, ident], w=[pst])
        bn = small.next()
        bn = SM.next()
        O(dve, lambda: V.tensor_copy(out=bn[:, 0:128], in_=pst[:, 0:128]), r=[pst], w=[bn])
        O(dve, lambda: V.tensor_tensor(out=modT[:, 0:192].rearrange("p (c r) -> p c r", r=2),
                                       in0=psm[:, 0:192].rearrange("p (c r) -> p c r", r=2),
                                       in1=bn[:, 0:96].unsqueeze(2).to_broadcast([128, 96, 2]), op=ALU.add), r=[psm, bn], w=[modT])

        def mv(m):
            return modT[:, m * 32:(m + 1) * 32].rearrange("p (j r) -> p j r", r=2)

        def pv(which):
            return prm[:, which * 32:(which + 1) * 32].rearrange("p (j r) -> p j r", r=2)
        for which, (msc, msh, nwo) in enumerate([(1, 0, 96), (4, 3, 112)]):
            base = 0 if which == 0 else 3
            O(dve, lambda: V.tensor_scalar(out=pv(base), in0=mv(msc), scalar1=1.0, scalar2=None, op0=ALU.add), r=[modT], w=[prm])
            O(dve, lambda: V.tensor_tensor(out=pv(base), in0=pv(base), in1=bn[:, nwo:nwo + 16].unsqueeze(2).to_broadcast([128, 16, 2]),
                                           op=ALU.mult), r=[prm, bn], w=[prm])
            O(dve, lambda: V.tensor_copy(out=pv(base + 1), in_=mv(msh)), r=[modT], w=[prm])
            O(dve, lambda: V.tensor_copy(out=pv(base + 2), in_=mv(msc + 1)), r=[modT], w=[prm])
        rows = [(3, 0), (4, 0), (5, 0), (3, 1), (4, 1), (5, 1), (2, 0), (2, 1)]
        with nc.allow_non_contiguous_dma(reason="param row layout"):
            for ri, (which, r) in enumerate(rows):
                src = prm[:, which * 32:(which + 1) * 32].rearrange("p (j r) -> p j r", r=2)[:, :, r:r + 1]
                DMA(sp, lambda: nc.sync.dma_start(out=vecs[ri:ri + 1, :].rearrange("o (j p) -> p j o", p=128), in_=src),
                    f"vecs{ri}", [prm], mw=[b_vecs])

    def bc_row(ri):
        t = FP.next()
        ld(t, t[:, :], vecs[ri:ri + 1, :].to_broadcast([128, D]), r=[b_vecs])
        return t

    def norm_tile(src_dram, ti, which, want_h2=None):
        xt = FP.next()
        ld(xt, xt[:, :], src_dram[ti * 128:(ti + 1) * 128, :], r=[b_xres] if src_dram is xres else [])
        sq = FP.next()
        st = small.next()
        O(act, lambda: S.activation(out=sq[:, :], in_=xt[:, :], func=AF.Square, accum_out=st[:, 0:1]), r=[xt], w=[sq, st])
        rstd_from_ss(st[:, 0:1], st[:, 1:2], D, st)
        O(act, lambda: S.activation(out=sq[:, :], in_=xt[:, :], func=AF.Copy, scale=st[:, 1:2]), r=[xt, st], w=[sq])
        return xt, sq

    def phase_norm1(l):
        src = xin if l == 0 else xres
        for g in range(9):
            ntile = 4 if g < 8 else 2
            r = 0 if g < 8 else 1
            hs = ABp.next()
            hv = hs[:, :].rearrange("p (k t) -> p k t", t=512)
            for tt in range(ntile):
                ti = g * 4 + tt
                xt, xn = norm_tile(src, ti, 0)
                for kq in range(4):
                    ps = PS.next()
                    for q in range(4):
                        k = kq * 4 + q
                        O(pe, lambda: P.transpose(ps[:, q * 128:(q + 1) * 128], xn[:, k * 128:(k + 1) * 128], ident[:]), r=[xn, ident], w=[ps])
                    for q in range(4):
                        k = kq * 4 + q
                        e = dve if q % 2 == 0 else pool
                        if q % 2 == 0:
                            O(dve, lambda: V.tensor_scalar(out=hv[:, k, tt * 128:(tt + 1) * 128], in0=ps[:, q * 128:(q + 1) * 128],
                                                           scalar1=prm_ap(0, k, r), scalar2=prm_ap(1, k, r), op0=ALU.mult, op1=ALU.add),
                              r=[ps, prm], w=[hs])
                        else:
                            O(act, lambda: S.activation(out=hv[:, k, tt * 128:(tt + 1) * 128], in_=ps[:, q * 128:(q + 1) * 128],
                                                        func=AF.Identity, scale=prm_ap(0, k, r), bias=prm_ap(1, k, r)),
                              r=[ps, prm], w=[hs])
            n = ntile * 128
            stq(hT[:, g * 512:g * 512 + n].rearrange("(k p) t -> p k t", p=128), hs, hv[:, :, 0:n], b_hT)

    def phase_inproj(l):
        w = W[l]
        def wload(cb):
            wt_ = WB.next()
            ld(wt_, wt_[:, :].rearrange("p (k c) -> p k c", c=512),
               w["w_in"][:, cb * 512:(cb + 1) * 512].rearrange("(k p) c -> p k c", p=128), cast=True)
            return wt_
        nxt = wload(0)
        for cb in range(24):
            wt = nxt
            wv = wt[:, :].rearrange("p (k c) -> p k c", c=512)
            if cb + 1 < 24:
                nxt = wload(cb + 1)
            for g in range(9):
                n = 512 if g < 8 else 256
                ht = ABp.next()
                hv = ht[:, :].rearrange("p (k t) -> p k t", t=512)
                ld(ht, hv[:, :, 0:n], hT[:, g * 512:g * 512 + n].rearrange("(k p) t -> p k t", p=128), r=[b_hT])
                if cb == 5:
                    for tt in range(n // 128):
                        ps = PS.next()
                        OG(pe, [(lambda k=k: P.matmul(ps[:, :], hv[:, k, tt * 128:(tt + 1) * 128], wv[:, k, :], start=(k == 0), stop=(k == 15)))
                                for k in range(16)], r=[ht, wt], w=[ps])
                        vt = small16.next()
                        O(act, lambda: S.copy(out=vt[:, 0:512], in_=ps[:, :]), r=[ps], w=[vt])
                        stq(vtok[g * 512 + tt * 128: g * 512 + (tt + 1) * 128, :], vt, vt[:, 0:512], b_vtok)
                    continue
                zt = small16.next()
                zv = zt[:, :].rearrange("p (j t) -> p j t", t=512)
                for j in range(4):
                    ps = PS.next()
                    OG(pe, [(lambda k=k: P.matmul(ps[:, 0:n], wv[:, k, j * 128:(j + 1) * 128], hv[:, k, 0:n], start=(k == 0), stop=(k == 15)))
                            for k in range(16)], r=[ht, wt], w=[ps])
                    if cb >= 8:
                        O(act, lambda: S.activation(out=zv[:, j, 0:n], in_=ps[:, 0:n], func=AF.Sigmoid), r=[ps], w=[zt])
                    else:
                        O(dve, lambda: V.tensor_copy(out=zv[:, j, 0:n], in_=ps[:, 0:n]), r=[ps], w=[zt])
                stq(zT[cb * 512:(cb + 1) * 512, g * 512:g * 512 + n].rearrange("(j p) t -> p j t", p=128), zt, zv[:, :, 0:n], b_zT)

    def seg_list():
        if cur["last"]:
            return [(0, T)]
        return [(0, T), (T, L)]

    def phase_conv(l):
        w = W[l]
        cw = small.next()
        with nc.allow_non_contiguous_dma(reason="tiny"):
            for k_ in range(3):
                ld(cw, cw[:, k_ * 4:(k_ + 1) * 4], w["conv"][k_].rearrange("(j p) -> p j", p=128))
        for j in range(4):
            for (s0, n) in seg_list():
                for c0 in range(0, n, 1024):
                    m = min(1024, n - c0)
                    xa = small16.next(); gb = small16.next(); gc = small16.next()
                    lo = 1 if c0 > 0 else 0
                    hi = 1 if c0 + m < n else 0
                    for tl, row in ((xa, 0), (gc, 1024)):
                        if not lo:
                            O(pool, lambda: G.memset(tl[:, 0:1], 0.0), w=[tl])
                        if not hi:
                            O(pool, lambda: G.memset(tl[:, m + 1:m + 2], 0.0), w=[tl])
                        ld(tl, tl[:, 1 - lo:m + 1 + hi], zT[row + j * 128: row + (j + 1) * 128, s0 + c0 - lo: s0 + c0 + m + hi], r=[b_zT])
                    ld(gb, gb[:, 0:m], zT[512 + j * 128: 512 + (j + 1) * 128, s0 + c0: s0 + c0 + m], r=[b_zT])
                    u = SM.next()
                    u2 = SM.next()
                    O(dve, lambda: V.tensor_tensor(out=u[:, 0:m + 2], in0=xa[:, 0:m + 2], in1=gc[:, 0:m + 2], op=ALU.mult), r=[xa, gc], w=[u])
                    O(dve, lambda: V.tensor_scalar(out=u2[:, 0:m], in0=u[:, 1:m + 1], scalar1=cw[:, 4 + j:5 + j], scalar2=None, op0=ALU.mult), r=[u, cw], w=[u2])
                    O(dve, lambda: V.scalar_tensor_tensor(out=u2[:, 0:m], in0=u[:, 0:m], scalar=cw[:, j:j + 1], in1=u2[:, 0:m], op0=ALU.mult, op1=ALU.add), r=[u, cw, u2], w=[u2])
                    O(dve, lambda: V.scalar_tensor_tensor(out=u2[:, 0:m], in0=u[:, 2:m + 2], scalar=cw[:, 8 + j:9 + j], in1=u2[:, 0:m], op0=ALU.mult, op1=ALU.add), r=[u, cw, u2], w=[u2])
                    yo = small16.next()
                    O(dve, lambda: V.tensor_tensor(out=yo[:, 0:m], in0=u2[:, 0:m], in1=gb[:, 0:m], op=ALU.mult), r=[u2, gb], w=[yo])
                    stq(brT[j * 128:(j + 1) * 128, s0 + c0: s0 + c0 + m], yo, yo[:, 0:m], b_brT)

    def phase_pool(l):
        w = W[l]
        pw = WB.next()
        pwv = pw[:, 0:512].rearrange("p (g c) -> p g c", c=128)
        ld(pw, pwv, w["pool_w"].rearrange("g p c -> p g c"), cast=True)
        psc = small.next()
        with nc.allow_non_contiguous_dma(reason="tiny"):
            ld(psc, psc[:, 0:4], w["pool_s"].rearrange("(g p) o -> p (g o)", p=128))
        for g in range(4):
            win = (2, 4, 8, 16)[g]
            for (s0, n) in seg_list():
                for c0 in range(0, n, 512):
                    m = min(512, n - c0)
                    H = 16
                    lo = min(H, c0); hi = min(H, n - c0 - m)
                    zb = small16.next()
                    ld(zb, zb[:, H - lo:H + m + hi], zT[3584 + g * 128: 3584 + (g + 1) * 128, s0 + c0 - lo: s0 + c0 + m + hi], r=[b_zT])
                    u = SM.next(); a = SM.next(); b2 = SM.next()
                    O(pool, lambda: G.memset(u[:, 0:m + 2 * H], 0.0), w=[u])
                    O(dve, lambda: V.tensor_copy(out=u[:, H - lo:H + m + hi], in_=zb[:, H - lo:H + m + hi]), r=[zb, u], w=[u])
                    O(dve, lambda: V.tensor_tensor(out=a[:, 1:m + 2 * H], in0=u[:, 0:m + 2 * H - 1], in1=u[:, 1:m + 2 * H], op=ALU.add), r=[u], w=[a])
                    cur, oth = a, b2
                    lo_v = 1; hi_v = m + 2 * H
                    sh = 1
                    wdt = 2
                    while wdt < win:
                        nlo = lo_v + sh; nhi = hi_v - sh
                        O(dve, lambda: V.tensor_tensor(out=oth[:, nlo:nhi], in0=cur[:, nlo - sh:nhi - sh], in1=cur[:, nlo + sh:nhi + sh], op=ALU.add), r=[cur], w=[oth])
                        cur, oth = oth, cur
                        lo_v, hi_v = nlo, nhi
                        sh *= 2; wdt *= 2
                    ic = SM.next()
                    ld(ic, ic[:, 0:m], invcnt[g:g + 1, s0 + c0:s0 + c0 + m].to_broadcast([128, m]))
                    O(dve, lambda: V.tensor_tensor(out=oth[:, 0:m], in0=cur[:, H:H + m], in1=ic[:, 0:m], op=ALU.mult), r=[cur, ic], w=[oth])
                    pb = small16.next()
                    O(dve, lambda: V.tensor_tensor(out=pb[:, 0:m], in0=oth[:, 0:m], in1=u[:, H:H + m], op=ALU.subtract), r=[oth, u], w=[pb])
                    ps = PS.next()
                    O(pe, lambda: P.matmul(ps[:, 0:m], pwv[:, g, :], pb[:, 0:m], start=True, stop=True), r=[pw, pb], w=[ps])
                    yo = small16.next()
                    O(act, lambda: S.activation(out=yo[:, 0:m], in_=ps[:, 0:m], func=AF.Copy, scale=psc[:, g:g + 1]), r=[ps, psc], w=[yo])
                    stq(brT[1536 + g * 128:1536 + (g + 1) * 128, s0 + c0:s0 + c0 + m], yo, yo[:, 0:m], b_brT)

    def phase_fourier(l):
        ld(chd, chd[:, 0:256], chdft)
        for ti in range((T if cur["last"] else NT) // 128):
            zf = small16.next()
            ld(zf, zf[:, 0:512].rearrange("p (g t) -> p g t", t=128),
               zT[3072:3584, ti * 128:(ti + 1) * 128].rearrange("(g p) t -> p g t", p=128), r=[b_zT])
            ab = small16.next()
            for half in range(2):
                ps = PS.next()
                for gg in range(2):
                    g = half * 2 + gg
                    O(pe, lambda: P.matmul(ps[:, gg * 256:(gg + 1) * 256], zf[:, g * 128:(g + 1) * 128], chd[:, 0:256], start=True, stop=True),
                      r=[zf, chd], w=[ps])
                O(act if half else dve, (lambda: S.copy(out=ab[:, half * 512:(half + 1) * 512], in_=ps[:, :])) if half else
                  (lambda: V.tensor_copy(out=ab[:, half * 512:(half + 1) * 512], in_=ps[:, :])), r=[ps], w=[ab])
            stq(fab[ti * 128:(ti + 1) * 128, :, :].rearrange("t g c -> t (g c)"), ab, ab[:, 0:1024], b_fab)
        for (s0, n, tc, ts) in ((0, T, dftc, dfts), (T, L, dftcc, dftsc)):
            if cur["last"] and s0 == T:
                continue
            na = n // 128
            for gp in range(2):
                abts = []
                for gg in range(2):
                    g = gp * 2 + gg
                    abt = ABp.next()
                    abv = abt[:, 0:na * 256].rearrange("p (a c) -> p a c", c=256)
                    ld(abt, abv, fab[s0:s0 + n, g, :].rearrange("(a p) c -> p a c", p=128), r=[b_fab])
                    abts.append((g, abt, abv))
                for nb in range(n // 256):
                    ct = WB.next(); st_ = WB.next()
                    cv = ct[:, 0:na * 256].rearrange("p (a c) -> p a c", c=256)
                    sv = st_[:, 0:na * 256].rearrange("p (a c) -> p a c", c=256)
                    ld(ct, cv, tc[:, nb * 256:(nb + 1) * 256].rearrange("(a p) c -> p a c", p=128))
                    ld(st_, sv, ts[:, nb * 256:(nb + 1) * 256].rearrange("(a p) c -> p a c", p=128))
                    for (g, abt, abv) in abts:
                        ps = PS.next()
                        fns = []
                        for a in range(na):
                            fns.append(lambda a=a, abv=abv, ps=ps: P.matmul(ps[:, 0:256], abv[:, a, 0:128], cv[:, a, :], start=(a == 0), stop=False))
                            fns.append(lambda a=a, abv=abv, ps=ps: P.matmul(ps[:, 0:256], abv[:, a, 128:256], sv[:, a, :], start=False, stop=(a == na - 1)))
                        OG(pe, fns, r=[abt, ct, st_], w=[ps])
                        yo = small16.next()
                        O(act, lambda: S.activation(out=yo[:, 0:256], in_=ps[:, 0:256], func=AF.Copy, scale=float((n * 128) ** -0.5)), r=[ps], w=[yo])
                        stq(brT[1024 + g * 128:1024 + (g + 1) * 128, s0 + nb * 256:s0 + (nb + 1) * 256], yo, yo[:, 0:256], b_brT)

    def phase_qknorm(l):
        w = W[l]
        qw = small.next()
        ld(qw, qw[:, 0:1], w["qn"])
        ld(qw, qw[:, 1:2], w["kn"])
        O(dve, lambda: V.tensor_scalar(out=qw[:, 0:1], in0=qw[:, 0:1], scalar1=NA_SCALE, scalar2=None, op0=ALU.mult), r=[qw], w=[qw])
        for g in range(9):
            n = 512 if g < 8 else 256
            for c in range(8):
                row = 1536 + c * 128 if c < 4 else 2048 + (c - 4) * 128
                z = small16.next()
                ld(z, z[:, 0:n], zT[row:row + 128, g * 512:g * 512 + n], r=[b_zT])
                sq = SM.next()
                O(act, lambda: S.activation(out=sq[:, 0:n], in_=z[:, 0:n], func=AF.Square), r=[z], w=[sq])
                ps = PS.next()
                O(pe, lambda: P.matmul(ps[:, 0:n], blk64[:, :], sq[:, 0:n], start=True, stop=True), r=[blk64, sq], w=[ps])
                rs = SM.next()
                O(dve, lambda: V.tensor_scalar(out=rs[:, 0:n], in0=ps[:, 0:n], scalar1=1.0 / 64, scalar2=EPS, op0=ALU.mult, op1=ALU.add), r=[ps], w=[rs])
                O(act, lambda: S.sqrt(out=rs[:, 0:n], in_=rs[:, 0:n]), r=[rs], w=[rs])
                O(dve, lambda: V.reciprocal(out=rs[:, 0:n], in_=rs[:, 0:n]), r=[rs], w=[rs])
                O(dve, lambda: V.tensor_tensor(out=rs[:, 0:n], in0=rs[:, 0:n], in1=z[:, 0:n], op=ALU.mult), r=[rs, z], w=[rs])
                zo = small16.next()
                O(act, lambda: S.activation(out=zo[:, 0:n], in_=rs[:, 0:n], func=AF.Copy, scale=qw[:, (0 if c < 4 else 1):(1 if c < 4 else 2)]), r=[rs, qw], w=[zo])
                stq(qkT[c * 128:(c + 1) * 128, g * 512:g * 512 + n], zo, zo[:, 0:n], b_qkT)

    def phase_attn(l):
        w = W[l]
        O(pool, lambda: G.memset(ones[:, 0:8], 1.0), w=[ones])
        onesb = ones
        for j in range((T if cur["last"] else NT) // 128):
            lat = j < 32
            if lat:
                ks = min(max(2 * j - 4, 0), 54)
                kt0 = ks // 2
                cls = {0: 0, 1: 1, 30: 3, 31: 4}.get(j, 2)
                nloc = 5
            else:
                nloc = 0
            nch = nloc + 2
            qv = qt[:, 0:512].rearrange("p (c t) -> p c t", t=128)
            ld(qt, qv, qkT[0:512, j * 128:(j + 1) * 128].rearrange("(c p) t -> p c t", p=128), r=[b_qkT])
            kt = ABp.next()
            kv = kt[:, 0:4 * 896].rearrange("p (c t) -> p c t", t=896)
            if lat:
                ld(kt, kv[:, :, 0:640], qkT[512:1024, kt0 * 128:kt0 * 128 + 640].rearrange("(c p) t -> p c t", p=128), r=[b_qkT])
            ld(kt, kv[:, :, nloc * 128:nloc * 128 + 256], qkT[512:1024, T:T + 256].rearrange("(c p) t -> p c t", p=128), r=[b_qkT])
            vt = ABp.next()
            vv = vt[:, 0:7 * 512].rearrange("p (a c) -> p a c", c=512)
            if lat:
                ld(vt, vv[:, 0:5, :], vtok[kt0 * 128:kt0 * 128 + 640, :].rearrange("(a p) c -> p a c", p=128), r=[b_vtok])
            ld(vt, vv[:, nloc:nloc + 2, :], vtok[T:T + 256, :].rearrange("(a p) c -> p a c", p=128), r=[b_vtok])
            for h in range(8):
                pc, pb = h // 2, (h % 2) * 64
                psA = PS.next(); psB = PS.next()

                def sdst(c):
                    return (psA if c < 4 else psB)[:, (c % 4) * 128:(c % 4 + 1) * 128]
                OG(pe, [(lambda c=c: P.matmul(sdst(c), kv[pb:pb + 64, pc, c * 128:(c + 1) * 128], qv[pb:pb + 64, pc, :], start=True, stop=True))
                        for c in range(nch)], r=[kt, qt], w=[psA, psB])
                pt = small16.next()
                if lat:
                    bt = SM.next()
                    bv = bt[:, 0:640].rearrange("p (c q) -> p c q", q=128)
                    ld(bt, bv, w["bias"][cls, h].rearrange("(c p) q -> p c q", p=128))
                    O(dve, lambda: V.tensor_tensor(out=bt[:, 0:512], in0=bt[:, 0:512], in1=psA[:, 0:512], op=ALU.add), r=[bt, psA], w=[bt])
                    O(dve, lambda: V.tensor_tensor(out=bt[:, 512:640], in0=bt[:, 512:640], in1=psB[:, 0:128], op=ALU.add), r=[bt, psB], w=[bt])
                    O(act, lambda: S.activation(out=pt[:, 0:640], in_=bt[:, 0:640], func=AF.Exp), r=[bt], w=[pt])
                    O(act, lambda: S.activation(out=pt[:, 640:896], in_=psB[:, 128:384], func=AF.Exp), r=[psB], w=[pt])
                else:
                    O(act, lambda: S.activation(out=pt[:, 0:256], in_=psA[:, 0:256], func=AF.Exp), r=[psA], w=[pt])
                OG(pe, [(lambda c=c: P.matmul(psO[:, h * 64:(h + 1) * 64], pt[:, c * 128:(c + 1) * 128], vv[:, c, h * 64:(h + 1) * 64],
                                           start=(c == 0), stop=(c == nch - 1))) for c in range(nch)], r=[pt, vt], w=[psO])
                OG(pe, [(lambda c=c: P.matmul(psD[:, h:h + 1], pt[:, c * 128:(c + 1) * 128], onesb[:, 0:1],
                                           start=(c == 0), stop=(c == nch - 1))) for c in range(nch)], r=[pt, onesb], w=[psD])
            rc = small.next()
            O(dve, lambda: V.reciprocal(out=rc[:, 0:8], in_=psD[:, 0:8]), r=[psD], w=[rc])
            yt = small16.next()
            O(dve, lambda: V.tensor_tensor(out=yt[:, 0:512].rearrange("p (h d) -> p h d", d=64), in0=psO[:, 0:512].rearrange("p (h d) -> p h d", d=64),
                                           in1=rc[:, 0:8].unsqueeze(2).to_broadcast([128, 8, 64]), op=ALU.mult), r=[psO, rc], w=[yt])
            pst = PSB.next()
            for c in range(4):
                O(pe, lambda: P.transpose(pst[:, c * 128:(c + 1) * 128], yt[:, c * 128:(c + 1) * 128], identb[:]), r=[yt, identb], w=[pst])
            yo = small16.next()
            O(act, lambda: S.copy(out=yo[:, 0:512], in_=pst[:, 0:512]), r=[pst], w=[yo])
            stq(brT[512:1024, j * 128:(j + 1) * 128].rearrange("(c p) t -> p c t", p=128), yo, yo[:, 0:512].rearrange("p (c t) -> p c t", t=128), b_brT)

    def phase_merge(l):
        w = W[l]
        src = xin if l == 0 else xres
        ld(bcA, bcA[:, :], vecs[6:7, :].to_broadcast([128, D]), r=[b_vecs])
        ld(bcB, bcB[:, :], vecs[7:8, :].to_broadcast([128, D]), r=[b_vecs])
        for g in range(8 if cur["last"] else 9):
            n = 512 if g < 8 else 256
            gbc = bcA if g < 8 else bcB
            bt = ABp.next()
            bv = bt[:, :].rearrange("p (k t) -> p k t", t=512)
            ld(bt, bv[:, :, 0:n], brT[:, g * 512:g * 512 + n].rearrange("(k p) t -> p k t", p=128), r=[b_brT])
            mt = ABp.next()
            mv_ = mt[:, :].rearrange("p (k t) -> p k t", t=512)
            for dj in range(16):
                wb = small16.next()
                wbv = wb[:, :].rearrange("p (k c) -> p k c", c=128)
                ld(wb, wbv, wbr16[dj], r=[b_wbr])
                gt = small16.next()
                gv = gt[:, :].rearrange("p (i t) -> p i t", t=512)
                ld(gt, gv[:, :, 0:n], zT[4096:4096 + 4 * D, g * 512:g * 512 + n].rearrange("(i d) t -> d i t", d=D)[dj * 128:(dj + 1) * 128], r=[b_zT])
                acc = SM.next()
                for i in range(4):
                    ps = PS.next()
                    OG(pe, [(lambda kk=kk: P.matmul(ps[:, 0:n], wbv[:, i * 4 + kk, :], bv[:, i * 4 + kk, 0:n], start=(kk == 0), stop=(kk == 3))) for kk in range(4)], r=[wb, bt], w=[ps])
                    if i == 0:
                        O(dve, lambda: V.tensor_tensor(out=acc[:, 0:n], in0=ps[:, 0:n], in1=gv[:, i, 0:n], op=ALU.mult), r=[ps, gt], w=[acc])
                    else:
                        tmp = SM.next()
                        O(dve, lambda: V.tensor_tensor(out=tmp[:, 0:n], in0=ps[:, 0:n], in1=gv[:, i, 0:n], op=ALU.mult), r=[ps, gt], w=[tmp])
                        O(pool, lambda: G.tensor_tensor(out=acc[:, 0:n], in0=acc[:, 0:n], in1=tmp[:, 0:n], op=ALU.add), r=[acc, tmp], w=[acc])
                O(act, lambda: S.copy(out=mv_[:, dj, 0:n], in_=acc[:, 0:n]), r=[acc], w=[mt])
            for db in range(4):
                wo = WB.next()
                wov = wo[:, :].rearrange("p (k c) -> p k c", c=512)
                ld(wo, wov, wout16[db], r=[b_wout])
                for tt in range(n // 128):
                    ti = g * 4 + tt
                    ps = PS.next()
                    OG(pe, [(lambda k=k: P.matmul(ps[:, :], mv_[:, k, tt * 128:(tt + 1) * 128], wov[:, k, :], start=(k == 0), stop=(k == 15))) for k in range(16)], r=[mt, wo], w=[ps])
                    xs = SM.next()
                    ld(xs, xs[:, 0:512], src[ti * 128:(ti + 1) * 128, db * 512:(db + 1) * 512], r=[b_xres] if l > 0 else [])
                    tmp = SM.next()
                    O(dve, lambda: V.tensor_tensor(out=tmp[:, 0:512], in0=ps[:, :], in1=gbc[:, db * 512:(db + 1) * 512], op=ALU.mult), r=[ps, gbc], w=[tmp])
                    O(pool, lambda: G.tensor_tensor(out=tmp[:, 0:512], in0=tmp[:, 0:512], in1=xs[:, 0:512], op=ALU.add), r=[tmp, xs], w=[tmp])
                    stq(xres[ti * 128:(ti + 1) * 128, db * 512:(db + 1) * 512], tmp, tmp[:, 0:512], b_xres)

    def phase_norm2_router(l):
        w = W[l]
        ld(wr, wr[:, 0:256].rearrange("p (k e) -> p k e", e=16), w["w_rt"].rearrange("(k p) e -> p k e", p=128))
        for ti in range((T if cur["last"] else NT) // 128):
            r = 0 if ti < 32 else 1
            if ti == 0 or ti == 32:
                ld(bcA, bcA[:, :], vecs[3 * r:3 * r + 1, :].to_broadcast([128, D]), r=[b_vecs])
                ld(bcB, bcB[:, :], vecs[3 * r + 1:3 * r + 2, :].to_broadcast([128, D]), r=[b_vecs])
            xt, xn = norm_tile(xres, ti, 1)
            h2 = xt
            O(dve, lambda: V.tensor_tensor(out=h2[:, :], in0=xn[:, :], in1=bcA[:, :], op=ALU.mult), r=[xn, bcA], w=[h2])
            O(pool, lambda: G.tensor_tensor(out=h2[:, :], in0=h2[:, :], in1=bcB[:, :], op=ALU.add), r=[h2, bcB], w=[h2])
            hb = small16.next()
            O(act, lambda: S.copy(out=hb[:, :], in_=h2[:, :]), r=[h2], w=[hb])
            stq(h2tok[ti * 128:(ti + 1) * 128, :], hb, hb[:, :], b_h2)
            hT_ = xn
            for kq in range(4):
                ps = PS.next()
                for q in range(4):
                    k = kq * 4 + q
                    O(pe, lambda: P.transpose(ps[:, q * 128:(q + 1) * 128], h2[:, k * 128:(k + 1) * 128], ident[:]), r=[h2, ident], w=[ps])
                O(dve if kq % 2 else act, (lambda: V.tensor_copy(out=hT_[:, kq * 512:(kq + 1) * 512], in_=ps[:, :])) if kq % 2 else
                  (lambda: S.copy(out=hT_[:, kq * 512:(kq + 1) * 512], in_=ps[:, :])), r=[ps], w=[hT_])
            psl = PS.next()
            OG(pe, [(lambda k=k: P.matmul(psl[:, 0:16], hT_[:, k * 128:(k + 1) * 128], wr[:, k * 16:(k + 1) * 16], start=(k == 0), stop=(k == 15))) for k in range(16)], r=[hT_, wr], w=[psl])
            sm_ = small.next()
            O(dve, lambda: V.reduce_max(out=sm_[:, 16:17], in_=psl[:, 0:16], axis=AX.X), r=[psl], w=[sm_])
            O(dve, lambda: V.tensor_scalar(out=sm_[:, 16:17], in0=sm_[:, 16:17], scalar1=-1.0, scalar2=None, op0=ALU.mult), r=[sm_], w=[sm_])
            O(act, lambda: S.activation(out=sm_[:, 0:16], in_=psl[:, 0:16], func=AF.Exp, bias=sm_[:, 16:17], accum_out=sm_[:, 17:18]), r=[psl, sm_], w=[sm_])
            O(dve, lambda: V.reciprocal(out=sm_[:, 18:19], in_=sm_[:, 17:18]), r=[sm_], w=[sm_])
            O(dve, lambda: V.tensor_scalar(out=sm_[:, 0:16], in0=sm_[:, 0:16], scalar1=sm_[:, 18:19], scalar2=None, op0=ALU.mult), r=[sm_], w=[sm_])
            pst = PS.next()
            O(pe, lambda: P.transpose(pst[0:16, 0:128], sm_[:, 0:16], ident[:]), r=[sm_, ident], w=[pst])
            O(dve, lambda: V.tensor_copy(out=affT[0:16, ti * 128:(ti + 1) * 128], in_=pst[0:16, 0:128]), r=[pst], w=[affT])

    def phase_topk():
        for (c0, n, cap, o0) in ((0, T, CAP, 0), (T, L, CAPC, CAP)):
            if cur["last"] and c0 == T:
                continue
            for it in range(cap // 8):
                vs = tv[0:16, o0 + it * 8:o0 + it * 8 + 8]
                O(dve, lambda: V.max(out=vs, in_=affT[0:16, c0:c0 + n]), r=[affT], w=[tv])
                O(dve, lambda: V.max_index(out=tix[0:16, o0 + it * 8:o0 + it * 8 + 8], in_max=vs, in_values=affT[0:16, c0:c0 + n]), r=[affT, tv], w=[tix])
                O(dve, lambda: V.match_replace(out=affT[0:16, c0:c0 + n], in_to_replace=vs, in_values=affT[0:16, c0:c0 + n], imm_value=-1.0), r=[tv, affT], w=[affT])
        tf = SM.next()
        O(dve, lambda: V.tensor_copy(out=tf[0:16, 0:SLOTS], in_=tix[0:16, 0:SLOTS]), r=[tix], w=[tf])
        O(dve, lambda: V.tensor_scalar(out=tf[0:16, CAP:SLOTS], in0=tf[0:16, CAP:SLOTS], scalar1=float(T), scalar2=None, op0=ALU.add), r=[tf], w=[tf])
        O(pool, lambda: G.iota(idxF[:, 64:80], pattern=[[0, 16]], base=NT, channel_multiplier=1, allow_small_or_imprecise_dtypes=True), w=[idxF])
        O(pool, lambda: G.memset(gateT[:, :], 0.0), w=[gateT])
        for s in range(5):
            n = 128 if s < 4 else CAPC
            ps = PS.next()
            O(pe, lambda: P.transpose(ps[0:n, 0:16], tf[0:16, s * 128:s * 128 + n], ident[0:16, 0:16]), r=[tf, ident], w=[ps])
            O(pe, lambda: P.transpose(ps[0:n, 16:32], tv[0:16, s * 128:s * 128 + n], ident[0:16, 0:16]), r=[tv, ident], w=[ps])
            O(dve, lambda: V.tensor_copy(out=idxF[0:n, s * 16:(s + 1) * 16], in_=ps[0:n, 0:16]), r=[ps, idxF], w=[idxF])
            O(dve, lambda: V.tensor_copy(out=gateT[0:n, s * 16:(s + 1) * 16], in_=ps[0:n, 16:32]), r=[ps, gateT], w=[gateT])
        O(dve, lambda: V.tensor_copy(out=idxT[:, 0:80], in_=idxF[:, 0:80]), r=[idxF], w=[idxT])
        for db in range(4):
            tq = small.next()
            O(dve, lambda: V.tensor_scalar(out=tq[:, 0:80].bitcast(F32) if False else idxF4[:, 0:80], in0=idxF[:, 0:80], scalar1=4.0, scalar2=float(db), op0=ALU.mult, op1=ALU.add), r=[idxF], w=[idxF4])
            O(dve, lambda: V.tensor_copy(out=idxT4[:, db * 80:(db + 1) * 80], in_=idxF4[:, 0:80]), r=[idxF4], w=[idxT4])

    nsc = [0]
    cur = {"last": False}

    def phase_moe(l):
        w = W[l]
        ld(bcA, bcA[:, :], vecs[2:3, :].to_broadcast([128, D]), r=[b_vecs])
        ld(bcB, bcB[:, :], vecs[5:6, :].to_broadcast([128, D]), r=[b_vecs])
        for e in range(NE):
            xs = ABp.next()
            xsv = xs[:, :].rearrange("p (k t) -> p k t", t=512)
            xcv = xc[:, 0:512].rearrange("p (k t) -> p k t", t=32)
            nst = 4 if cur["last"] else 5
            for s in range(nst):
                xg = small16.next()
                DMA(pool, lambda: G.indirect_dma_start(out=xg[:, :], out_offset=None, in_=h2tok,
                                                       in_offset=bass.IndirectOffsetOnAxis(ap=idxT[:, s * 16 + e:s * 16 + e + 1], axis=0)),
                    xg.name, [idxT, b_h2], [xg])
                for kh in range(2):
                    pst = PSB.next()
                    for q in range(8):
                        k = kh * 8 + q
                        O(pe, lambda: P.transpose(pst[:, q * 128:(q + 1) * 128], xg[:, k * 128:(k + 1) * 128], identb[:]), r=[xg, identb], w=[pst])
                    pv_ = pst[:, :].rearrange("p (q t) -> p q t", t=128)
                    if s < 4:
                        O(act if kh else dve, (lambda: S.copy(out=xsv[:, kh * 8:(kh + 1) * 8, s * 128:(s + 1) * 128], in_=pv_)) if kh else
                          (lambda: V.tensor_copy(out=xsv[:, kh * 8:(kh + 1) * 8, s * 128:(s + 1) * 128], in_=pv_)), r=[pst], w=[xs])
                    else:
                        O(dve, lambda: V.tensor_copy(out=xcv[:, kh * 8:(kh + 1) * 8, :], in_=pv_[:, :, 0:32]), r=[pst], w=[xc])
            if MOE_STOP == "gather":
                return
            at = ABp.next()
            atv = at[:, :].rearrange("p (k t) -> p k t", t=512)
            acv = ac[:, 0:512].rearrange("p (k t) -> p k t", t=32)

            def gu_load(fb):
                a_, b_ = WH[(fb % 2) * 2], WH[(fb % 2) * 2 + 1]
                ld(a_, a_[:, :].rearrange("p (k c) -> p k c", c=256), w["weg"][e, :, fb * 256:(fb + 1) * 256].rearrange("(k p) c -> p k c", p=128), cast=True)
                ld(b_, b_[:, :].rearrange("p (k c) -> p k c", c=256), w["weu"][e, :, fb * 256:(fb + 1) * 256].rearrange("(k p) c -> p k c", p=128), cast=True)
                return a_, b_

            def d_load(db):
                t_ = WB.tiles[db % 2]
                ld(t_, t_[:, :].rearrange("p (k c) -> p k c", c=512), w["wed"][e, :, db * 512:(db + 1) * 512].rearrange("(k p) c -> p k c", p=128), cast=True)
                return t_
            nxt = gu_load(0)
            nxt_d = None
            for fb in range(8):
                wg, wu = nxt
                wgv = wg[:, :].rearrange("p (k c) -> p k c", c=256)
                wuv = wu[:, :].rearrange("p (k c) -> p k c", c=256)
                if fb + 1 < 8:
                    nxt = gu_load(fb + 1)
                for fj in range(2):
                    f = fb * 2 + fj
                    pa = PS.next(); pu = PS.next(); pc_ = PS.next()
                    OG(pe, [(lambda k=k: P.matmul(pa[:, :], wgv[:, k, fj * 128:(fj + 1) * 128], xsv[:, k, :], start=(k == 0), stop=(k == 15))) for k in range(16)], r=[wg, xs], w=[pa])
                    OG(pe, [(lambda k=k: P.matmul(pu[:, :], wuv[:, k, fj * 128:(fj + 1) * 128], xsv[:, k, :], start=(k == 0), stop=(k == 15))) for k in range(16)], r=[wu, xs], w=[pu])
                    if nst == 5:
                        OG(pe, [(lambda k=k: P.matmul(pc_[:, 0:32], wgv[:, k, fj * 128:(fj + 1) * 128], xcv[:, k, :], start=(k == 0), stop=(k == 15))) for k in range(16)], r=[wg, xc], w=[pc_])
                        OG(pe, [(lambda k=k: P.matmul(pc_[:, 32:64], wuv[:, k, fj * 128:(fj + 1) * 128], xcv[:, k, :], start=(k == 0), stop=(k == 15))) for k in range(16)], r=[wu, xc], w=[pc_])
                    sa = SM.next()
                    O(act, lambda: S.activation(out=sa[:, 0:512], in_=pa[:, :], func=AF.Silu), r=[pa], w=[sa])
                    O(dve, lambda: V.tensor_tensor(out=atv[:, f, :], in0=sa[:, 0:512], in1=pu[:, :], op=ALU.mult), r=[sa, pu], w=[at])
                    if nst == 5:
                        O(act, lambda: S.activation(out=sa[:, 512:544], in_=pc_[:, 0:32], func=AF.Silu), r=[pc_, sa], w=[sa])
                        O(dve, lambda: V.tensor_tensor(out=acv[:, f, :], in0=sa[:, 512:544], in1=pc_[:, 32:64], op=ALU.mult), r=[sa, pc_], w=[ac])
                if fb == 6:
                    nxt_d = d_load(0)
            if MOE_STOP == "gateup":
                return
            for db in range(4):
                wd = nxt_d
                wdv = wd[:, :].rearrange("p (k c) -> p k c", c=512)
                if db + 1 < 4:
                    nxt_d = d_load(db + 1)
                for s in range(nst):
                    n = 128 if s < 4 else CAPC
                    ps = PS.next()
                    OG(pe, [(lambda f=f: P.matmul(ps[0:n, :], (atv[:, f, s * 128:(s + 1) * 128] if s < 4 else acv[:, f, :]), wdv[:, f, :],
                                                  start=(f == 0), stop=(f == 15))) for f in range(16)], r=[at if s < 4 else ac, wd], w=[ps])
                    yo = SM.next()
                    if s == 4:
                        O(pool, lambda: G.memset(yo[:, 0:512], 0.0), w=[yo])
                    gbc = bcA if s < 4 else bcB
                    O(dve, lambda: V.scalar_tensor_tensor(out=yo[0:n, 0:512], in0=ps[0:n, :], scalar=gateT[0:n, s * 16 + e:s * 16 + e + 1],
                                                          in1=gbc[0:n, db * 512:(db + 1) * 512], op0=ALU.mult, op1=ALU.mult), r=[ps, gateT, gbc, yo], w=[yo])
                    DMA(pool, lambda: G.indirect_dma_start(out=xres.rearrange("n (q c) -> (n q) c", c=512),
                                                           out_offset=bass.IndirectOffsetOnAxis(ap=idxT4[:, db * 80 + s * 16 + e:db * 80 + s * 16 + e + 1], axis=0),
                                                           in_=yo[:, 0:512], in_offset=None, compute_op=ALU.add),
                        yo.name + "_sc", [idxT4, yo], [b_xres])

    for l in range(NL):
        cur["last"] = (l == NL - 1) and NL > 1
        phase_precast(l)
        phase_mod(l)
        if dbg and l == 0:
            dbgv = nc.dram_tensor("dbgv", [128, 448], F32, kind="ExternalOutput").ap()
            DMA(sp, lambda: nc.sync.dma_start(out=dbgv[:, 0:192], in_=modT[:, :]), "dbg0", [modT], [b_out])
            DMA(sp, lambda: nc.sync.dma_start(out=dbgv[:, 192:448], in_=prm[:, :]), "dbg1", [prm], [b_out])
        phase_norm1(l)
        phase_inproj(l)
        phase_conv(l)
        phase_pool(l)
        phase_fourier(l)
        phase_qknorm(l)
        phase_attn(l)
        phase_merge(l)
        if do_moe:
            phase_norm2_router(l)
            phase_topk()
            if do_moe == "topk":
                dbi = nc.dram_tensor("dbi", [128, 80], I32, kind="ExternalOutput").ap()
                dbg_ = nc.dram_tensor("dbgate", [128, 80], F32, kind="ExternalOutput").ap()
                DMA(sp, lambda: nc.sync.dma_start(out=dbi, in_=idxT[:, :]), "dbg2", [idxT], [b_out])
                DMA(sp, lambda: nc.sync.dma_start(out=dbg_, in_=gateT[:, :]), "dbg3", [gateT], [b_out])
                continue
            phase_moe(l)
    DMA(sp, lambda: nc.sync.dma_start(out=out, in_=xres[0:T, :]), "final", [b_xres], [b_out])
    fw.wait_all(sp, [b_out])
    if dbg:
        top = sorted(((v[1], k) for k, v in fw.dma_sems.items()), reverse=True)[:6]
        print("nsem", fw.nsem, "top dma sem counts", top, "eng counts", [(e.name, e.cnt) for e in (pe, act, dve, pool, sp)], flush=True)
    return nc


def _const_tables():
    bf = ml_dtypes.bfloat16
    n = np.arange(T, dtype=np.int64)
    ph = (np.outer(n, n) % T).astype(np.float64) * (2 * np.pi / T)
    dftc = np.cos(ph).astype(np.float32).astype(bf)
    dfts = (-np.sin(ph)).astype(np.float32).astype(bf)
    m = np.arange(L, dtype=np.int64)
    phc = (np.outer(m, m) % L).astype(np.float64) * (2 * np.pi / L)
    dftcc = np.cos(phc).astype(np.float32).astype(bf)
    dftsc = (-np.sin(phc)).astype(np.float32).astype(bf)
    c = np.arange(128, dtype=np.int64)
    pc = (np.outer(c, c) % 128).astype(np.float64) * (2 * np.pi / 128)
    chdft = np.concatenate([np.cos(pc), np.sin(pc)], axis=1).astype(np.float32).astype(bf)
    inv = np.zeros((4, NT), np.float32)
    for g, wdt in enumerate((2, 4, 8, 16)):
        for s0, nn in ((0, T), (T, L)):
            t = np.arange(nn)
            lo = np.clip(t - wdt // 2, 0, nn - 1)
            hi = np.clip(t + wdt - wdt // 2 - 1, 0, nn - 1)
            inv[g, s0:s0 + nn] = 1.0 / (hi - lo + 1)
    return dict(dftc=dftc, dfts=dfts, dftcc=dftcc, dftsc=dftsc, chdft=chdft, invcnt=inv)


def _bias_table(rpb):
    out = np.full((5, 8, 640, 128), NEG, np.float32)
    ql = np.arange(128)
    kl = np.arange(640)
    for cls, j in enumerate((0, 1, 2, 30, 31)):
        ks = min(max(2 * j - 4, 0), 54)
        r = 2 * j + ql // 64
        c = ql % 64
        kr = ks + kl // 64
        kc = kl % 64
        rs = np.clip(r - 4, 0, 56)
        cs = np.clip(c - 8, 0, 48)
        ok = ((kr[:, None] >= rs[None, :]) & (kr[:, None] < rs[None, :] + 8) &
              (kc[:, None] >= cs[None, :]) & (kc[:, None] < cs[None, :] + 16))
        dr = np.clip(kr[:, None] - r[None, :] + 7, 0, 14)
        dc = np.clip(kc[:, None] - c[None, :] + 15, 0, 30)
        for h in range(8):
            gath = rpb[h][dr, dc]
            out[cls, h] = np.where(ok, gath, np.float32(NEG))
    return out


def _layer_inputs(inp, l):
    f = lambda a: np.ascontiguousarray(a, dtype=np.float32)
    return {
        f"w_ada{l}": f(inp["w_ada"][l]), f"b_ada{l}": f(inp["b_ada"][l].reshape(96, 128)),
        f"n1_{l}": f(inp["norm1_w"][l].reshape(16, 128)), f"n2_{l}": f(inp["norm2_w"][l].reshape(16, 128)),
        f"w_in{l}": f(inp["w_in"][l]), f"conv{l}": f(inp["conv_w"][l]),
        f"qn{l}": f(np.tile(inp["q_norm_w"][l], 2).reshape(128, 1)), f"kn{l}": f(np.tile(inp["k_norm_w"][l], 2).reshape(128, 1)),
        f"bias{l}": _bias_table(np.asarray(inp["na_rpb"][l], np.float32)),
        f"pool_w{l}": f(inp["pool_w"][l]), f"pool_s{l}": f(inp["pool_scale"][l].reshape(MIXW, 1)),
        f"w_br{l}": f(inp["w_branch"][l].reshape(D, D)), f"w_out{l}": f(inp["w_out"][l]),
        f"w_rt{l}": f(inp["w_router"][l]),
        f"weg{l}": f(inp["w_exp_gate"][l]), f"weu{l}": f(inp["w_exp_up"][l]), f"wed{l}": f(inp["w_exp_down"][l]),
    }


def make_in_maps(inp, cores, NL=2):
    shared = _const_tables()
    for l in range(NL):
        shared.update(_layer_inputs(inp, l))
    maps = []
    for b in cores:
        m = dict(shared)
        m["xin"] = np.ascontiguousarray(np.concatenate([inp["x"][b], inp["ctx"][b]], axis=0), dtype=np.float32)
        m["cvec"] = np.ascontiguousarray(np.stack([inp["c"][b], inp["c_ctx"]]), dtype=np.float32)
        maps.append(m)
    return maps


def kernel(**inputs):
    inp = {k: np.asarray(v) for k, v in inputs.items()}
    nc = build(NL=2)
    maps = make_in_maps(inp, list(range(8)))
    res = run_bass_kernel_spmd(nc, maps, core_ids=list(range(8)))
    return np.stack([np.asarray(r["out"], dtype=np.float32) for r in res.results], axis=0)
```
